# Optimizing a Trainium2 kernel written in Bass

```python
import numpy as np
import jax
import jax.numpy as jnp
from jax import lax

D_MODEL = 2048
BATCH = 2
SEQ = 4096
DEPTH = 4

CHUNK = 64
RET_HEADS = 8
RET_DK = D_MODEL // 16
RET_DV = D_MODEL // RET_HEADS
GDN_HEADS = 16
GDN_DK = D_MODEL // GDN_HEADS
GDN_DV = D_MODEL // GDN_HEADS
CONV_K = 4
XATTN_HEADS = 4
XATTN_DH = D_MODEL // XATTN_HEADS
MEM_TOKENS = 256
D_FF = 4 * D_MODEL
ROPE_THETA = 10000.0
NORM_EPS = 1e-6
GDN_QKV = 2 * GDN_HEADS * GDN_DK + GDN_HEADS * GDN_DV
IN_SPLITS = (RET_HEADS * RET_DK, RET_HEADS * RET_DK, RET_HEADS * RET_DV, RET_HEADS * RET_DV,
             GDN_QKV, GDN_HEADS, GDN_HEADS, GDN_HEADS * GDN_DV, D_MODEL, D_MODEL)
IN_WIDTH = sum(IN_SPLITS)

kernel_name = "hybrid_retention_gdn_xattn_encoder"


def rms_norm(x, g):
    xf = x.astype(jnp.float32)
    y = xf * lax.rsqrt(jnp.mean(xf * xf, axis=-1, keepdims=True) + NORM_EPS)
    return (y * g.astype(jnp.float32)).astype(x.dtype)


def l2_normalize(t):
    return t * lax.rsqrt(jnp.sum(t * t, axis=-1, keepdims=True) + 1e-6)


def rope(t, positions):
    d = t.shape[-1]
    inv_freq = 1.0 / (ROPE_THETA ** (jnp.arange(0, d, 2, dtype=jnp.float32) / d))
    ang = positions.astype(jnp.float32)[..., None] * inv_freq
    cos = jnp.cos(ang)[:, :, None, :]
    sin = jnp.sin(ang)[:, :, None, :]
    t1, t2 = t[..., : d // 2], t[..., d // 2:]
    return jnp.concatenate([t1 * cos - t2 * sin, t1 * sin + t2 * cos], axis=-1)


def to_chunks(t):
    b, s, h, d = t.shape
    return t.reshape(b, s // CHUNK, CHUNK, h, d).transpose(1, 0, 3, 2, 4)


def from_chunks(t):
    n, b, h, c, d = t.shape
    return t.transpose(1, 0, 3, 2, 4).reshape(b, n * c, h, d)


def causal_conv(t, w):
    c = t.shape[-1]
    return lax.conv_general_dilated(
        t, w.astype(t.dtype)[:, None, :], window_strides=(1,),
        padding=[(CONV_K - 1, 0)], dimension_numbers=("NWC", "WIO", "NWC"),
        feature_group_count=c)


def retention_chunked(q, k, v):
    b, _, h, dk = q.shape
    dv = v.shape[-1]
    idx = jnp.arange(CHUNK, dtype=jnp.float32)
    log_gamma = jnp.log1p(-jnp.exp2(-5.0 - jnp.arange(RET_HEADS, dtype=jnp.float32)))
    intra_decay = jnp.exp(log_gamma[:, None, None] * jnp.abs(idx[:, None] - idx[None, :]))
    q_decay = jnp.exp(log_gamma[:, None] * (idx + 1.0))[:, :, None]
    k_decay = jnp.exp(log_gamma[:, None] * (CHUNK - 1.0 - idx))[:, :, None]
    chunk_decay = jnp.exp(log_gamma * CHUNK)[:, None, None]

    def step(state, xs):
        q_i, k_i, v_i = xs
        scores = jnp.einsum("bhid,bhjd->bhij", q_i, k_i) * intra_decay
        o = (jnp.einsum("bhij,bhje->bhie", scores, v_i)
             + jnp.einsum("bhid,bhde->bhie", q_i * q_decay, state))
        state = state * chunk_decay + jnp.einsum("bhjd,bhje->bhde", k_i * k_decay, v_i)
        return state, o

    s0 = jnp.zeros((b, h, dk, dv), jnp.float32)
    _, o = lax.scan(step, s0, (to_chunks(q), to_chunks(k), to_chunks(v)))
    return from_chunks(o)


def gated_delta_chunked(q, k, v, g, beta):
    b, _, h, dk = q.shape
    dv = v.shape[-1]
    qc, kc, vc = to_chunks(q), to_chunks(k), to_chunks(v)
    gc = to_chunks(g[..., None])[..., 0]
    bc = to_chunks(beta[..., None])
    cum_g = jnp.cumsum(gc, axis=-1)
    causal = jnp.tril(jnp.ones((CHUNK, CHUNK), dtype=bool))
    strict = jnp.tril(jnp.ones((CHUNK, CHUNK), dtype=bool), -1)
    decay = jnp.exp(jnp.where(causal, cum_g[..., :, None] - cum_g[..., None, :], -jnp.inf))
    kb = kc * bc
    eye = jnp.eye(CHUNK, dtype=jnp.float32)
    m = jnp.where(strict, jnp.einsum("nbhid,nbhjd->nbhij", kb, kc) * decay, 0.0)
    t = lax.linalg.triangular_solve(eye + m, jnp.broadcast_to(eye, m.shape),
                                    left_side=True, lower=True, unit_diagonal=True)
    u = t @ (vc * bc)
    w = t @ (kb * jnp.exp(cum_g)[..., None])
    qk = jnp.einsum("nbhid,nbhjd->nbhij", qc, kc) * decay

    def step(state, xs):
        q_i, k_i, u_i, w_i, g_i, qk_i = xs
        v_new = u_i - w_i @ state
        o = (q_i * jnp.exp(g_i)[..., None]) @ state + qk_i @ v_new
        g_last = g_i[..., -1:]
        state = (state * jnp.exp(g_last)[..., None]
                 + jnp.einsum("bhcd,bhce->bhde", k_i * jnp.exp(g_last - g_i)[..., None], v_new))
        return state, o

    s0 = jnp.zeros((b, h, dk, dv), jnp.float32)
    _, o = lax.scan(step, s0, (qc, kc, u, w, cum_g, qk))
    return from_chunks(o)


def hybrid_mixer(h, positions, w_in, conv_w, a_log, dt_bias, ret_gn_g, gdn_norm_g, w_out):
    f32 = jnp.float32
    b, s, _ = h.shape
    proj = h @ w_in
    (rq, rk, rv, rg, gqkv, ga, gb, gz, gate_a, gate_b) = jnp.split(
        proj, np.cumsum(IN_SPLITS)[:-1].tolist(), axis=-1)

    rq = rope(rq.astype(f32).reshape(b, s, RET_HEADS, RET_DK), positions)
    rk = rope(rk.astype(f32).reshape(b, s, RET_HEADS, RET_DK), positions) * (RET_DK ** -0.5)
    rv = rv.astype(f32).reshape(b, s, RET_HEADS, RET_DV)
    yr = retention_chunked(rq, rk, rv)
    mu = jnp.mean(yr, axis=-1, keepdims=True)
    var = jnp.mean(jnp.square(yr - mu), axis=-1, keepdims=True)
    yr = (yr - mu) * lax.rsqrt(var + 1e-5) * ret_gn_g.astype(f32).reshape(RET_HEADS, RET_DV)
    yr = jax.nn.silu(rg.astype(f32)) * yr.reshape(b, s, RET_HEADS * RET_DV)

    gqkv = jax.nn.silu(causal_conv(gqkv, conv_w)).astype(f32)
    gq, gk, gv = jnp.split(gqkv, [GDN_HEADS * GDN_DK, 2 * GDN_HEADS * GDN_DK], axis=-1)
    gq = l2_normalize(gq.reshape(b, s, GDN_HEADS, GDN_DK)) * (GDN_DK ** -0.5)
    gk = l2_normalize(gk.reshape(b, s, GDN_HEADS, GDN_DK))
    gv = gv.reshape(b, s, GDN_HEADS, GDN_DV)
    beta = jax.nn.sigmoid(gb.astype(f32))
    log_a = -jnp.exp(a_log.astype(f32)) * jax.nn.softplus(ga.astype(f32) + dt_bias.astype(f32))
    yg = gated_delta_chunked(gq, gk, gv, log_a, beta)
    yg = yg * lax.rsqrt(jnp.mean(yg * yg, axis=-1, keepdims=True) + NORM_EPS) * gdn_norm_g.astype(f32)
    yg = jax.nn.silu(gz.astype(f32)) * yg.reshape(b, s, GDN_HEADS * GDN_DV)

    merged = jax.nn.sigmoid(gate_a.astype(f32)) * yr + jax.nn.sigmoid(gate_b.astype(f32)) * yg
    return merged.astype(h.dtype) @ w_out


def cross_attention(h, mem_n, w_q, w_kv, w_o):
    f32 = jnp.float32
    b, s, _ = h.shape
    q = (h @ w_q).reshape(b, s, XATTN_HEADS, XATTN_DH).astype(f32)
    k, v = jnp.split(mem_n @ w_kv, 2, axis=-1)
    k = k.reshape(b, -1, XATTN_HEADS, XATTN_DH).astype(f32)
    v = v.reshape(b, -1, XATTN_HEADS, XATTN_DH).astype(f32)
    scores = jnp.einsum("bshd,bmhd->bhsm", q, k) * (XATTN_DH ** -0.5)
    p = jax.nn.softmax(scores, axis=-1)
    o = jnp.einsum("bhsm,bmhd->bshd", p, v).reshape(b, s, D_MODEL).astype(h.dtype)
    return o @ w_o


def setup_inputs(seed: int = 0) -> dict:
    key = jax.random.key(seed)
    ks = jax.random.split(key, 24)
    f32 = jnp.float32

    def dense(k, shape, fan_in):
        return jax.random.normal(k, shape, f32) * (fan_in ** -0.5)

    def gain(k, shape):
        return 1.0 + 0.02 * jax.random.normal(k, shape, f32)

    dt = jnp.exp(jax.random.uniform(ks[5], (DEPTH, GDN_HEADS), f32)
                 * (jnp.log(0.1) - jnp.log(0.001)) + jnp.log(0.001))
    return {
        "x": jax.random.normal(ks[0], (BATCH, SEQ, D_MODEL), f32),
        "mem": jax.random.normal(ks[1], (BATCH, MEM_TOKENS, D_MODEL), f32),
        "positions": jnp.broadcast_to(jnp.arange(SEQ, dtype=jnp.int32), (BATCH, SEQ)),
        "norm_mix_g": gain(ks[2], (DEPTH, D_MODEL)),
        "w_in": dense(ks[3], (DEPTH, D_MODEL, IN_WIDTH), D_MODEL),
        "conv_w": dense(ks[4], (DEPTH, CONV_K, GDN_QKV), CONV_K),
        "gdn_a_log": jnp.log(jax.random.uniform(ks[6], (DEPTH, GDN_HEADS), f32, 1.0, 16.0)),
        "gdn_dt_bias": dt + jnp.log(-jnp.expm1(-dt)),
        "ret_gn_g": gain(ks[7], (DEPTH, RET_HEADS * RET_DV)),
        "gdn_norm_g": gain(ks[8], (DEPTH, GDN_DV)),
        "w_out": dense(ks[9], (DEPTH, D_MODEL, D_MODEL), D_MODEL),
        "norm_x_g": gain(ks[10], (DEPTH, D_MODEL)),
        "norm_mem_g": gain(ks[11], (DEPTH, D_MODEL)),
        "w_xq": dense(ks[12], (DEPTH, D_MODEL, D_MODEL), D_MODEL),
        "w_xkv": dense(ks[13], (DEPTH, D_MODEL, 2 * D_MODEL), D_MODEL),
        "w_xo": dense(ks[14], (DEPTH, D_MODEL, D_MODEL), D_MODEL),
        "norm_ffn_g": gain(ks[15], (DEPTH, D_MODEL)),
        "w_ff1": dense(ks[16], (DEPTH, D_MODEL, D_FF), D_MODEL),
        "w_ff2": dense(ks[17], (DEPTH, D_FF, D_MODEL), D_FF),
        "norm_final_g": gain(ks[18], (D_MODEL,)),
    }


def reference(x, mem, positions, norm_mix_g, w_in, conv_w, gdn_a_log, gdn_dt_bias,
              ret_gn_g, gdn_norm_g, w_out, norm_x_g, norm_mem_g, w_xq, w_xkv, w_xo,
              norm_ffn_g, w_ff1, w_ff2, norm_final_g):
    for l in range(DEPTH):
        h = rms_norm(x, norm_mix_g[l])
        x = x + hybrid_mixer(h, positions, w_in[l], conv_w[l], gdn_a_log[l], gdn_dt_bias[l],
                             ret_gn_g[l], gdn_norm_g[l], w_out[l])
        h = rms_norm(x, norm_x_g[l])
        mem_n = rms_norm(mem, norm_mem_g[l])
        x = x + cross_attention(h, mem_n, w_xq[l], w_xkv[l], w_xo[l])
        h = rms_norm(x, norm_ffn_g[l])
        x = x + jnp.square(jax.nn.relu(h @ w_ff1[l])) @ w_ff2[l]
    return rms_norm(x, norm_final_g)
```

```python
import math
from contextlib import ExitStack

import numpy as np
import ml_dtypes

import concourse.bass as bass
import concourse.mybir as mybir
from concourse.bass_utils import run_bass_kernel_spmd

F32 = mybir.dt.float32
BF16 = mybir.dt.bfloat16
I32 = mybir.dt.int32
AF = mybir.ActivationFunctionType
ALU = mybir.AluOpType
AX = mybir.AxisListType

NCORES = 8
D = 2048
KC = 16
SEQ = 4096
BATCH = 2
NTOK = BATCH * SEQ
TPC = NTOK // NCORES
DEPTH = 4
MEM = 256
DFF = 8192
EPS = 1e-6
RET_HEADS = 8

O_RQ, O_RK, O_RV, O_RG = 0, 1024, 2048, 4096
O_GQKV = 6144
O_GA = O_GQKV + 6144
O_GB = O_GA + 16
O_GZ = O_GB + 16
O_MA = O_GZ + 2048
O_MB = O_MA + 2048


class Buf:
    __slots__ = ("name", "w", "r", "dsem", "excl")

    def __init__(self, name, excl=False):
        self.name = name
        self.w = None
        self.r = []
        self.dsem = None
        self.excl = excl


class Prog:
    ENG = ("pe", "act", "dve", "pool", "sp")

    def __init__(self, nc, es, same_sync=True):
        self.nc = nc
        self.es = es
        self.eng = dict(pe=nc.tensor, act=nc.scalar, dve=nc.vector, pool=nc.gpsimd, sp=nc.sync)
        self.sems = []
        self.esem = {}
        for e in self.ENG:
            self.esem[e] = self._newsem("e_" + e, False)
        self.seen = {e: {} for e in self.ENG}
        self.same_sync = same_sync
        self.out_toks = []
        self.nbuf = 0

    def _newsem(self, name, is_dma):
        h = self.es.enter_context(self.nc.semaphore(name))
        self.sems.append([h, 0, is_dma])
        return len(self.sems) - 1

    def buf(self, name=None, excl=False):
        self.nbuf += 1
        return Buf(name or f"b{self.nbuf}", excl)

    def sb(self, name, shape, dt):
        return self.es.enter_context(self.nc.sbuf_tensor("s_" + name, list(shape), dt))

    def ps(self, name, shape, dt=F32):
        return self.es.enter_context(self.nc.psum_tensor("p_" + name, list(shape), dt))

    def _wait(self, e, toks):
        need = {}
        for t in toks:
            if t is None:
                continue
            k, v = t
            if need.get(k, 0) < v:
                need[k] = v
        for k, v in need.items():
            h, issued, is_dma = self.sems[k]
            if is_dma:
                v = issued
            if k == self.esem[e] and (e == "pe" or not self.same_sync):
                continue
            if self.seen[e].get(k, 0) >= v:
                continue
            self.seen[e][k] = v
            self.eng[e].wait_ge(h, v)

    def _deps(self, reads, writes):
        toks = []
        for b in reads:
            toks.append(b.w)
        for b in writes:
            toks.append(b.w)
            toks.extend(b.r)
        return toks

    def _commit(self, tok, reads, writes):
        for b in reads:
            b.r.append(tok)
        for b in writes:
            b.w = tok
            b.r = []

    def op(self, e, fn, reads=(), writes=()):
        if any(b.excl for b in reads):
            writes = list(writes) + [b for b in reads if b.excl]
            reads = [b for b in reads if not b.excl]
        self._wait(e, self._deps(reads, writes))
        inst = fn(self.eng[e])
        k = self.esem[e]
        self.sems[k][1] += 1
        inst.then_inc(self.sems[k][0], 1)
        tok = (k, self.sems[k][1])
        self._commit(tok, reads, writes)
        return tok

    def dma(self, q, out, in_, reads=(), writes=(), sem_buf=None, is_output=False, **kw):
        self._wait(q, self._deps(reads, writes))
        sb = sem_buf or (writes[0] if writes else reads[0])
        if sb.dsem is None:
            sb.dsem = self._newsem("d_" + sb.name, True)
        k = sb.dsem
        inst = self.eng[q].dma_start(out=out, in_=in_, **kw)
        self.sems[k][1] += 16
        inst.then_inc(self.sems[k][0], 16)
        tok = (k, self.sems[k][1])
        self._commit(tok, reads, writes)
        if is_output:
            self.out_toks.append(tok)
        return tok

    def finish(self):
        self._wait("sp", self.out_toks)


class WStream:
    def __init__(self, P, nslot=2):
        self.P = P
        self.n = nslot
        self.slots = [P.sb(f"wslot{i}", [128, KC, 512], BF16) for i in range(nslot)]
        self.bufs = [P.buf(f"wslot{i}") for i in range(nslot)]
        self.tiles = []
        self.issued = 0

    def add(self, w_ap, r0, c0):
        self.tiles.append(w_ap[r0:r0 + 2048, c0:c0 + 512].rearrange("(kc p) n -> p kc n", p=128))
        return len(self.tiles) - 1

    def _issue(self, i):
        s = i % self.n
        self.P.dma("pool", self.slots[s][:], self.tiles[i], writes=[self.bufs[s]])

    def get(self, i):
        while self.issued <= min(i + self.n - 1, len(self.tiles) - 1):
            self._issue(self.issued)
            self.issued += 1
        s = i % self.n
        return self.slots[s], self.bufs[s]


def _consts():
    ident = np.eye(128, dtype=np.float32)
    return ident


def _lay_g(g):
    return np.ascontiguousarray(np.asarray(g, np.float32).reshape(KC, 128).T)


class TokPhase:
    def __init__(self, nc, es):
        self.nc = nc
        self.P = P = Prog(nc, es)
        self.xT = P.sb("xT", [128, KC, TPC], F32)
        self.xb = [[P.buf(f"x{c}_{tt}") for tt in range(2)] for c in range(KC)]
        self.hT = P.sb("hT", [128, KC, TPC], BF16)
        self.hb = [[P.buf(f"h{c}_{tt}") for tt in range(2)] for c in range(KC)]
        self.ident = P.sb("ident", [128, 128], F32)
        self.identb = P.buf("ident")
        self.ones = P.sb("ones", [128, 128], BF16)
        self.onesb = P.buf("ones")
        self.eps = P.sb("eps", [128, 1], F32)
        self.epsb = P.buf("eps")
        self.sq = [P.sb(f"sq{i}", [128, 512], BF16) for i in range(2)]
        self.sqb = [P.buf(f"sq{i}") for i in range(2)]
        self.tmp = P.sb("tmpn", [128, 512], F32)
        self.tmpb = P.buf("tmpn")
        self.rstd = P.sb("rstd", [128, 512], F32)
        self.rstdb = P.buf("rstd")
        self.psb = [P.ps(f"psb{i}", [128, 512], F32) for i in range(8)]
        self.psbb = [P.buf(f"psb{i}", excl=True) for i in range(8)]
        P.op("dve", lambda e: e.memset(self.ones[:], 1.0), writes=[self.onesb])
        P.op("dve", lambda e: e.memset(self.eps[:], EPS), writes=[self.epsb])

    def load_ident(self, ident_ap):
        self.P.dma("sp", self.ident[:], ident_ap, writes=[self.identb])

    def norm(self, g_sb, g_b, emit, bank=4):
        P = self.P
        ps_n, ps_nb = self.psb[bank], self.psbb[bank]
        for tt in range(2):
            ts = slice(tt * 512, (tt + 1) * 512)
            for c in range(KC):
                s = c % 2
                P.op("act", lambda e: e.activation(out=self.sq[s][:], in_=self.xT[:, c, ts], func=AF.Square),
                     reads=[self.xb[c][tt]], writes=[self.sqb[s]])
                P.op("pe", lambda e: e.matmul(ps_n[:], self.ones[:], self.sq[s][:], start=(c == 0), stop=(c == KC - 1)),
                     reads=[self.sqb[s], self.onesb], writes=[ps_nb])
            P.op("act", lambda e: e.activation(out=self.tmp[:], in_=ps_n[:], func=AF.Ln, scale=1.0 / D,
                                               bias=self.eps[:, 0:1]),
                 reads=[ps_nb, self.epsb], writes=[self.tmpb])
            P.op("act", lambda e: e.activation(out=self.rstd[:], in_=self.tmp[:], func=AF.Exp, scale=-0.5),
                 reads=[self.tmpb], writes=[self.rstdb])
            for c in range(KC):
                emit(c, tt, ts)

    def norm_to_hT(self, g_sb, g_b):
        P = self.P

        def emit(c, tt, ts):
            P.op("dve", lambda e: e.scalar_tensor_tensor(out=self.hT[:, c, ts], in0=self.xT[:, c, ts],
                                                         scalar=g_sb[:, c:c + 1], in1=self.rstd[:],
                                                         op0=ALU.mult, op1=ALU.mult),
                 reads=[self.xb[c][tt], g_b, self.rstdb], writes=[self.hb[c][tt]])
        self.norm(g_sb, g_b, emit)


def build_A0():
    nc = bass.Bass("TRN2", target_bir_lowering=False)
    x = nc.dram_tensor("x", [TPC, D], F32, kind="ExternalInput").ap()
    pos = nc.dram_tensor("pos", [1, TPC], I32, kind="ExternalInput").ap()
    g0 = nc.dram_tensor("g0", [128, KC], F32, kind="ExternalInput").ap()
    ident = nc.dram_tensor("ident", [128, 128], F32, kind="ExternalInput").ap()
    ropec = nc.dram_tensor("ropec", [128, 2], F32, kind="ExternalInput").ap()
    xT_o = nc.dram_tensor("xT_o", [D, TPC], F32, kind="ExternalOutput").ap()
    hT_o = nc.dram_tensor("hT_o", [D, TPC], BF16, kind="ExternalOutput").ap()
    cos_o = nc.dram_tensor("cos_o", [128, TPC], F32, kind="ExternalOutput").ap()
    sin_o = nc.dram_tensor("sin_o", [128, TPC], F32, kind="ExternalOutput").ap()
    with ExitStack() as es:
        S = TokPhase(nc, es)
        P = S.P
        S.load_ident(ident)
        g_sb = P.sb("g0", [128, KC], F32)
        g_b = P.buf("g0")
        P.dma("sp", g_sb[:], g0, writes=[g_b])
        xin = [P.sb(f"xin{i}", [128, D], F32) for i in range(2)]
        xinb = [P.buf(f"xin{i}") for i in range(2)]
        for blk in range(TPC // 128):
            s = blk % 2
            tt = blk // 4
            P.dma("sp", xin[s][:], x[blk * 128:(blk + 1) * 128, :], writes=[xinb[s]])
            for c in range(KC):
                bk = c % 4
                P.op("pe", lambda e: e.transpose(S.psb[bk][:, 0:128], xin[s][:, c * 128:(c + 1) * 128], S.ident[:]),
                     reads=[xinb[s], S.identb], writes=[S.psbb[bk]])
                eng = "act" if c % 2 else "dve"
                if eng == "act":
                    P.op("act", lambda e: e.copy(out=S.xT[:, c, blk * 128:(blk + 1) * 128], in_=S.psb[bk][:, 0:128]),
                         reads=[S.psbb[bk]], writes=[S.xb[c][tt]])
                else:
                    P.op("dve", lambda e: e.tensor_copy(out=S.xT[:, c, blk * 128:(blk + 1) * 128], in_=S.psb[bk][:, 0:128]),
                         reads=[S.psbb[bk]], writes=[S.xb[c][tt]])
        xT_ov = xT_o.rearrange("(kc p) t -> p kc t", p=128)
        for c4 in range(4):
            P.dma("sp", xT_ov[:, c4 * 4:(c4 + 1) * 4, :], S.xT[:, c4 * 4:(c4 + 1) * 4, :],
                  reads=[S.xb[c][tt] for c in range(c4 * 4, c4 * 4 + 4) for tt in range(2)],
                  sem_buf=S.xb[c4 * 4][0], is_output=True)
        S.norm_to_hT(g_sb, g_b)
        hT_ov = hT_o.rearrange("(kc p) t -> p kc t", p=128)
        for c4 in range(4):
            P.dma("sp", hT_ov[:, c4 * 4:(c4 + 1) * 4, :], S.hT[:, c4 * 4:(c4 + 1) * 4, :],
                  reads=[S.hb[c][tt] for c in range(c4 * 4, c4 * 4 + 4) for tt in range(2)],
                  sem_buf=S.hb[c4 * 4][0], is_output=True)
        rc = P.sb("ropec", [128, 2], F32)
        rcb = P.buf("ropec")
        P.dma("sp", rc[:], ropec, writes=[rcb])
        pi_ = P.sb("pos_i", [128, TPC], I32)
        pib = P.buf("pos_i")
        P.dma("sp", pi_[:], pos.partition_broadcast(128) if hasattr(pos, "partition_broadcast") else pos,
              writes=[pib])
        ang = P.sb("ang", [128, TPC], F32)
        angb = P.buf("ang")
        kf = P.sb("kf", [128, TPC], F32)
        kfb = P.buf("kf")
        ki = P.sb("ki", [128, TPC], I32)
        kib = P.buf("ki")
        r = P.sb("rr", [128, TPC], F32)
        rb = P.buf("rr")
        m = P.sb("mm", [128, TPC], F32)
        mb = P.buf("mm")
        so = P.sb("so", [128, TPC], F32)
        sob = P.buf("so")
        TWO_PI = 2.0 * math.pi
        C1 = 6.28125
        C2 = TWO_PI - C1
        P.op("dve", lambda e: e.tensor_copy(out=ang[:], in_=pi_[:]), reads=[pib], writes=[angb])
        P.op("dve", lambda e: e.tensor_scalar(out=ang[:], in0=ang[:], scalar1=rc[:, 0:1], scalar2=None, op0=ALU.mult),
             reads=[angb, rcb], writes=[angb])
        P.op("dve", lambda e: e.tensor_scalar(out=kf[:], in0=ang[:], scalar1=1.0 / TWO_PI, scalar2=None, op0=ALU.mult),
             reads=[angb], writes=[kfb])
        P.op("dve", lambda e: e.tensor_copy(out=ki[:], in_=kf[:]), reads=[kfb], writes=[kib])
        P.op("dve", lambda e: e.tensor_copy(out=kf[:], in_=ki[:]), reads=[kib], writes=[kfb])
        P.op("dve", lambda e: e.scalar_tensor_tensor(out=r[:], in0=kf[:], scalar=-C1, in1=ang[:], op0=ALU.mult, op1=ALU.add),
             reads=[kfb, angb], writes=[rb])
        P.op("dve", lambda e: e.scalar_tensor_tensor(out=r[:], in0=kf[:], scalar=-C2, in1=r[:], op0=ALU.mult, op1=ALU.add),
             reads=[kfb, rb], writes=[rb])

        def wrap(t, tb):
            P.op("dve", lambda e: e.tensor_single_scalar(out=m[:], in_=t[:], scalar=math.pi, op=ALU.is_gt),
                 reads=[tb], writes=[mb])
            P.op("dve", lambda e: e.scalar_tensor_tensor(out=t[:], in0=m[:], scalar=-TWO_PI, in1=t[:], op0=ALU.mult, op1=ALU.add),
                 reads=[mb, tb], writes=[tb])
            P.op("dve", lambda e: e.tensor_single_scalar(out=m[:], in_=t[:], scalar=-math.pi, op=ALU.is_lt),
                 reads=[tb], writes=[mb])
            P.op("dve", lambda e: e.scalar_tensor_tensor(out=t[:], in0=m[:], scalar=TWO_PI, in1=t[:], op0=ALU.mult, op1=ALU.add),
                 reads=[mb, tb], writes=[tb])
            P.op("dve", lambda e: e.tensor_scalar(out=t[:], in0=t[:], scalar1=math.pi, scalar2=-math.pi, op0=ALU.min, op1=ALU.max),
                 reads=[tb], writes=[tb])

        wrap(r, rb)
        P.op("act", lambda e: e.activation(out=so[:], in_=r[:], func=AF.Sin), reads=[rb], writes=[sob])
        P.op("dve", lambda e: e.tensor_scalar(out=so[:], in0=so[:], scalar1=rc[:, 1:2], scalar2=None, op0=ALU.mult),
             reads=[sob, rcb], writes=[sob])
        P.dma("sp", sin_o, so[:], reads=[sob], is_output=True)
        P.op("dve", lambda e: e.tensor_scalar(out=r[:], in0=r[:], scalar1=math.pi / 2, scalar2=None, op0=ALU.add),
             reads=[rb], writes=[rb])
        wrap(r, rb)
        P.op("act", lambda e: e.activation(out=kf[:], in_=r[:], func=AF.Sin), reads=[rb], writes=[kfb])
        P.dma("sp", cos_o, kf[:], reads=[kfb], is_output=True)
        P.finish()
    return nc


def build_T(last):
    nc = bass.Bass("TRN2", target_bir_lowering=False)
    dt = lambda n, s, d=F32: nc.dram_tensor(n, list(s), d, kind="ExternalInput").ap()
    xT_i = dt("xT_i", [D, TPC])
    mT_i = dt("mT_i", [D, TPC], BF16)
    w_out = dt("w_out", [D, D])
    w_xq = dt("w_xq", [D, D])
    w_xkv = dt("w_xkv", [D, 2 * D])
    w_xo = dt("w_xo", [D, D])
    w_ff1 = dt("w_ff1", [D, DFF])
    w_ff2 = dt("w_ff2", [DFF, D])
    gains = dt("gains", [128, 4 * KC])
    mem = dt("mem", [MEM, D])
    ident = dt("ident", [128, 128])
    if last:
        out_o = nc.dram_tensor("out_o", [TPC, D], F32, kind="ExternalOutput").ap()
    else:
        xT_o = nc.dram_tensor("xT_o", [D, TPC], F32, kind="ExternalOutput").ap()
        hT_o = nc.dram_tensor("hT_o", [D, TPC], BF16, kind="ExternalOutput").ap()
    with ExitStack() as es:
        S = TokPhase(nc, es)
        P = S.P
        xT, hT, xb, hb = S.xT, S.hT, S.xb, S.hb
        psb, psbb = S.psb, S.psbb
        S.load_ident(ident)
        g_sb = P.sb("gains", [128, 4 * KC], F32)
        g_b = P.buf("gains")
        P.dma("sp", g_sb[:], gains, writes=[g_b])
        gX, gM, gF, gN = (g_sb[:, i * KC:(i + 1) * KC] for i in range(4))
        big2 = P.sb("big2", [128, KC, TPC], BF16)
        b2b = [[P.buf(f"b2_{c}_{tt}") for tt in range(2)] for c in range(KC)]
        xT_iv = xT_i.rearrange("(kc p) t -> p kc t", p=128)
        mT_iv = mT_i.rearrange("(kc p) t -> p kc t", p=128)
        for c4 in range(4):
            cs = slice(c4 * 4, c4 * 4 + 4)
            P.dma("sp", hT[:, cs, :], mT_iv[:, cs, :], writes=[hb[c][tt] for c in range(c4 * 4, c4 * 4 + 4) for tt in range(2)],
                  sem_buf=hb[c4 * 4][0])
        for c4 in range(4):
            cs = slice(c4 * 4, c4 * 4 + 4)
            P.dma("sp", xT[:, cs, :], xT_iv[:, cs, :], writes=[xb[c][tt] for c in range(c4 * 4, c4 * 4 + 4) for tt in range(2)],
                  sem_buf=xb[c4 * 4][0])
        WS = WStream(P, nslot=2)
        t_out = [WS.add(w_out, 0, j * 512) for j in range(4)]
        t_kv = [WS.add(w_xkv, 0, j * 512) for j in range(8)]
        t_xq = [WS.add(w_xq, 0, j * 512) for j in range(4)]
        t_xo = [WS.add(w_xo, 0, j * 512) for j in range(4)]
        t_ff = []
        for hg in range(4):
            t_ff.append(([WS.add(w_ff1, 0, hg * 2048 + j * 512) for j in range(4)],
                         [WS.add(w_ff2, hg * 2048, j * 512) for j in range(4)]))
        bank_ctr = [0]

        def linear(tiles, rhs, rhsb, evac):
            for j, ti in enumerate(tiles):
                wsl, wb = WS.get(ti)
                for oc in range(4):
                    for tt in range(2):
                        bk = bank_ctr[0] % 4
                        bank_ctr[0] += 1
                        ts = slice(tt * 512, (tt + 1) * 512)

                        def mm(e):
                            ins = None
                            for kc in range(KC):
                                ins = e.matmul(psb[bk][:], wsl[:, kc, oc * 128:(oc + 1) * 128], rhs[:, kc, ts],
                                               start=(kc == 0), stop=(kc == KC - 1))
                            return ins
                        P.op("pe", mm, reads=[wb] + [rhsb[kc][tt] for kc in range(KC)], writes=[psbb[bk]])
                        evac(j * 4 + oc, tt, ts, psb[bk], psbb[bk])

        def evac_add_x(c, tt, ts, ps, psb_):
            P.op("dve", lambda e: e.tensor_tensor(out=xT[:, c, ts], in0=ps[:], in1=xT[:, c, ts], op=ALU.add),
                 reads=[psb_, xb[c][tt]], writes=[xb[c][tt]])

        linear(t_out, hT, hb, evac_add_x)

        b2flat = big2[:].rearrange("p a b -> p (a b)")
        mst = b2flat[:, 0:4096].bitcast(F32)
        msq = b2flat[:, 4096:6144]
        memT = b2flat[:, 6144:6144 + KC * MEM].rearrange("p (c m) -> p c m", c=KC)
        memTb = P.buf("memT")
        mstb = P.buf("mst")
        msqb = P.buf("msq")
        mss = P.sb("mss", [128, 4], F32)
        mssb = P.buf("mss")
        for mc in range(2):
            P.dma("sp", mst[:], mem[mc * 128:(mc + 1) * 128, :], writes=[mstb])
            P.op("act", lambda e: e.activation(out=msq[:], in_=mst[:], func=AF.Square, accum_out=mss[:, 0:1]),
                 reads=[mstb], writes=[msqb, mssb])
            P.op("act", lambda e: e.activation(out=mss[:, 1:2], in_=mss[:, 0:1], func=AF.Ln, scale=1.0 / D, bias=S.eps[:, 0:1]),
                 reads=[mssb, S.epsb], writes=[mssb])
            P.op("act", lambda e: e.activation(out=mss[:, 2:3], in_=mss[:, 1:2], func=AF.Exp, scale=-0.5),
                 reads=[mssb], writes=[mssb])
            P.op("dve", lambda e: e.tensor_scalar(out=mst[:], in0=mst[:], scalar1=mss[:, 2:3], scalar2=None, op0=ALU.mult),
                 reads=[mstb, mssb], writes=[mstb])
            for c in range(KC):
                bk = 4 + c % 2
                P.op("pe", lambda e: e.transpose(psb[bk][:, 0:128], mst[:, c * 128:(c + 1) * 128], S.ident[:]),
                     reads=[mstb, S.identb], writes=[psbb[bk]])
                P.op("act", lambda e: e.activation(out=memT[:, c, mc * 128:(mc + 1) * 128], in_=psb[bk][:, 0:128],
                                                   func=AF.Copy, scale=gM[:, c:c + 1]),
                     reads=[psbb[bk], g_b], writes=[memTb])
        kT = P.sb("kT", [128, KC, MEM], BF16)
        kTb = P.buf("kT")
        V = P.sb("V", [128, 2, D], BF16)
        Vb = P.buf("V")
        for j in range(4):
            wsl, wb = WS.get(t_kv[j])
            for oc in range(4):
                bk = bank_ctr[0] % 4
                bank_ctr[0] += 1

                def mm(e):
                    ins = None
                    for kc in range(KC):
                        ins = e.matmul(psb[bk][:, 0:MEM], wsl[:, kc, oc * 128:(oc + 1) * 128], memT[:, kc, :],
                                       start=(kc == 0), stop=(kc == KC - 1))
                    return ins
                P.op("pe", mm, reads=[wb, memTb], writes=[psbb[bk]])
                P.op("act", lambda e: e.copy(out=kT[:, j * 4 + oc, :], in_=psb[bk][:, 0:MEM]),
                     reads=[psbb[bk]], writes=[kTb])
        for j in range(4):
            wsl, wb = WS.get(t_kv[4 + j])
            for mc in range(2):
                bk = bank_ctr[0] % 4
                bank_ctr[0] += 1

                def mm(e):
                    ins = None
                    for kc in range(KC):
                        ins = e.matmul(psb[bk][:], memT[:, kc, mc * 128:(mc + 1) * 128], wsl[:, kc, :],
                                       start=(kc == 0), stop=(kc == KC - 1))
                    return ins
                P.op("pe", mm, reads=[wb, memTb], writes=[psbb[bk]])
                P.op("act", lambda e: e.copy(out=V[:, mc, j * 512:(j + 1) * 512], in_=psb[bk][:]),
                     reads=[psbb[bk]], writes=[Vb])
        S.norm_to_hT(gX, g_b)

        def evac_q(c, tt, ts, ps, psb_):
            P.op("act", lambda e: e.copy(out=big2[:, c, ts], in_=ps[:]), reads=[psb_],
                 writes=[b2b[c][tt], mstb, msqb, memTb])
        linear(t_xq, hT, hb, evac_q)
        expP = P.sb("expP", [128, 2, 512], BF16)
        expPb = [P.buf("expP0"), P.buf("expP1")]
        rden = P.sb("rden", [128, 512], F32)
        rdenb = P.buf("rden")
        SCALE = 512.0 ** -0.5
        for hd in range(4):
            for tt in range(2):
                ts = slice(tt * 512, (tt + 1) * 512)
                for mc in range(2):
                    bk = 4 + mc

                    def mm(e):
                        ins = None
                        for dc in range(4):
                            ins = e.matmul(psb[bk][:], kT[:, hd * 4 + dc, mc * 128:(mc + 1) * 128], big2[:, hd * 4 + dc, ts],
                                           start=(dc == 0), stop=(dc == 3))
                        return ins
                    P.op("pe", mm, reads=[kTb] + [b2b[hd * 4 + dc][tt] for dc in range(4)], writes=[psbb[bk]])
                    P.op("act", lambda e: e.activation(out=expP[:, mc, :], in_=psb[bk][:], func=AF.Exp, scale=SCALE),
                         reads=[psbb[bk]], writes=[expPb[mc]])

                def mmd(e):
                    e.matmul(psb[6][:], S.ones[:], expP[:, 0, :], start=True, stop=False)
                    return e.matmul(psb[6][:], S.ones[:], expP[:, 1, :], start=False, stop=True)
                P.op("pe", mmd, reads=[S.onesb] + expPb, writes=[psbb[6]])
                P.op("act", lambda e: e.activation(out=S.tmp[:], in_=psb[6][:], func=AF.Ln), reads=[psbb[6]], writes=[S.tmpb])
                P.op("act", lambda e: e.activation(out=rden[:], in_=S.tmp[:], func=AF.Exp, scale=-1.0),
                     reads=[S.tmpb], writes=[rdenb])
                for dc in range(4):
                    bk = bank_ctr[0] % 4
                    bank_ctr[0] += 1
                    c = hd * 4 + dc

                    def mmo(e):
                        e.matmul(psb[bk][:], V[:, 0, c * 128:(c + 1) * 128], expP[:, 0, :], start=True, stop=False)
                        return e.matmul(psb[bk][:], V[:, 1, c * 128:(c + 1) * 128], expP[:, 1, :], start=False, stop=True)
                    P.op("pe", mmo, reads=[Vb] + expPb, writes=[psbb[bk]])
                    P.op("dve", lambda e: e.tensor_tensor(out=hT[:, c, ts], in0=psb[bk][:], in1=rden[:], op=ALU.mult),
                         reads=[psbb[bk], rdenb], writes=[hb[c][tt]])
        linear(t_xo, hT, hb, evac_add_x)

        S.norm_to_hT(gF, g_b)
        rl = [P.sb(f"rl{i}", [128, 512], F32) for i in range(2)]
        rlb = [P.buf(f"rl{i}") for i in range(2)]
        rl_ctr = [0]

        def evac_h(cc_base):
            def f(c, tt, ts, ps, psb_):
                s = rl_ctr[0] % 2
                rl_ctr[0] += 1
                P.op("act", lambda e: e.activation(out=rl[s][:], in_=ps[:], func=AF.Relu), reads=[psb_], writes=[rlb[s]])
                P.op("dve", lambda e: e.tensor_tensor(out=big2[:, c, ts], in0=rl[s][:], in1=rl[s][:], op=ALU.mult),
                     reads=[rlb[s]], writes=[b2b[c][tt]])
            return f
        for hg in range(4):
            linear(t_ff[hg][0], hT, hb, evac_h(hg))
            linear(t_ff[hg][1], big2, b2b, evac_add_x)

        if not last:
            xT_ov = xT_o.rearrange("(kc p) t -> p kc t", p=128)
            for c4 in range(4):
                P.dma("sp", xT_ov[:, c4 * 4:(c4 + 1) * 4, :], xT[:, c4 * 4:(c4 + 1) * 4, :],
                      reads=[xb[c][tt] for c in range(c4 * 4, c4 * 4 + 4) for tt in range(2)],
                      sem_buf=xb[c4 * 4][0], is_output=True)
            S.norm_to_hT(gN, g_b)
            hT_ov = hT_o.rearrange("(kc p) t -> p kc t", p=128)
            for c4 in range(4):
                P.dma("sp", hT_ov[:, c4 * 4:(c4 + 1) * 4, :], hT[:, c4 * 4:(c4 + 1) * 4, :],
                      reads=[hb[c][tt] for c in range(c4 * 4, c4 * 4 + 4) for tt in range(2)],
                      sem_buf=hb[c4 * 4][0], is_output=True)
        else:
            hflat = hT[:].rearrange("p a b -> p (a b)").bitcast(F32)
            ost = [hflat[:, i * D:(i + 1) * D] for i in range(2)]
            ostb = [P.buf(f"ost{i}") for i in range(2)]
            for ob in ostb:
                for c in range(KC):
                    for tt in range(2):
                        ob.r.extend(hb[c][tt].r)
                        ob.r.append(hb[c][tt].w)
            finv = big2[:].rearrange("p a b -> p (a b)").bitcast(F32)
            finb = P.buf("fin")

            def emit(c, tt, ts):
                P.op("dve", lambda e: e.scalar_tensor_tensor(out=finv[:, c * 512:(c + 1) * 512], in0=xT[:, c, ts],
                                                             scalar=gN[:, c:c + 1], in1=S.rstd[:],
                                                             op0=ALU.mult, op1=ALU.mult),
                     reads=[xb[c][tt], g_b, S.rstdb] + [b2b[cc][t2] for cc in range(KC) for t2 in range(2)],
                     writes=[finb])
                if c == KC - 1:
                    for tb in range(4):
                        blk = tt * 4 + tb
                        s = blk % 2
                        for cc in range(KC):
                            bk = cc % 4
                            P.op("pe", lambda e: e.transpose(psb[bk][:, 0:128],
                                                             finv[:, cc * 512 + tb * 128: cc * 512 + (tb + 1) * 128], S.ident[:]),
                                 reads=[finb, S.identb], writes=[psbb[bk]])
                            if cc % 2:
                                P.op("act", lambda e: e.copy(out=ost[s][:, cc * 128:(cc + 1) * 128], in_=psb[bk][:, 0:128]),
                                     reads=[psbb[bk]], writes=[ostb[s]])
                            else:
                                P.op("dve", lambda e: e.tensor_copy(out=ost[s][:, cc * 128:(cc + 1) * 128], in_=psb[bk][:, 0:128]),
                                     reads=[psbb[bk]], writes=[ostb[s]])
                        P.dma("sp", out_o[blk * 128:(blk + 1) * 128, :], ost[s], reads=[ostb[s]], is_output=True)
            S.norm(gN, g_b, emit)
        P.finish()
    return nc


NFM = 10 * 128
NTM = 1284
C_ID, C_DM, C_QD, C_U, C_NS, C_NC = 0, 128, 256, 768, 896, 1024
C_KDEC, C_G128, C_ONE, C_EPS6, C_EPS5, C_LNS, C_HV = 1152, 1153, 1154, 1155, 1156, 1157, 1158
C_RETG = 1162
C_GDNG = C_RETG + 256
C_CONV = C_GDNG + 128
C_MD32 = C_CONV + 24
C_MC0 = C_MD32 + 128
C_MC1 = C_MC0 + 128
NCST = C_MC1 + 128
NEG = -30000.0


def build_B(nt=None):
    nc = bass.Bass("TRN2", target_bir_lowering=False)
    dt = lambda n, s, d=F32: nc.dram_tensor(n, list(s), d, kind="ExternalInput").ap()
    hT_all = dt("hT_all", [D, NTOK], BF16)
    wfm_d = dt("wfm", [D, NFM])
    wtm_d = dt("wtm", [D, NTM])
    cst_d = dt("cst", [128, NCST])
    cos_d = dt("cosT", [128, NTOK])
    sin_d = dt("sinT", [128, NTOK])
    mT_o = nc.dram_tensor("mT_o", [256, NTOK], BF16, kind="ExternalOutput").ap()
    with ExitStack() as es:
        P = Prog(nc, es)
        A = lambda fn, r=(), w=(): P.op("act", fn, r, w)
        V = lambda fn, r=(), w=(): P.op("dve", fn, r, w)
        G = lambda fn, r=(), w=(): P.op("pool", fn, r, w)
        T = lambda fn, r=(), w=(): P.op("pe", fn, r, w)
        cst = P.sb("cst", [128, NCST], F32)
        cstb = P.buf("cst")
        P.dma("sp", cst[:], cst_d, writes=[cstb])
        ident = cst[:, C_ID:C_ID + 128]
        DMt = cst[:, C_DM:C_DM + 128]
        qdec = cst[:, C_QD:C_QD + 512]
        Utri = cst[:, C_U:C_U + 128]
        NEGs = cst[:, C_NS:C_NS + 128]
        NEGc = cst[:, C_NC:C_NC + 128]
        col = lambda i: cst[:, i:i + 1]
        MD32 = cst[:, C_MD32:C_MD32 + 128]
        MC0 = cst[:, C_MC0:C_MC0 + 128]
        MC1 = cst[:, C_MC1:C_MC1 + 128]
        retg = cst[:, C_RETG:C_RETG + 256]
        gdng = cst[:, C_GDNG:C_GDNG + 128]
        identb = P.sb("identb", [128, 128], BF16)
        identbb = P.buf("identb")
        V(lambda e: e.tensor_copy(out=identb[:], in_=ident), [cstb], [identbb])
        ones_bf = P.sb("ones_bf", [128, 128], BF16)
        ones_f = P.sb("ones_f", [128, 128], F32)
        onesb = P.buf("ones")
        V(lambda e: e.memset(ones_bf[:], 1.0), [], [onesb])
        V(lambda e: e.memset(ones_f[:], 1.0), [], [onesb])
        nea = P.sb("nea", [128, 2], F32)
        neab = P.buf("nea")
        A(lambda e: e.activation(out=nea[:], in_=cst[:, C_HV:C_HV + 2], func=AF.Exp), [cstb], [neab])
        V(lambda e: e.tensor_scalar(out=nea[:], in0=nea[:], scalar1=-1.0, scalar2=None, op0=ALU.mult), [neab], [neab])
        wfm = P.sb("wfm", [128, KC, NFM], BF16)
        wtm = P.sb("wtm", [128, KC, NTM], BF16)
        wfmb = P.buf("wfm")
        wtmb = P.buf("wtm")
        wfm_v = wfm_d.rearrange("(kc p) n -> p kc n", p=128)
        wtm_v = wtm_d.rearrange("(kc p) n -> p kc n", p=128)
        for a, b in ((0, 512), (512, 1024), (1024, NFM)):
            P.dma("pool", wfm[:, :, a:b], wfm_v[:, :, a:b], writes=[wfmb])
        for a, b in ((0, 512), (512, 1024), (1024, NTM)):
            P.dma("pool", wtm[:, :, a:b], wtm_v[:, :, a:b], writes=[wtmb])
        hsl = [P.sb(f"hsl{i}", [128, KC, 512], BF16) for i in range(2)]
        hslb = [P.buf(f"hsl{i}") for i in range(2)]
        cs_sl = [P.sb(f"cs{i}", [128, 2, 512], F32) for i in range(2)]
        cs_b = [P.buf(f"cs{i}") for i in range(2)]
        hT_v = hT_all.rearrange("(kc p) t -> p kc t", p=128)
        bank = [P.ps(f"bank{i}", [128, 512]) for i in range(8)]
        bankb = [P.buf(f"bank{i}", excl=True) for i in range(8)]
        psFM, psFMb = bank[0:2], bankb[0:2]
        psTM, psTMb = bank[0:2], bankb[0:2]
        psN, psNb = bank[2], bankb[2]
        psRo, psRs = bank[2][:, 0:256], bank[2][:, 256:512]
        psRob, psRsb = bankb[2], bankb[2]
        NSM = 5
        small = [(bank[3 + i][:, 0:128], bankb[3 + i]) for i in range(NSM)]
        sm_ctr = [0]

        def sm():
            r = small[sm_ctr[0] % NSM]
            sm_ctr[0] += 1
            return r

        def bfv(ap):
            return ap.bitcast(BF16)[:, 0:128]

        t1 = P.sb("t1", [128, 512], F32)
        t2 = P.sb("t2", [128, 512], F32)
        t1b, t2b = P.buf("t1"), P.buf("t2")
        qr = [P.sb(f"qr{i}", [128, 512], BF16) for i in range(1)]
        qd = [P.sb(f"qd{i}", [128, 512], BF16) for i in range(1)]
        kr = [P.sb(f"kr{i}", [128, 512], BF16) for i in range(1)]
        qrb = [P.buf(f"qr{i}") for i in range(1)]
        qdb = [P.buf(f"qd{i}") for i in range(1)]
        krb = [P.buf(f"kr{i}") for i in range(1)]
        stage = [P.sb(f"stage{j}", [128, 515], F32) for j in range(6)]
        stageb = [P.buf(f"stage{j}") for j in range(6)]
        acc = [P.sb(f"acc{i}", [128, 512], F32) for i in range(2)]
        accb = [P.buf(f"acc{i}") for i in range(2)]
        sl = [P.sb(f"sl{i}", [128, 512], F32) for i in range(2)]
        slb = [P.buf(f"sl{i}") for i in range(2)]
        sqv = P.sb("sqv", [128, 512], BF16)
        sqvb = P.buf("sqv")
        lnn = P.sb("lnn", [128, 512], F32)
        lnnb = P.buf("lnn")
        rn = P.sb("rn", [128, 512], F32)
        rnb = P.buf("rn")
        gqkv = [[P.sb(f"gqkv{s}_{j}", [128, 512], BF16) for j in range(6)] for s in range(1)]
        gqkvb = [[P.buf(f"gqkv{s}_{j}") for j in range(6)] for s in range(1)]
        sg = P.sb("sg", [128, 512], F32)
        smg = P.sb("smg", [128, 512], F32)
        Gt = P.sb("Gt", [128, 512], F32)
        sgb, smgb, Gtb = P.buf("sg"), P.buf("smg"), P.buf("Gt")
        Vr = P.sb("Vr", [128, 256], BF16)
        Vrb = P.buf("Vr")
        sc = P.sb("sc", [128, 32], F32)
        scb = P.buf("sc")
        KD = P.sb("KD", [128, 128], BF16)
        KDb = P.buf("KD")
        Pt = P.sb("Pt", [128, 128], BF16)
        Ptb = P.buf("Pt")
        St = P.sb("St", [128, 256], F32)
        Stf = P.sb("Stf", [128, 256], BF16)
        Stb, Stfb = P.buf("St"), P.buf("Stf")
        bst = P.sb("bst", [128, 8], F32)
        bstb = P.buf("bst")
        yr = P.sb("yr", [128, 256], F32)
        yrb = P.buf("yr")
        mA = P.sb("mA", [128, 256], F32)
        mAb = P.buf("mA")
        mB = P.sb("mB", [128, 256], F32)
        mBb = P.buf("mB")
        mbf = P.sb("mbf", [128, 256], BF16)
        mbfb = P.buf("mbf")
        mTs = [P.sb(f"mTs{i}", [128, 2, 512], BF16) for i in range(2)]
        mTsb = [P.buf(f"mTs{i}") for i in range(2)]
        def per_head(name, shape, dtp):
            return [P.sb(f"{name}{h}", shape, dtp) for h in range(2)], [P.buf(f"{name}{h}") for h in range(2)]
        Gbc, Gbcb = per_head("Gbc", [128, 128], F32)
        LBc, LBcb = per_head("LBc", [128, 128], F32)
        hs, hsb = per_head("hs", [128, 16], F32)
        E1, E1b = per_head("E1", [128, 128], F32)
        E3, E3b = per_head("E3", [128, 128], F32)
        ER, ERb = per_head("ER", [128, 128], F32)
        qg, qgb = per_head("qg", [128, 128], BF16)
        Nm = [[P.sb(f"Nm{h}_{i}", [128, 128], F32) for i in range(2)] for h in range(2)]
        Nmb = [[P.buf(f"Nm{h}_{i}") for i in range(2)] for h in range(2)]
        Mm = [[P.sb(f"Mm{h}_{i}", [128, 128], F32) for i in range(2)] for h in range(2)]
        Mmb = [[P.buf(f"Mm{h}_{i}") for i in range(2)] for h in range(2)]
        Xm = [[P.sb(f"Xm{h}_{i}", [128, 128], F32) for i in range(2)] for h in range(2)]
        Xmb = [[P.buf(f"Xm{h}_{i}") for i in range(2)] for h in range(2)]
        Nf, Nfb = per_head("Nf", [128, 128], F32)
        Mf, Mfb = per_head("Mf", [128, 128], F32)
        Cm, Cmb = per_head("Cm", [128, 2, 128], F32)
        Tn, Tnb = per_head("Tn", [128, 128], F32)
        Pp, Ppb = per_head("Pp", [128, 128], F32)
        Xb, Xbb = per_head("Xb", [128, 128], BF16)
        QKt, QKtb = per_head("QKt", [128, 128], BF16)
        KDg, KDgb = per_head("KDg", [128, 128], BF16)
        VB, VBb = per_head("VB", [128, 128], F32)
        Zt, Ztb = per_head("Zt", [128, 128], BF16)
        Vn, Vnb = per_head("Vn", [128, 128], BF16)
        Sg, Sgb = per_head("Sg", [128, 128], F32)
        Sgf, Sgfb = per_head("Sgf", [128, 128], BF16)
        y1, y1b = per_head("y1", [128, 128], F32)
        junk = P.sb("junk", [128, 256], F32)
        junkb = P.buf("junk")

        mT_ov = mT_o.rearrange("(cc p) t -> p cc t", p=128)
        NT = nt or (NTOK // 512)

        def load_tile(ti):
            s = ti % 2
            t0 = ti * 512
            P.dma("sp", hsl[s][:], hT_v[:, :, t0:t0 + 512], writes=[hslb[s]])
            P.dma("sp", cs_sl[s][:, 0, :], cos_d[:, t0:t0 + 512], writes=[cs_b[s]])
            P.dma("sp", cs_sl[s][:, 1, :], sin_d[:, t0:t0 + 512], writes=[cs_b[s]])

        def fm_stage(ti):
            s = ti % 2
            first = (ti % (SEQ // 512) == 0)
            for j in range(10):
                bk = j % 2

                def mm(e):
                    ins = None
                    for kc in range(KC):
                        ins = e.matmul(psFM[bk][:], wfm[:, kc, j * 128:(j + 1) * 128], hsl[s][:, kc, :],
                                       start=(kc == 0), stop=(kc == KC - 1))
                    return ins
                T(mm, [wfmb, hslb[s]], [psFMb[bk]])
                ps, psb_ = psFM[bk], psFMb[bk]
                if j in (0, 2):
                    V(lambda e: e.tensor_tensor(out=t1[:], in0=ps[:], in1=cs_sl[s][:, 0, :], op=ALU.mult),
                      [psb_, cs_b[s]], [t1b])
                elif j in (1, 3):
                    V(lambda e: e.tensor_tensor(out=t2[:], in0=ps[:], in1=cs_sl[s][:, 1, :], op=ALU.mult),
                      [psb_, cs_b[s]], [t2b])
                    G(lambda e: e.tensor_tensor(out=t1[:], in0=t1[:], in1=t2[:], op=ALU.add), [t1b, t2b], [t1b])
                    if j == 1:
                        A(lambda e: e.copy(out=qr[0][:], in_=t1[:]), [t1b], [qrb[0]])
                        G(lambda e: e.tensor_tensor(out=qd[0][:], in0=t1[:], in1=qdec, op=ALU.mult), [t1b, cstb], [qdb[0]])
                    else:
                        A(lambda e: e.copy(out=kr[0][:], in_=t1[:]), [t1b], [krb[0]])
                else:
                    jj = j - 4
                    stg, stgb = stage[jj], stageb[jj]
                    if first:
                        V(lambda e: e.memset(stg[:, 0:3], 0.0), [], [stgb])
                    A(lambda e: e.copy(out=stg[:, 3:515], in_=ps[:]), [psb_], [stgb])
                    a = jj % 2
                    cw = lambda i: cst[:, C_CONV + jj * 4 + i:C_CONV + jj * 4 + i + 1]
                    V(lambda e: e.tensor_scalar(out=acc[a][:], in0=stg[:, 0:512], scalar1=cw(0), scalar2=None, op0=ALU.mult),
                      [stgb, cstb], [accb[a]])
                    for i in range(1, 4):
                        V(lambda e: e.scalar_tensor_tensor(out=acc[a][:], in0=stg[:, i:i + 512], scalar=cw(i), in1=acc[a][:],
                                                           op0=ALU.mult, op1=ALU.add),
                          [stgb, cstb, accb[a]], [accb[a]])
                    A(lambda e: e.copy(out=stg[:, 0:3], in_=stg[:, 512:515]), [stgb], [stgb])
                    if jj >= 4:
                        A(lambda e: e.activation(out=gqkv[0][jj][:], in_=acc[a][:], func=AF.Silu), [accb[a]], [gqkvb[0][jj]])
                    else:
                        A(lambda e: e.activation(out=sl[a][:], in_=acc[a][:], func=AF.Silu), [accb[a]], [slb[a]])
                        A(lambda e: e.activation(out=sqv[:], in_=sl[a][:], func=AF.Square), [slb[a]], [sqvb])
                        T(lambda e: e.matmul(psN[:], ones_bf[:], sqv[:], start=True, stop=True), [onesb, sqvb], [psNb])
                        A(lambda e: e.activation(out=lnn[:], in_=psN[:], func=AF.Ln, bias=col(C_EPS6)), [psNb, cstb], [lnnb])
                        if jj < 2:
                            A(lambda e: e.activation(out=rn[:], in_=lnn[:], func=AF.Exp, scale=-0.5, bias=col(C_LNS)),
                              [lnnb, cstb], [rnb])
                        else:
                            A(lambda e: e.activation(out=rn[:], in_=lnn[:], func=AF.Exp, scale=-0.5), [lnnb], [rnb])
                        V(lambda e: e.tensor_tensor(out=gqkv[0][jj][:], in0=sl[a][:], in1=rn[:], op=ALU.mult),
                          [slb[a], rnb], [gqkvb[0][jj]])

        def block(ti, bi):
            s = ti % 2
            first = (ti % (SEQ // 512) == 0) and bi == 0
            bs = slice(bi * 128, (bi + 1) * 128)
            def tm(bk, c0, c1):
                def mm(e):
                    ins = None
                    for kc in range(KC):
                        ins = e.matmul(psTM[bk][:, 0:c1 - c0], hsl[s][:, kc, bs], wtm[:, kc, c0:c1],
                                       start=(kc == 0), stop=(kc == KC - 1))
                    return ins
                T(mm, [wtmb, hslb[s]], [psTMb[bk]])
            tm(0, 0, 512)
            A(lambda e: e.activation(out=sg[:], in_=psTM[0][:], func=AF.Silu), [psTMb[0]], [sgb])
            tm(1, 512, 1024)
            A(lambda e: e.activation(out=smg[:], in_=psTM[1][:], func=AF.Sigmoid), [psTMb[1]], [smgb])
            G(lambda e: e.tensor_tensor(out=Gt[:], in0=sg[:], in1=smg[:], op=ALU.mult), [sgb, smgb], [Gtb])
            tm(0, 1024, NTM)
            V(lambda e: e.tensor_copy(out=Vr[:], in_=psTM[0][:, 0:256]), [psTMb[0]], [Vrb])
            V(lambda e: e.tensor_tensor(out=sc[:, 0:2], in0=psTM[0][:, 256:258], in1=cst[:, C_HV + 2:C_HV + 4], op=ALU.add),
              [psTMb[0], cstb], [scb])
            A(lambda e: e.activation(out=sc[:, 2:4], in_=sc[:, 0:2], func=AF.Exp), [scb], [scb])
            A(lambda e: e.activation(out=sc[:, 4:6], in_=sc[:, 2:4], func=AF.Ln, bias=col(C_ONE)), [scb, cstb], [scb])
            V(lambda e: e.tensor_tensor(out=sc[:, 6:8], in0=sc[:, 4:6], in1=nea[:], op=ALU.mult), [scb, neab], [scb])
            A(lambda e: e.activation(out=sc[:, 8:10], in_=psTM[0][:, 258:260], func=AF.Exp, scale=-1.0), [psTMb[0], scb], [scb])
            A(lambda e: e.activation(out=sc[:, 10:12], in_=sc[:, 8:10], func=AF.Ln, bias=col(C_ONE)), [scb, cstb], [scb])
            A(lambda e: e.activation(out=sc[:, 12:14], in_=sc[:, 10:12], func=AF.Exp, scale=-1.0), [scb], [scb])

            if first:
                V(lambda e: e.memset(St[:], 0.0), [], [Stb])
                V(lambda e: e.memset(Stf[:], 0.0), [], [Stfb])
            r1, r1b = sm()
            T(lambda e: e.transpose(bfv(r1), kr[0][:, bs], identb[:]), [krb[0], identbb], [r1b])
            A(lambda e: e.activation(out=KD[:], in_=bfv(r1), func=AF.Copy, scale=col(C_KDEC)), [r1b, cstb], [KDb])
            r2, r2b = sm()
            T(lambda e: e.matmul(r2, kr[0][:, bs], qr[0][:, bs], start=True, stop=True), [krb[0], qrb[0]], [r2b])
            V(lambda e: e.tensor_tensor(out=Pt[:], in0=r2, in1=DMt, op=ALU.mult), [r2b, cstb], [Ptb])

            def mmo(e):
                e.matmul(psRo, Pt[:], Vr[:], start=True, stop=False)
                return e.matmul(psRo, qd[0][:, bs], Stf[:], start=False, stop=True)
            T(mmo, [Ptb, Vrb, qdb[0], Stfb], [psRob])
            T(lambda e: e.matmul(psRs, KD[:], Vr[:], start=True, stop=True), [KDb, Vrb], [psRsb])
            V(lambda e: e.scalar_tensor_tensor(out=St[:], in0=St[:], scalar=col(C_G128), in1=psRs, op0=ALU.mult, op1=ALU.add),
              [Stb, cstb, psRsb], [Stb])
            A(lambda e: e.copy(out=Stf[:], in_=St[:]), [Stb], [Stfb])
            V(lambda e: e.bn_stats(out=bst[:, 0:6], in_=psRo), [psRob], [bstb])
            V(lambda e: e.bn_aggr(out=bst[:, 6:8], in_=bst[:, 0:6]), [bstb], [bstb])
            A(lambda e: e.activation(out=bst[:, 0:1], in_=bst[:, 7:8], func=AF.Ln, bias=col(C_EPS5)), [bstb, cstb], [bstb])
            A(lambda e: e.activation(out=bst[:, 1:2], in_=bst[:, 0:1], func=AF.Exp, scale=-0.5), [bstb], [bstb])
            V(lambda e: e.tensor_scalar(out=yr[:], in0=psRo, scalar1=bst[:, 6:7], scalar2=bst[:, 1:2],
                                        op0=ALU.subtract, op1=ALU.mult), [psRob, bstb], [yrb])
            G(lambda e: e.tensor_tensor(out=yr[:], in0=yr[:], in1=retg, op=ALU.mult), [yrb, cstb], [yrb])
            G(lambda e: e.tensor_tensor(out=mA[:], in0=yr[:], in1=Gt[:, 0:256], op=ALU.mult), [yrb, Gtb], [mAb])

            for h in range(2):
                gq, gk, gv = gqkv[0][h], gqkv[0][2 + h], gqkv[0][4 + h]
                gqb_, gkb_, gvb_ = gqkvb[0][h], gqkvb[0][2 + h], gqkvb[0][4 + h]
                if first:
                    V(lambda e: e.memset(Sg[h][:], 0.0), [], [Sgb[h]])
                    V(lambda e: e.memset(Sgf[h][:], 0.0), [], [Sgfb[h]])
                H, Hb = hs[h], hsb[h]
                V(lambda e: e.tensor_scalar(out=Gbc[h][:], in0=ones_f[:], scalar1=sc[:, 6 + h:7 + h], scalar2=None, op0=ALU.mult),
                  [onesb, scb], [Gbcb[h]])
                V(lambda e: e.tensor_scalar(out=LBc[h][:], in0=ones_f[:], scalar1=sc[:, 10 + h:11 + h], scalar2=-1.0,
                                            op0=ALU.mult, op1=ALU.mult), [onesb, scb], [LBcb[h]])
                pR, pRb = sm()
                T(lambda e: e.matmul(pR, Gbc[h][:], Utri, start=True, stop=True), [Gbcb[h], cstb], [pRb])
                pR2, pR2b = sm()

                def mm2(e):
                    e.matmul(pR2, Gbc[h][:], Utri, start=True, stop=False)
                    return e.matmul(pR2, LBc[h][:], ident, start=False, stop=True)
                T(mm2, [Gbcb[h], LBcb[h], cstb], [pR2b])
                pc, pcb = sm()
                T(lambda e: e.matmul(pc[:, 0:1], Utri, sc[:, 6 + h:7 + h], start=True, stop=True), [cstb, scb], [pcb])
                V(lambda e: e.tensor_scalar(out=H[:, 0:1], in0=pc[:, 0:1], scalar1=-1.0, scalar2=None, op0=ALU.mult), [pcb], [Hb])
                V(lambda e: e.tensor_copy(out=H[:, 1:2], in_=pR[:, 127:128]), [pRb], [Hb])
                A(lambda e: e.activation(out=H[:, 2:3], in_=H[:, 1:2], func=AF.Exp), [Hb], [Hb])
                A(lambda e: e.activation(out=H[:, 3:4], in_=pc[:, 0:1], func=AF.Exp, scale=-1.0, bias=H[:, 1:2]), [pcb, Hb], [Hb])
                V(lambda e: e.tensor_scalar(out=H[:, 5:6], in0=sc[:, 10 + h:11 + h], scalar1=-1.0, scalar2=None, op0=ALU.mult),
                  [scb], [Hb])
                A(lambda e: e.activation(out=H[:, 4:5], in_=pc[:, 0:1], func=AF.Exp, bias=H[:, 5:6]), [pcb, Hb], [Hb])
                V(lambda e: e.tensor_scalar(out=H[:, 4:5], in0=H[:, 4:5], scalar1=-1.0, scalar2=None, op0=ALU.mult), [Hb], [Hb])
                V(lambda e: e.scalar_tensor_tensor(out=E1[h][:], in0=pR2, scalar=H[:, 0:1], in1=NEGs, op0=ALU.add, op1=ALU.add),
                  [pR2b, Hb, cstb], [E1b[h]])
                A(lambda e: e.activation(out=E1[h][:], in_=E1[h][:], func=AF.Exp), [E1b[h]], [E1b[h]])
                V(lambda e: e.scalar_tensor_tensor(out=E3[h][:], in0=pR, scalar=H[:, 0:1], in1=NEGc, op0=ALU.add, op1=ALU.add),
                  [pRb, Hb, cstb], [E3b[h]])
                A(lambda e: e.activation(out=E3[h][:], in_=E3[h][:], func=AF.Exp), [E3b[h]], [E3b[h]])
                A(lambda e: e.activation(out=ER[h][:], in_=pR, func=AF.Exp), [pRb], [ERb[h]])
                G(lambda e: e.tensor_tensor(out=qg[h][:], in0=gq[:, bs], in1=ER[h][:], op=ALU.mult), [gqb_, ERb[h]], [qgb[h]])
                pKK, pKKb = sm()
                T(lambda e: e.matmul(pKK, gk[:, bs], gk[:, bs], start=True, stop=True), [gkb_], [pKKb])
                V(lambda e: e.tensor_tensor(out=Nf[h][:], in0=pKK, in1=E1[h][:], op=ALU.mult), [pKKb, E1b[h]], [Nfb[h]])
                pM, pMb = sm()
                T(lambda e: e.transpose(pM, Nf[h][:], ident), [Nfb[h], cstb], [pMb])
                A(lambda e: e.copy(out=Mf[h][:], in_=pM), [pMb], [Mfb[h]])
                pQK, pQKb = sm()
                T(lambda e: e.matmul(pQK, gk[:, bs], gq[:, bs], start=True, stop=True), [gkb_, gqb_], [pQKb])
                V(lambda e: e.tensor_tensor(out=QKt[h][:], in0=pQK, in1=E3[h][:], op=ALU.mult), [pQKb, E3b[h]], [QKtb[h]])
                G(lambda e: e.tensor_tensor(out=Nm[h][0][:], in0=Nf[h][:], in1=MD32, op=ALU.mult), [Nfb[h], cstb], [Nmb[h][0]])
                G(lambda e: e.tensor_tensor(out=Mm[h][0][:], in0=Mf[h][:], in1=MD32, op=ALU.mult), [Mfb[h], cstb], [Mmb[h][0]])
                G(lambda e: e.tensor_tensor(out=Cm[h][:, 0, :], in0=Mf[h][:], in1=MC0, op=ALU.mult), [Mfb[h], cstb], [Cmb[h]])
                G(lambda e: e.tensor_tensor(out=Cm[h][:, 1, :], in0=Mf[h][:], in1=MC1, op=ALU.mult), [Mfb[h], cstb], [Cmb[h]])
                V(lambda e: e.tensor_tensor(out=Xm[h][0][:], in0=ident, in1=Nm[h][0][:], op=ALU.subtract), [cstb, Nmb[h][0]], [Xmb[h][0]])
                for k in range(1, 5):
                    a, b = (k - 1) % 2, k % 2
                    pm, pmb = sm()
                    T(lambda e: e.matmul(pm, Nm[h][a][:], Mm[h][a][:], start=True, stop=True), [Nmb[h][a], Mmb[h][a]], [pmb])
                    if k < 4:
                        pn, pnb = sm()
                        T(lambda e: e.matmul(pn, Mm[h][a][:], Nm[h][a][:], start=True, stop=True), [Nmb[h][a], Mmb[h][a]], [pnb])
                    A(lambda e: e.copy(out=Mm[h][b][:], in_=pm), [pmb], [Mmb[h][b]])
                    if k < 4:
                        V(lambda e: e.tensor_copy(out=Nm[h][b][:], in_=pn), [pnb], [Nmb[h][b]])
                    px, pxb = sm()
                    T(lambda e: e.matmul(px, Mm[h][b][:], Xm[h][a][:], start=True, stop=True), [Mmb[h][b], Xmb[h][a]], [pxb])
                    V(lambda e: e.tensor_tensor(out=Xm[h][b][:], in0=px, in1=Xm[h][a][:], op=ALU.add), [pxb, Xmb[h][a]], [Xmb[h][b]])
                xc = 0
                for lv in range(2):
                    xn = 1 - xc
                    ptp, ptpb = sm()
                    T(lambda e: e.transpose(ptp, Xm[h][xc][:], ident), [Xmb[h][xc], cstb], [ptpb])
                    A(lambda e: e.copy(out=Tn[h][:], in_=ptp), [ptpb], [Tnb[h]])
                    pp1, pp1b = sm()
                    T(lambda e: e.matmul(pp1, Cm[h][:, lv, :], Xm[h][xc][:], start=True, stop=True), [Cmb[h], Xmb[h][xc]], [pp1b])
                    V(lambda e: e.tensor_copy(out=Pp[h][:], in_=pp1), [pp1b], [Ppb[h]])
                    pq, pqb = sm()
                    T(lambda e: e.matmul(pq, Tn[h][:], Pp[h][:], start=True, stop=True), [Tnb[h], Ppb[h]], [pqb])
                    V(lambda e: e.tensor_tensor(out=Xm[h][xn][:], in0=Xm[h][xc][:], in1=pq, op=ALU.subtract), [Xmb[h][xc], pqb], [Xmb[h][xn]])
                    xc = xn
                A(lambda e: e.copy(out=Xb[h][:], in_=Xm[h][xc][:]), [Xmb[h][xc]], [Xbb[h]])
                X6, X6b = Xb[h], Xbb[h]
                pk, pkb = sm()
                T(lambda e: e.transpose(bfv(pk), gk[:, bs], identb[:]), [gkb_, identbb], [pkb])
                A(lambda e: e.activation(out=KDg[h][:], in_=bfv(pk), func=AF.Copy, scale=H[:, 3:4]), [pkb, Hb], [KDgb[h]])
                pv, pvb = sm()
                T(lambda e: e.transpose(bfv(pv), gv[:, bs], identb[:]), [gvb_, identbb], [pvb])
                A(lambda e: e.activation(out=VB[h][:], in_=bfv(pv), func=AF.Copy, scale=sc[:, 12 + h:13 + h]), [pvb, scb], [VBb[h]])
                pz, pzb = sm()
                T(lambda e: e.matmul(pz, gk[:, bs], Sgf[h][:], start=True, stop=True), [gkb_, Sgfb[h]], [pzb])
                V(lambda e: e.scalar_tensor_tensor(out=Zt[h][:], in0=pz, scalar=H[:, 4:5], in1=VB[h][:], op0=ALU.mult, op1=ALU.add),
                  [pzb, Hb, VBb[h]], [Ztb[h]])
                pvn, pvnb = sm()
                T(lambda e: e.matmul(pvn, X6[:], Zt[h][:], start=True, stop=True), [X6b, Ztb[h]], [pvnb])
                A(lambda e: e.copy(out=Vn[h][:], in_=pvn), [pvnb], [Vnb[h]])
                po, pob = sm()

                def mmg(e):
                    e.matmul(po, qg[h][:], Sgf[h][:], start=True, stop=False)
                    return e.matmul(po, QKt[h][:], Vn[h][:], start=False, stop=True)
                T(mmg, [qgb[h], Sgfb[h], QKtb[h], Vnb[h]], [pob])
                pss, pssb = sm()
                T(lambda e: e.matmul(pss, KDg[h][:], Vn[h][:], start=True, stop=True), [KDgb[h], Vnb[h]], [pssb])
                V(lambda e: e.scalar_tensor_tensor(out=Sg[h][:], in0=Sg[h][:], scalar=H[:, 2:3], in1=pss, op0=ALU.mult, op1=ALU.add),
                  [Sgb[h], Hb, pssb], [Sgb[h]])
                A(lambda e: e.copy(out=Sgf[h][:], in_=Sg[h][:]), [Sgb[h]], [Sgfb[h]])
                A(lambda e: e.activation(out=junk[:, 0:128], in_=po, func=AF.Square, accum_out=H[:, 6:7]), [pob, Hb], [junkb, Hb])
                A(lambda e: e.activation(out=H[:, 7:8], in_=H[:, 6:7], func=AF.Ln, scale=1.0 / 128, bias=col(C_EPS6)), [Hb, cstb], [Hb])
                A(lambda e: e.activation(out=H[:, 8:9], in_=H[:, 7:8], func=AF.Exp, scale=-0.5), [Hb], [Hb])
                V(lambda e: e.scalar_tensor_tensor(out=y1[h][:], in0=po, scalar=H[:, 8:9], in1=gdng, op0=ALU.mult, op1=ALU.mult),
                  [pob, Hb, cstb], [y1b[h]])
                G(lambda e: e.tensor_tensor(out=mB[:, h * 128:(h + 1) * 128], in0=y1[h][:], in1=Gt[:, 256 + h * 128:256 + (h + 1) * 128],
                                            op=ALU.mult), [y1b[h], Gtb], [mBb])
            G(lambda e: e.tensor_tensor(out=mbf[:], in0=mA[:], in1=mB[:], op=ALU.add), [mAb, mBb], [mbfb])
            for cc in range(2):
                pt, ptb = sm()
                T(lambda e: e.transpose(bfv(pt), mbf[:, cc * 128:(cc + 1) * 128], identb[:]), [mbfb, identbb], [ptb])
                A(lambda e: e.copy(out=mTs[s][:, cc, bs], in_=bfv(pt)), [ptb], [mTsb[s]])

        load_tile(0)
        for ti in range(NT):
            if ti + 1 < NT:
                load_tile(ti + 1)
            fm_stage(ti)
            for bi in range(4):
                block(ti, bi)
            s = ti % 2
            P.dma("sp", mT_ov[:, :, ti * 512:(ti + 1) * 512], mTs[s][:], reads=[mTsb[s]], is_output=True)
        P.finish()
    return nc


def _ret_consts(c):
    lg = np.log1p(-np.exp2(-5.0 - np.float64(c)))
    idx = np.arange(128)
    jj, ii = idx[:, None], idx[None, :]
    same = (jj // 64) == (ii // 64)
    later = (jj < 64) & (ii >= 64)
    DMt = np.where(same, np.exp(lg * np.abs(ii - jj)), np.where(later, np.exp(lg * (ii - jj)), 0.0))
    DMt = DMt * (128.0 ** -0.5)
    qdec = np.exp(lg * (np.arange(512) % 128 + 1.0))
    kdec = np.exp(lg * (127.0 - idx)) * (128.0 ** -0.5)
    g128 = np.exp(lg * 128.0)
    return DMt, qdec, kdec, g128


def pack_cst(c, conv_w_l, a_log_l, dt_bias_l, ret_gn_g_l, gdn_norm_g_l):
    cst = np.zeros((128, NCST), np.float32)
    idx = np.arange(128)
    cst[:, C_ID:C_ID + 128] = np.eye(128)
    DMt, qdec, kdec, g128 = _ret_consts(c)
    cst[:, C_DM:C_DM + 128] = DMt
    cst[:, C_QD:C_QD + 512] = qdec[None, :]
    cst[:, C_U:C_U + 128] = (idx[:, None] <= idx[None, :])
    cst[:, C_NS:C_NS + 128] = np.where(idx[None, :] > idx[:, None], 0.0, NEG)
    cst[:, C_NC:C_NC + 128] = np.where(idx[None, :] >= idx[:, None], 0.0, NEG)
    cst[:, C_KDEC] = kdec
    cst[:, C_G128] = g128
    cst[:, C_ONE] = 1.0
    cst[:, C_EPS6] = 1e-6
    cst[:, C_EPS5] = 1e-5
    cst[:, C_LNS] = math.log(128.0 ** -0.5)
    cst[:, C_HV:C_HV + 2] = a_log_l[None, 2 * c:2 * c + 2]
    cst[:, C_HV + 2:C_HV + 4] = dt_bias_l[None, 2 * c:2 * c + 2]
    cst[:, C_RETG:C_RETG + 256] = ret_gn_g_l[None, c * 256:(c + 1) * 256]
    cst[:, C_GDNG:C_GDNG + 128] = gdn_norm_g_l[None, :]
    b32, b64 = idx // 32, idx // 64
    cst[:, C_MD32:C_MD32 + 128] = (b32[:, None] == b32[None, :])
    cst[:, C_MC0:C_MC0 + 128] = (b64[:, None] == b64[None, :]) & (b32[:, None] != b32[None, :])
    cst[:, C_MC1:C_MC1 + 128] = (b64[:, None] != b64[None, :])
    for jj in range(6):
        grp, h = jj // 2, jj % 2
        ch0 = grp * 2048 + (2 * c + h) * 128
        cst[:, C_CONV + jj * 4:C_CONV + jj * 4 + 4] = conv_w_l[:, ch0:ch0 + 128].T
    return cst


def pack_w_in(c, w_in_l):
    sw = (np.arange(128) + 64) % 128
    rq = w_in_l[:, O_RQ + c * 128:O_RQ + (c + 1) * 128]
    rk = w_in_l[:, O_RK + c * 128:O_RK + (c + 1) * 128]
    cols = [rq, rq[:, sw], rk, rk[:, sw]]
    for grp in range(3):
        for h in range(2):
            o = O_GQKV + grp * 2048 + (2 * c + h) * 128
            cols.append(w_in_l[:, o:o + 128])
    wfm = np.ascontiguousarray(np.concatenate(cols, axis=1))
    wtm = np.ascontiguousarray(np.concatenate([
        w_in_l[:, O_RG + c * 256:O_RG + (c + 1) * 256],
        w_in_l[:, O_GZ + c * 256:O_GZ + (c + 1) * 256],
        w_in_l[:, O_MA + c * 256:O_MA + (c + 1) * 256],
        w_in_l[:, O_MB + c * 256:O_MB + (c + 1) * 256],
        w_in_l[:, O_RV + c * 256:O_RV + (c + 1) * 256],
        w_in_l[:, O_GA + 2 * c:O_GA + 2 * c + 2],
        w_in_l[:, O_GB + 2 * c:O_GB + 2 * c + 2]], axis=1))
    return wfm, wtm


_PROGS = {}


def _prog(name):
    if name not in _PROGS:
        _PROGS[name] = {"A0": build_A0, "B": build_B, "T": lambda: build_T(False), "TL": lambda: build_T(True)}[name]()
    return _PROGS[name]


def _run(name, in_maps):
    res = run_bass_kernel_spmd(_prog(name), in_maps, core_ids=list(range(NCORES)))
    return res.results


def kernel(x, mem, positions, norm_mix_g, w_in, conv_w, gdn_a_log, gdn_dt_bias, ret_gn_g, gdn_norm_g,
           w_out, norm_x_g, norm_mem_g, w_xq, w_xkv, w_xo, norm_ffn_g, w_ff1, w_ff2, norm_final_g):
    f = lambda a: np.ascontiguousarray(np.asarray(a, dtype=np.float32))
    x = f(x).reshape(NTOK, D)
    mem = f(mem)
    pos = np.ascontiguousarray(np.asarray(positions, dtype=np.int32)).reshape(NTOK)
    norm_mix_g, norm_x_g, norm_mem_g, norm_ffn_g, norm_final_g = map(f, (norm_mix_g, norm_x_g, norm_mem_g, norm_ffn_g, norm_final_g))
    w_in, conv_w, gdn_a_log, gdn_dt_bias, ret_gn_g, gdn_norm_g = map(f, (w_in, conv_w, gdn_a_log, gdn_dt_bias, ret_gn_g, gdn_norm_g))
    w_out, w_xq, w_xkv, w_xo, w_ff1, w_ff2 = map(f, (w_out, w_xq, w_xkv, w_xo, w_ff1, w_ff2))
    ident = np.eye(128, dtype=np.float32)
    inv_freq = (1.0 / (np.float32(10000.0) ** (np.arange(0, 128, 2, dtype=np.float32) / np.float32(128)))).astype(np.float32)
    ropec = np.zeros((128, 2), np.float32)
    ropec[:, 0] = np.concatenate([inv_freq, inv_freq])
    ropec[:64, 1] = -1.0
    ropec[64:, 1] = 1.0
    tsl = lambda c: slice(c * TPC, (c + 1) * TPC)

    r = _run("A0", [dict(x=x[tsl(c)], pos=pos[tsl(c)].reshape(1, TPC), g0=_lay_g(norm_mix_g[0]), ident=ident, ropec=ropec)
                    for c in range(NCORES)])
    xT = [r[c]["xT_o"] for c in range(NCORES)]
    hT = [r[c]["hT_o"] for c in range(NCORES)]
    cosT = np.ascontiguousarray(np.concatenate([r[c]["cos_o"] for c in range(NCORES)], axis=1))
    sinT = np.ascontiguousarray(np.concatenate([r[c]["sin_o"] for c in range(NCORES)], axis=1))
    out = None
    for l in range(DEPTH):
        hT_all = np.ascontiguousarray(np.concatenate(hT, axis=1))
        maps = []
        for c in range(NCORES):
            wfm, wtm = pack_w_in(c, w_in[l])
            maps.append(dict(hT_all=hT_all, wfm=wfm, wtm=wtm,
                             cst=pack_cst(c, conv_w[l], gdn_a_log[l], gdn_dt_bias[l], ret_gn_g[l], gdn_norm_g[l]),
                             cosT=cosT, sinT=sinT))
        r = _run("B", maps)
        mT_full = np.concatenate([r[c]["mT_o"] for c in range(NCORES)], axis=0)
        last = (l == DEPTH - 1)
        g_next = norm_final_g if last else norm_mix_g[l + 1]
        gains = np.ascontiguousarray(np.concatenate([_lay_g(norm_x_g[l]), _lay_g(norm_mem_g[l]), _lay_g(norm_ffn_g[l]),
                                                     _lay_g(g_next)], axis=1))
        maps = [dict(xT_i=xT[c], mT_i=np.ascontiguousarray(mT_full[:, tsl(c)]), w_out=w_out[l], w_xq=w_xq[l], w_xkv=w_xkv[l],
                     w_xo=w_xo[l], w_ff1=w_ff1[l], w_ff2=w_ff2[l], gains=gains, mem=mem[c // 4], ident=ident)
                for c in range(NCORES)]
        r = _run("TL" if last else "T", maps)
        if last:
            out = np.concatenate([r[c]["out_o"] for c in range(NCORES)], axis=0)
        else:
            xT = [r[c]["xT_o"] for c in range(NCORES)]
            hT = [r[c]["hT_o"] for c in range(NCORES)]
    return np.ascontiguousarray(out.reshape(BATCH, SEQ, D).astype(np.float32))
```

```python
import math
from contextlib import ExitStack

import numpy as np
import ml_dtypes

import concourse.bass as bass
import concourse.mybir as mybir
from concourse.bass_utils import run_bass_kernel_spmd

F32 = mybir.dt.float32
BF16 = mybir.dt.bfloat16
I32 = mybir.dt.int32
AF = mybir.ActivationFunctionType
ALU = mybir.AluOpType
AX = mybir.AxisListType

NCORES = 8
D = 2048
KC = 16
SEQ = 4096
BATCH = 2
NTOK = BATCH * SEQ
TPC = NTOK // NCORES
DEPTH = 4
MEM = 256
DFF = 8192
EPS = 1e-6
RET_HEADS = 8

O_RQ, O_RK, O_RV, O_RG = 0, 1024, 2048, 4096
O_GQKV = 6144
O_GA = O_GQKV + 6144
O_GB = O_GA + 16
O_GZ = O_GB + 16
O_MA = O_GZ + 2048
O_MB = O_MA + 2048


class Buf:
    __slots__ = ("name", "w", "r", "dsem", "excl")

    def __init__(self, name, excl=False):
        self.name = name
        self.w = None
        self.r = []
        self.dsem = None
        self.excl = excl


class Prog:
    ENG = ("pe", "act", "dve", "pool", "sp")

    def __init__(self, nc, es, same_sync=("act", "dve", "pool", "sp")):
        self.nc = nc
        self.es = es
        self.eng = dict(pe=nc.tensor, act=nc.scalar, dve=nc.vector, pool=nc.gpsimd, sp=nc.sync)
        self.sems = []
        self.esem = {}
        for e in self.ENG:
            self.esem[e] = self._newsem("e_" + e, False)
        self.seen = {e: {} for e in self.ENG}
        self.same_sync = same_sync
        self.out_toks = []
        self.nbuf = 0
        self.free_dsems = {}
        self.scope = None
        self.banks = [es.enter_context(nc.psum_tensor(f"p_bank{i}", [128, 512], F32)) for i in range(8)]
        self.bankb = [Buf(f"bank{i}", True) for i in range(8)]

    def _newsem(self, name, is_dma):
        h = self.es.enter_context(self.nc.semaphore(name))
        self.sems.append([h, 0, is_dma])
        return len(self.sems) - 1

    def buf(self, name=None, excl=False):
        self.nbuf += 1
        b = Buf(name or f"b{self.nbuf}", excl)
        if self.scope is not None:
            self.scope[1].append(b)
        return b

    def sb(self, name, shape, dt):
        st = self.scope[0] if self.scope is not None else self.es
        self.nbuf += 1
        return st.enter_context(self.nc.sbuf_tensor(f"s{self.nbuf}_" + name, list(shape), dt))

    def push_scope(self):
        assert self.scope is None
        self.scope = (ExitStack(), [])

    def pop_scope(self):
        self.barrier()
        st, bufs = self.scope
        for b in bufs:
            if b.dsem:
                for qc, k in b.dsem.items():
                    self.free_dsems.setdefault(qc, []).append(k)
                b.dsem = None
        st.close()
        self.scope = None
        for b in self.bankb:
            b.w = None
            b.r = []

    def barrier(self):
        toks = [(k, v[1]) for k, v in enumerate(self.sems) if v[1] > 0]
        for e in self.ENG:
            self._wait(e, toks)

    def collective(self, kind, in_ap, out_ap):
        self.barrier()
        k = self._newsem(f"cc{len(self.sems)}", True)
        inst = self.nc.gpsimd.collective_compute(kind, ALU.bypass, replica_groups=[list(range(NCORES))],
                                                 ins=[in_ap.opt()], outs=[out_ap.opt()])
        inst.then_inc(self.sems[k][0])
        self.sems[k][1] = 1
        self.barrier()

    def ps(self, name, shape, dt=F32):
        return self.es.enter_context(self.nc.psum_tensor("p_" + name, list(shape), dt))

    def _wait(self, e, toks):
        need = {}
        for t in toks:
            if t is None:
                continue
            k, v = t
            if need.get(k, 0) < v:
                need[k] = v
        for k, v in need.items():
            h, issued, is_dma = self.sems[k]
            if is_dma:
                v = issued
            if k == self.esem[e] and e not in self.same_sync:
                continue
            if self.seen[e].get(k, 0) >= v:
                continue
            self.seen[e][k] = v
            self.eng[e].wait_ge(h, v)

    def _deps(self, reads, writes):
        toks = []
        for b in reads:
            toks.append(b.w)
        for b in writes:
            toks.append(b.w)
            toks.extend(b.r)
        return toks

    def _commit(self, tok, reads, writes):
        for b in reads:
            b.r.append(tok)
        for b in writes:
            b.w = tok
            b.r = []

    def op(self, e, fn, reads=(), writes=()):
        if any(b.excl for b in reads):
            writes = list(writes) + [b for b in reads if b.excl]
            reads = [b for b in reads if not b.excl]
        self._wait(e, self._deps(reads, writes))
        inst = fn(self.eng[e])
        k = self.esem[e]
        self.sems[k][1] += 1
        inst.then_inc(self.sems[k][0], 1)
        tok = (k, self.sems[k][1])
        self._commit(tok, reads, writes)
        return tok

    def dma(self, q, out, in_, reads=(), writes=(), sem_buf=None, is_output=False, **kw):
        self._wait(q, self._deps(reads, writes))
        sb = sem_buf or (writes[0] if writes else reads[0])
        if sb.dsem is None:
            sb.dsem = {}
        qc = "sw" if q == "pool" else "hw"
        if qc not in sb.dsem:
            fl = self.free_dsems.setdefault(qc, [])
            sb.dsem[qc] = fl.pop() if fl else self._newsem(f"d{len(self.sems)}", True)
        k = sb.dsem[qc]
        inst = self.eng[q].dma_start(out=out, in_=in_, **kw)
        self.sems[k][1] += 16
        inst.then_inc(self.sems[k][0], 16)
        tok = (k, self.sems[k][1])
        self._commit(tok, reads, writes)
        if is_output:
            self.out_toks.append(tok)
        return tok

    def finish(self):
        self.barrier()


class WStream:
    def __init__(self, P, nslot=2):
        self.P = P
        self.n = nslot
        self.slots = [P.sb(f"wslot{i}", [128, KC, 512], BF16) for i in range(nslot)]
        self.bufs = [P.buf(f"wslot{i}") for i in range(nslot)]
        self.tiles = []
        self.issued = 0

    def add(self, w_ap, r0, c0):
        self.tiles.append(w_ap[r0:r0 + 2048, c0:c0 + 512].rearrange("(kc p) n -> p kc n", p=128))
        return len(self.tiles) - 1

    def _issue(self, i):
        s = i % self.n
        self.P.dma("pool", self.slots[s][:], self.tiles[i], writes=[self.bufs[s]])

    def get(self, i):
        while self.issued <= min(i + self.n - 1, len(self.tiles) - 1):
            self._issue(self.issued)
            self.issued += 1
        s = i % self.n
        return self.slots[s], self.bufs[s]


def _consts():
    ident = np.eye(128, dtype=np.float32)
    return ident


def _lay_g(g):
    return np.ascontiguousarray(np.asarray(g, np.float32).reshape(KC, 128).T)


class TokPhase:
    def __init__(self, P):
        self.nc = P.nc
        self.P = P
        self.xT = P.sb("xT", [128, KC, TPC], F32)
        self.xb = [[P.buf(f"x{c}_{tt}") for tt in range(2)] for c in range(KC)]
        self.hT = P.sb("hT", [128, KC, TPC], BF16)
        self.hb = [[P.buf(f"h{c}_{tt}") for tt in range(2)] for c in range(KC)]
        self.ident = P.sb("ident", [128, 128], F32)
        self.identb = P.buf("ident")
        self.ones = P.sb("ones", [128, 128], BF16)
        self.onesb = P.buf("ones")
        self.eps = P.sb("eps", [128, 1], F32)
        self.epsb = P.buf("eps")
        self.sq = [P.sb(f"sq{i}", [128, 512], BF16) for i in range(2)]
        self.sqb = [P.buf(f"sq{i}") for i in range(2)]
        self.tmp = P.sb("tmpn", [128, 512], F32)
        self.tmpb = P.buf("tmpn")
        self.rstd = P.sb("rstd", [128, 512], F32)
        self.rstdb = P.buf("rstd")
        self.psb = P.banks
        self.psbb = P.bankb
        P.op("dve", lambda e: e.memset(self.ones[:], 1.0), writes=[self.onesb])
        P.op("dve", lambda e: e.memset(self.eps[:], EPS), writes=[self.epsb])

    def load_ident(self, ident_ap):
        self.P.dma("sp", self.ident[:], ident_ap, writes=[self.identb])

    def norm(self, g_sb, g_b, emit, bank=4):
        P = self.P
        ps_n, ps_nb = self.psb[bank], self.psbb[bank]
        for tt in range(2):
            ts = slice(tt * 512, (tt + 1) * 512)
            for c in range(KC):
                s = c % 2
                P.op("act", lambda e: e.activation(out=self.sq[s][:], in_=self.xT[:, c, ts], func=AF.Square),
                     reads=[self.xb[c][tt]], writes=[self.sqb[s]])
                P.op("pe", lambda e: e.matmul(ps_n[:], self.ones[:], self.sq[s][:], start=(c == 0), stop=(c == KC - 1)),
                     reads=[self.sqb[s], self.onesb], writes=[ps_nb])
            P.op("act", lambda e: e.activation(out=self.tmp[:], in_=ps_n[:], func=AF.Ln, scale=1.0 / D,
                                               bias=self.eps[:, 0:1]),
                 reads=[ps_nb, self.epsb], writes=[self.tmpb])
            P.op("act", lambda e: e.activation(out=self.rstd[:], in_=self.tmp[:], func=AF.Exp, scale=-0.5),
                 reads=[self.tmpb], writes=[self.rstdb])
            for c in range(KC):
                emit(c, tt, ts)

    def norm_to_hT(self, g_sb, g_b):
        P = self.P

        def emit(c, tt, ts):
            P.op("dve", lambda e: e.scalar_tensor_tensor(out=self.hT[:, c, ts], in0=self.xT[:, c, ts],
                                                         scalar=g_sb[:, c:c + 1], in1=self.rstd[:],
                                                         op0=ALU.mult, op1=ALU.mult),
                 reads=[self.xb[c][tt], g_b, self.rstdb], writes=[self.hb[c][tt]])
        self.norm(g_sb, g_b, emit)


def emit_A0(P, io):
    nc = P.nc
    x, pos, g0, ident, ropec = io["x"], io["pos"], io["g0"], io["ident"], io["ropec"]
    xT_o, hT_o, cos_o, sin_o = io["xT_o"], io["hT_o"], io["cos_o"], io["sin_o"]
    if True:
        S = TokPhase(P)
        S.load_ident(ident)
        g_sb = P.sb("g0", [128, KC], F32)
        g_b = P.buf("g0")
        P.dma("sp", g_sb[:], g0, writes=[g_b])
        xin = [P.sb(f"xin{i}", [128, D], F32) for i in range(2)]
        xinb = [P.buf(f"xin{i}") for i in range(2)]
        for blk in range(TPC // 128):
            s = blk % 2
            tt = blk // 4
            P.dma("sp", xin[s][:], x[blk * 128:(blk + 1) * 128, :], writes=[xinb[s]])
            for c in range(KC):
                bk = c % 4
                P.op("pe", lambda e: e.transpose(S.psb[bk][:, 0:128], xin[s][:, c * 128:(c + 1) * 128], S.ident[:]),
                     reads=[xinb[s], S.identb], writes=[S.psbb[bk]])
                eng = "act" if c % 2 else "dve"
                if eng == "act":
                    P.op("act", lambda e: e.copy(out=S.xT[:, c, blk * 128:(blk + 1) * 128], in_=S.psb[bk][:, 0:128]),
                         reads=[S.psbb[bk]], writes=[S.xb[c][tt]])
                else:
                    P.op("dve", lambda e: e.tensor_copy(out=S.xT[:, c, blk * 128:(blk + 1) * 128], in_=S.psb[bk][:, 0:128]),
                         reads=[S.psbb[bk]], writes=[S.xb[c][tt]])
        xT_ov = xT_o.rearrange("(kc p) t -> p kc t", p=128)
        for c4 in range(4):
            P.dma("sp", xT_ov[:, c4 * 4:(c4 + 1) * 4, :], S.xT[:, c4 * 4:(c4 + 1) * 4, :],
                  reads=[S.xb[c][tt] for c in range(c4 * 4, c4 * 4 + 4) for tt in range(2)],
                  sem_buf=S.xb[c4 * 4][0])
        S.norm_to_hT(g_sb, g_b)
        hT_ov = hT_o.rearrange("(kc p) t -> p kc t", p=128)
        for c4 in range(4):
            P.dma("sp", hT_ov[:, c4 * 4:(c4 + 1) * 4, :], S.hT[:, c4 * 4:(c4 + 1) * 4, :],
                  reads=[S.hb[c][tt] for c in range(c4 * 4, c4 * 4 + 4) for tt in range(2)],
                  sem_buf=S.hb[c4 * 4][0])
        rc = P.sb("ropec", [128, 2], F32)
        rcb = P.buf("ropec")
        P.dma("sp", rc[:], ropec, writes=[rcb])
        pi_ = P.sb("pos_i", [128, TPC], I32)
        pib = P.buf("pos_i")
        P.dma("sp", pi_[:], pos.partition_broadcast(128) if hasattr(pos, "partition_broadcast") else pos,
              writes=[pib])
        ang = P.sb("ang", [128, TPC], F32)
        angb = P.buf("ang")
        kf = P.sb("kf", [128, TPC], F32)
        kfb = P.buf("kf")
        ki = P.sb("ki", [128, TPC], I32)
        kib = P.buf("ki")
        r = P.sb("rr", [128, TPC], F32)
        rb = P.buf("rr")
        m = P.sb("mm", [128, TPC], F32)
        mb = P.buf("mm")
        so = P.sb("so", [128, TPC], F32)
        sob = P.buf("so")
        TWO_PI = 2.0 * math.pi
        C1 = 6.28125
        C2 = TWO_PI - C1
        P.op("dve", lambda e: e.tensor_copy(out=ang[:], in_=pi_[:]), reads=[pib], writes=[angb])
        P.op("dve", lambda e: e.tensor_scalar(out=ang[:], in0=ang[:], scalar1=rc[:, 0:1], scalar2=None, op0=ALU.mult),
             reads=[angb, rcb], writes=[angb])
        P.op("dve", lambda e: e.tensor_scalar(out=kf[:], in0=ang[:], scalar1=1.0 / TWO_PI, scalar2=None, op0=ALU.mult),
             reads=[angb], writes=[kfb])
        P.op("dve", lambda e: e.tensor_copy(out=ki[:], in_=kf[:]), reads=[kfb], writes=[kib])
        P.op("dve", lambda e: e.tensor_copy(out=kf[:], in_=ki[:]), reads=[kib], writes=[kfb])
        P.op("dve", lambda e: e.scalar_tensor_tensor(out=r[:], in0=kf[:], scalar=-C1, in1=ang[:], op0=ALU.mult, op1=ALU.add),
             reads=[kfb, angb], writes=[rb])
        P.op("dve", lambda e: e.scalar_tensor_tensor(out=r[:], in0=kf[:], scalar=-C2, in1=r[:], op0=ALU.mult, op1=ALU.add),
             reads=[kfb, rb], writes=[rb])

        def wrap(t, tb):
            P.op("dve", lambda e: e.tensor_single_scalar(out=m[:], in_=t[:], scalar=math.pi, op=ALU.is_gt),
                 reads=[tb], writes=[mb])
            P.op("dve", lambda e: e.scalar_tensor_tensor(out=t[:], in0=m[:], scalar=-TWO_PI, in1=t[:], op0=ALU.mult, op1=ALU.add),
                 reads=[mb, tb], writes=[tb])
            P.op("dve", lambda e: e.tensor_single_scalar(out=m[:], in_=t[:], scalar=-math.pi, op=ALU.is_lt),
                 reads=[tb], writes=[mb])
            P.op("dve", lambda e: e.scalar_tensor_tensor(out=t[:], in0=m[:], scalar=TWO_PI, in1=t[:], op0=ALU.mult, op1=ALU.add),
                 reads=[mb, tb], writes=[tb])
            P.op("dve", lambda e: e.tensor_scalar(out=t[:], in0=t[:], scalar1=math.pi, scalar2=-math.pi, op0=ALU.min, op1=ALU.max),
                 reads=[tb], writes=[tb])

        wrap(r, rb)
        P.op("act", lambda e: e.activation(out=so[:], in_=r[:], func=AF.Sin), reads=[rb], writes=[sob])
        P.op("dve", lambda e: e.tensor_scalar(out=so[:], in0=so[:], scalar1=rc[:, 1:2], scalar2=None, op0=ALU.mult),
             reads=[sob, rcb], writes=[sob])
        P.dma("sp", sin_o, so[:], reads=[sob])
        P.op("dve", lambda e: e.tensor_scalar(out=r[:], in0=r[:], scalar1=math.pi / 2, scalar2=None, op0=ALU.add),
             reads=[rb], writes=[rb])
        wrap(r, rb)
        P.op("act", lambda e: e.activation(out=kf[:], in_=r[:], func=AF.Sin), reads=[rb], writes=[kfb])
        P.dma("sp", cos_o, kf[:], reads=[kfb])


def emit_T(P, io, last):
    nc = P.nc
    xT_i, mT_ag, selm = io["xT_d"], io["mT_ag"], io["selm"]
    w_out, w_xq, w_xkv, w_xo, w_ff1, w_ff2 = (io[k] for k in ("w_out", "w_xq", "w_xkv", "w_xo", "w_ff1", "w_ff2"))
    gains, mem, ident = io["gains"], io["mem"], io["ident"]
    if last:
        out_o = io["out_o"]
    else:
        xT_o, hT_o = io.get("xT_o", io["xT_d"]), io["hT_o"]
    if True:
        S = TokPhase(P)
        xT, hT, xb, hb = S.xT, S.hT, S.xb, S.hb
        psb, psbb = S.psb, S.psbb
        S.load_ident(ident)
        g_sb = P.sb("gains", [128, 4 * KC], F32)
        g_b = P.buf("gains")
        P.dma("sp", g_sb[:], gains, writes=[g_b])
        gX, gM, gF, gN = (g_sb[:, i * KC:(i + 1) * KC] for i in range(4))
        big2 = P.sb("big2", [128, KC, TPC], BF16)
        b2b = [[P.buf(f"b2_{c}_{tt}") for tt in range(2)] for c in range(KC)]
        xT_iv = xT_i.rearrange("(kc p) t -> p kc t", p=128)
        sel_sb = P.sb("selm", [128, NCORES], F32)
        sel_b = P.buf("selm")
        P.dma("sp", sel_sb[:], selm, writes=[sel_b])
        mT_v = mT_ag.rearrange("(kc p) t -> p kc t", p=128)
        allh = [hb[c][tt] for c in range(KC) for tt in range(2)]
        for tt in range(2):
            hts = [hb[c][tt] for c in range(KC)]
            for r in range(NCORES):
                half = r % 2
                hbuf = [b2b[c][half] for c in range(KC)]
                P.dma("sp", big2[:, :, half * 512:(half + 1) * 512], mT_v[:, :, r * TPC + tt * 512:r * TPC + (tt + 1) * 512],
                      writes=hbuf, sem_buf=hbuf[0])
                if r == 0:
                    P.op("dve", lambda e: e.tensor_scalar(out=hT[:, :, tt * 512:(tt + 1) * 512], in0=big2[:, :, half * 512:(half + 1) * 512],
                                                          scalar1=sel_sb[:, r:r + 1], scalar2=None, op0=ALU.mult),
                         reads=hbuf + [sel_b], writes=hts)
                else:
                    P.op("dve", lambda e: e.scalar_tensor_tensor(out=hT[:, :, tt * 512:(tt + 1) * 512], in0=big2[:, :, half * 512:(half + 1) * 512],
                                                                 scalar=sel_sb[:, r:r + 1], in1=hT[:, :, tt * 512:(tt + 1) * 512],
                                                                 op0=ALU.mult, op1=ALU.add),
                         reads=hbuf + [sel_b], writes=hts)
        for c4 in range(4):
            cs = slice(c4 * 4, c4 * 4 + 4)
            P.dma("sp", xT[:, cs, :], xT_iv[:, cs, :], writes=[xb[c][tt] for c in range(c4 * 4, c4 * 4 + 4) for tt in range(2)],
                  sem_buf=xb[c4 * 4][0])
        WS = WStream(P, nslot=2)
        t_out = [WS.add(w_out, 0, j * 512) for j in range(4)]
        t_kv = [WS.add(w_xkv, 0, j * 512) for j in range(8)]
        t_xq = [WS.add(w_xq, 0, j * 512) for j in range(4)]
        t_xo = [WS.add(w_xo, 0, j * 512) for j in range(4)]
        t_ff = []
        for hg in range(4):
            t_ff.append(([WS.add(w_ff1, 0, hg * 2048 + j * 512) for j in range(4)],
                         [WS.add(w_ff2, hg * 2048, j * 512) for j in range(4)]))
        bank_ctr = [0]

        def linear(tiles, rhs, rhsb, evac):
            for j, ti in enumerate(tiles):
                wsl, wb = WS.get(ti)
                for oc in range(4):
                    for tt in range(2):
                        bk = bank_ctr[0] % 4
                        bank_ctr[0] += 1
                        ts = slice(tt * 512, (tt + 1) * 512)

                        def mm(e):
                            ins = None
                            for kc in range(KC):
                                ins = e.matmul(psb[bk][:], wsl[:, kc, oc * 128:(oc + 1) * 128], rhs[:, kc, ts],
                                               start=(kc == 0), stop=(kc == KC - 1))
                            return ins
                        P.op("pe", mm, reads=[wb] + [rhsb[kc][tt] for kc in range(KC)], writes=[psbb[bk]])
                        evac(j * 4 + oc, tt, ts, psb[bk], psbb[bk])

        def evac_add_x(c, tt, ts, ps, psb_):
            P.op("dve", lambda e: e.tensor_tensor(out=xT[:, c, ts], in0=ps[:], in1=xT[:, c, ts], op=ALU.add),
                 reads=[psb_, xb[c][tt]], writes=[xb[c][tt]])

        linear(t_out, hT, hb, evac_add_x)

        b2flat = big2[:].rearrange("p a b -> p (a b)")
        mst = b2flat[:, 0:4096].bitcast(F32)
        msq = b2flat[:, 4096:6144]
        memT = b2flat[:, 6144:6144 + KC * MEM].rearrange("p (c m) -> p c m", c=KC)
        memTb = P.buf("memT")
        mstb = P.buf("mst")
        msqb = P.buf("msq")
        mss = P.sb("mss", [128, 4], F32)
        mssb = P.buf("mss")
        allb2 = [b2b[c][tt] for c in range(KC) for tt in range(2)]
        for mc in range(2):
            P.dma("sp", mst[:], mem[mc * 128:(mc + 1) * 128, :], writes=[mstb] + allb2, sem_buf=mstb)
            P.op("act", lambda e: e.activation(out=msq[:], in_=mst[:], func=AF.Square, accum_out=mss[:, 0:1]),
                 reads=[mstb], writes=[msqb, mssb] + allb2)
            P.op("act", lambda e: e.activation(out=mss[:, 1:2], in_=mss[:, 0:1], func=AF.Ln, scale=1.0 / D, bias=S.eps[:, 0:1]),
                 reads=[mssb, S.epsb], writes=[mssb])
            P.op("act", lambda e: e.activation(out=mss[:, 2:3], in_=mss[:, 1:2], func=AF.Exp, scale=-0.5),
                 reads=[mssb], writes=[mssb])
            P.op("dve", lambda e: e.tensor_scalar(out=mst[:], in0=mst[:], scalar1=mss[:, 2:3], scalar2=None, op0=ALU.mult),
                 reads=[mstb, mssb], writes=[mstb])
            for c in range(KC):
                bk = 4 + c % 2
                P.op("pe", lambda e: e.transpose(psb[bk][:, 0:128], mst[:, c * 128:(c + 1) * 128], S.ident[:]),
                     reads=[mstb, S.identb], writes=[psbb[bk]])
                P.op("act", lambda e: e.activation(out=memT[:, c, mc * 128:(mc + 1) * 128], in_=psb[bk][:, 0:128],
                                                   func=AF.Copy, scale=gM[:, c:c + 1]),
                     reads=[psbb[bk], g_b], writes=[memTb] + allb2)
        kT = P.sb("kT", [128, KC, MEM], BF16)
        kTb = P.buf("kT")
        V = P.sb("V", [128, 2, D], BF16)
        Vb = P.buf("V")
        for j in range(4):
            wsl, wb = WS.get(t_kv[j])
            for oc in range(4):
                bk = bank_ctr[0] % 4
                bank_ctr[0] += 1

                def mm(e):
                    ins = None
                    for kc in range(KC):
                        ins = e.matmul(psb[bk][:, 0:MEM], wsl[:, kc, oc * 128:(oc + 1) * 128], memT[:, kc, :],
                                       start=(kc == 0), stop=(kc == KC - 1))
                    return ins
                P.op("pe", mm, reads=[wb, memTb], writes=[psbb[bk]])
                P.op("act", lambda e: e.copy(out=kT[:, j * 4 + oc, :], in_=psb[bk][:, 0:MEM]),
                     reads=[psbb[bk]], writes=[kTb])
        for j in range(4):
            wsl, wb = WS.get(t_kv[4 + j])
            for mc in range(2):
                bk = bank_ctr[0] % 4
                bank_ctr[0] += 1

                def mm(e):
                    ins = None
                    for kc in range(KC):
                        ins = e.matmul(psb[bk][:], memT[:, kc, mc * 128:(mc + 1) * 128], wsl[:, kc, :],
                                       start=(kc == 0), stop=(kc == KC - 1))
                    return ins
                P.op("pe", mm, reads=[wb, memTb], writes=[psbb[bk]])
                P.op("act", lambda e: e.copy(out=V[:, mc, j * 512:(j + 1) * 512], in_=psb[bk][:]),
                     reads=[psbb[bk]], writes=[Vb])
        S.norm_to_hT(gX, g_b)

        def evac_q(c, tt, ts, ps, psb_):
            P.op("act", lambda e: e.copy(out=big2[:, c, ts], in_=ps[:]), reads=[psb_],
                 writes=[b2b[c][tt], mstb, msqb, memTb])
        linear(t_xq, hT, hb, evac_q)
        expP = P.sb("expP", [128, 2, 512], BF16)
        expPb = [P.buf("expP0"), P.buf("expP1")]
        rden = P.sb("rden", [128, 512], F32)
        rdenb = P.buf("rden")
        SCALE = 512.0 ** -0.5
        for hd in range(4):
            for tt in range(2):
                ts = slice(tt * 512, (tt + 1) * 512)
                for mc in range(2):
                    bk = 4 + mc

                    def mm(e):
                        ins = None
                        for dc in range(4):
                            ins = e.matmul(psb[bk][:], kT[:, hd * 4 + dc, mc * 128:(mc + 1) * 128], big2[:, hd * 4 + dc, ts],
                                           start=(dc == 0), stop=(dc == 3))
                        return ins
                    P.op("pe", mm, reads=[kTb] + [b2b[hd * 4 + dc][tt] for dc in range(4)], writes=[psbb[bk]])
                    P.op("act", lambda e: e.activation(out=expP[:, mc, :], in_=psb[bk][:], func=AF.Exp, scale=SCALE),
                         reads=[psbb[bk]], writes=[expPb[mc]])

                def mmd(e):
                    e.matmul(psb[6][:], S.ones[:], expP[:, 0, :], start=True, stop=False)
                    return e.matmul(psb[6][:], S.ones[:], expP[:, 1, :], start=False, stop=True)
                P.op("pe", mmd, reads=[S.onesb] + expPb, writes=[psbb[6]])
                P.op("act", lambda e: e.activation(out=S.tmp[:], in_=psb[6][:], func=AF.Ln), reads=[psbb[6]], writes=[S.tmpb])
                P.op("act", lambda e: e.activation(out=rden[:], in_=S.tmp[:], func=AF.Exp, scale=-1.0),
                     reads=[S.tmpb], writes=[rdenb])
                for dc in range(4):
                    bk = bank_ctr[0] % 4
                    bank_ctr[0] += 1
                    c = hd * 4 + dc

                    def mmo(e):
                        e.matmul(psb[bk][:], V[:, 0, c * 128:(c + 1) * 128], expP[:, 0, :], start=True, stop=False)
                        return e.matmul(psb[bk][:], V[:, 1, c * 128:(c + 1) * 128], expP[:, 1, :], start=False, stop=True)
                    P.op("pe", mmo, reads=[Vb] + expPb, writes=[psbb[bk]])
                    P.op("dve", lambda e: e.tensor_tensor(out=hT[:, c, ts], in0=psb[bk][:], in1=rden[:], op=ALU.mult),
                         reads=[psbb[bk], rdenb], writes=[hb[c][tt]])
        linear(t_xo, hT, hb, evac_add_x)

        S.norm_to_hT(gF, g_b)
        rl = [P.sb(f"rl{i}", [128, 512], F32) for i in range(2)]
        rlb = [P.buf(f"rl{i}") for i in range(2)]
        rl_ctr = [0]

        def evac_h(cc_base):
            def f(c, tt, ts, ps, psb_):
                s = rl_ctr[0] % 2
                rl_ctr[0] += 1
                P.op("act", lambda e: e.activation(out=rl[s][:], in_=ps[:], func=AF.Relu), reads=[psb_], writes=[rlb[s]])
                P.op("dve", lambda e: e.tensor_tensor(out=big2[:, c, ts], in0=rl[s][:], in1=rl[s][:], op=ALU.mult),
                     reads=[rlb[s]], writes=[b2b[c][tt]])
            return f
        for hg in range(4):
            linear(t_ff[hg][0], hT, hb, evac_h(hg))
            linear(t_ff[hg][1], big2, b2b, evac_add_x)

        if not last:
            xT_ov = xT_o.rearrange("(kc p) t -> p kc t", p=128)
            for c4 in range(4):
                P.dma("sp", xT_ov[:, c4 * 4:(c4 + 1) * 4, :], xT[:, c4 * 4:(c4 + 1) * 4, :],
                      reads=[xb[c][tt] for c in range(c4 * 4, c4 * 4 + 4) for tt in range(2)],
                      sem_buf=xb[c4 * 4][0])
            S.norm_to_hT(gN, g_b)
            hT_ov = hT_o.rearrange("(kc p) t -> p kc t", p=128)
            for c4 in range(4):
                P.dma("sp", hT_ov[:, c4 * 4:(c4 + 1) * 4, :], hT[:, c4 * 4:(c4 + 1) * 4, :],
                      reads=[hb[c][tt] for c in range(c4 * 4, c4 * 4 + 4) for tt in range(2)],
                      sem_buf=hb[c4 * 4][0])
        else:
            hflat = hT[:].rearrange("p a b -> p (a b)").bitcast(F32)
            ost = [hflat[:, i * D:(i + 1) * D] for i in range(2)]
            ostb = [P.buf(f"ost{i}") for i in range(2)]
            for ob in ostb:
                for c in range(KC):
                    for tt in range(2):
                        ob.r.extend(hb[c][tt].r)
                        ob.r.append(hb[c][tt].w)
            finv = big2[:].rearrange("p a b -> p (a b)").bitcast(F32)
            finb = P.buf("fin")

            def emit(c, tt, ts):
                P.op("dve", lambda e: e.scalar_tensor_tensor(out=finv[:, c * 512:(c + 1) * 512], in0=xT[:, c, ts],
                                                             scalar=gN[:, c:c + 1], in1=S.rstd[:],
                                                             op0=ALU.mult, op1=ALU.mult),
                     reads=[xb[c][tt], g_b, S.rstdb] + [b2b[cc][t2] for cc in range(KC) for t2 in range(2)],
                     writes=[finb])
                if c == KC - 1:
                    for tb in range(4):
                        blk = tt * 4 + tb
                        s = blk % 2
                        for cc in range(KC):
                            bk = cc % 4
                            P.op("pe", lambda e: e.transpose(psb[bk][:, 0:128],
                                                             finv[:, cc * 512 + tb * 128: cc * 512 + (tb + 1) * 128], S.ident[:]),
                                 reads=[finb, S.identb], writes=[psbb[bk]])
                            if cc % 2:
                                P.op("act", lambda e: e.copy(out=ost[s][:, cc * 128:(cc + 1) * 128], in_=psb[bk][:, 0:128]),
                                     reads=[psbb[bk]], writes=[ostb[s]])
                            else:
                                P.op("dve", lambda e: e.tensor_copy(out=ost[s][:, cc * 128:(cc + 1) * 128], in_=psb[bk][:, 0:128]),
                                     reads=[psbb[bk]], writes=[ostb[s]])
                        P.dma("sp", out_o[blk * 128:(blk + 1) * 128, :], ost[s], reads=[ostb[s]], is_output=True)
            S.norm(gN, g_b, emit)


NFM = 10 * 128
NTM = 1284
C_ID, C_DM, C_QD, C_U, C_NS, C_NC = 0, 128, 256, 768, 896, 1024
C_KDEC, C_G128, C_ONE, C_EPS6, C_EPS5, C_LNS, C_HV = 1152, 1153, 1154, 1155, 1156, 1157, 1158
C_RETG = 1162
C_GDNG = C_RETG + 256
C_CONV = C_GDNG + 128
C_MD32 = C_CONV + 24
C_MC0 = C_MD32 + 128
C_MC1 = C_MC0 + 128
NCST = C_MC1 + 128
NEG = -30000.0


def emit_B(P, io, nt=None):
    nc = P.nc
    hT_ag, wfm_d, wtm_d, cst_d, cos_ag, sin_ag, mT_o = (io[k] for k in ("hT_ag", "wfm", "wtm", "cst", "cos_ag", "sin_ag", "mT_loc"))
    if True:
        A = lambda fn, r=(), w=(): P.op("act", fn, r, w)
        V = lambda fn, r=(), w=(): P.op("dve", fn, r, w)
        G = lambda fn, r=(), w=(): P.op("pool", fn, r, w)
        T = lambda fn, r=(), w=(): P.op("pe", fn, r, w)
        cst = P.sb("cst", [128, NCST], F32)
        cstb = P.buf("cst")
        P.dma("sp", cst[:], cst_d, writes=[cstb])
        ident = cst[:, C_ID:C_ID + 128]
        DMt = cst[:, C_DM:C_DM + 128]
        qdec = cst[:, C_QD:C_QD + 512]
        Utri = cst[:, C_U:C_U + 128]
        NEGs = cst[:, C_NS:C_NS + 128]
        NEGc = cst[:, C_NC:C_NC + 128]
        col = lambda i: cst[:, i:i + 1]
        MD32 = cst[:, C_MD32:C_MD32 + 128]
        MC0 = cst[:, C_MC0:C_MC0 + 128]
        MC1 = cst[:, C_MC1:C_MC1 + 128]
        retg = cst[:, C_RETG:C_RETG + 256]
        gdng = cst[:, C_GDNG:C_GDNG + 128]
        identb = P.sb("identb", [128, 128], BF16)
        identbb = P.buf("identb")
        V(lambda e: e.tensor_copy(out=identb[:], in_=ident), [cstb], [identbb])
        ones_bf = P.sb("ones_bf", [128, 128], BF16)
        ones_f = P.sb("ones_f", [128, 128], F32)
        onesb = P.buf("ones")
        V(lambda e: e.memset(ones_bf[:], 1.0), [], [onesb])
        V(lambda e: e.memset(ones_f[:], 1.0), [], [onesb])
        nea = P.sb("nea", [128, 2], F32)
        neab = P.buf("nea")
        A(lambda e: e.activation(out=nea[:], in_=cst[:, C_HV:C_HV + 2], func=AF.Exp), [cstb], [neab])
        V(lambda e: e.tensor_scalar(out=nea[:], in0=nea[:], scalar1=-1.0, scalar2=None, op0=ALU.mult), [neab], [neab])
        wfm = P.sb("wfm", [128, KC, NFM], BF16)
        wtm = P.sb("wtm", [128, KC, NTM], BF16)
        wfmb = P.buf("wfm")
        wtmb = P.buf("wtm")
        wfm_v = wfm_d.rearrange("(kc p) n -> p kc n", p=128)
        wtm_v = wtm_d.rearrange("(kc p) n -> p kc n", p=128)
        for a, b in ((0, 512), (512, 1024), (1024, NFM)):
            P.dma("pool", wfm[:, :, a:b], wfm_v[:, :, a:b], writes=[wfmb])
        for a, b in ((0, 512), (512, 1024), (1024, NTM)):
            P.dma("pool", wtm[:, :, a:b], wtm_v[:, :, a:b], writes=[wtmb])
        hsl = [P.sb(f"hsl{i}", [128, KC, 512], BF16) for i in range(2)]
        hslb = [P.buf(f"hsl{i}") for i in range(2)]
        cs_sl = [P.sb(f"cs{i}", [128, 2, 512], F32) for i in range(2)]
        cs_b = [P.buf(f"cs{i}") for i in range(2)]
        hT_v = hT_ag.rearrange("(r kc p) t -> r p kc t", p=128, kc=KC)
        cos_v = cos_ag.rearrange("(r p) t -> r p t", p=128)
        sin_v = sin_ag.rearrange("(r p) t -> r p t", p=128)
        bank, bankb = P.banks, P.bankb
        psFM, psFMb = bank[0:2], bankb[0:2]
        psTM, psTMb = bank[0:2], bankb[0:2]
        psN, psNb = bank[2], bankb[2]
        psRo, psRs = bank[2][:, 0:256], bank[2][:, 256:512]
        psRob, psRsb = bankb[2], bankb[2]
        pools = {0: [3, 4], 1: [5, 6], 2: [7]}
        sm_ctr = {0: 0, 1: 0, 2: 0}
        sm_last = [0]

        def sm(pool):
            i = pools[pool][sm_ctr[pool] % len(pools[pool])]
            sm_ctr[pool] += 1
            sm_last[0] = i
            return bank[i][:, 0:128], bankb[i]

        def bfv(ap):
            return ap.bitcast(BF16)[:, 0:128]

        t1 = P.sb("t1", [128, 512], F32)
        t2 = P.sb("t2", [128, 512], F32)
        t1b, t2b = P.buf("t1"), P.buf("t2")
        qr = [P.sb(f"qr{i}", [128, 512], BF16) for i in range(1)]
        qd = [P.sb(f"qd{i}", [128, 512], BF16) for i in range(1)]
        kr = [P.sb(f"kr{i}", [128, 512], BF16) for i in range(1)]
        qrb = [P.buf(f"qr{i}") for i in range(1)]
        qdb = [P.buf(f"qd{i}") for i in range(1)]
        krb = [P.buf(f"kr{i}") for i in range(1)]
        stage = [P.sb(f"stage{j}", [128, 515], F32) for j in range(6)]
        stageb = [P.buf(f"stage{j}") for j in range(6)]
        acc = [P.sb(f"acc{i}", [128, 512], F32) for i in range(2)]
        accb = [P.buf(f"acc{i}") for i in range(2)]
        sl = [P.sb(f"sl{i}", [128, 512], F32) for i in range(2)]
        slb = [P.buf(f"sl{i}") for i in range(2)]
        sqv = P.sb("sqv", [128, 512], BF16)
        sqvb = P.buf("sqv")
        lnn = P.sb("lnn", [128, 512], F32)
        lnnb = P.buf("lnn")
        rn = P.sb("rn", [128, 512], F32)
        rnb = P.buf("rn")
        gqkv = [[P.sb(f"gqkv{s}_{j}", [128, 512], BF16) for j in range(6)] for s in range(1)]
        gqkvb = [[P.buf(f"gqkv{s}_{j}") for j in range(6)] for s in range(1)]
        sg = P.sb("sg", [128, 512], F32)
        smg = P.sb("smg", [128, 512], F32)
        Gt = P.sb("Gt", [128, 512], F32)
        sgb, smgb, Gtb = P.buf("sg"), P.buf("smg"), P.buf("Gt")
        Vr = P.sb("Vr", [128, 256], BF16)
        Vrb = P.buf("Vr")
        sc = P.sb("sc", [128, 32], F32)
        scb = P.buf("sc")
        KD = P.sb("KD", [128, 128], BF16)
        KDb = P.buf("KD")
        Pt = P.sb("Pt", [128, 128], BF16)
        Ptb = P.buf("Pt")
        St = P.sb("St", [128, 256], F32)
        Stf = P.sb("Stf", [128, 256], BF16)
        Stb, Stfb = P.buf("St"), P.buf("Stf")
        bst = P.sb("bst", [128, 8], F32)
        bstb = P.buf("bst")
        yr = P.sb("yr", [128, 256], F32)
        yrb = P.buf("yr")
        mA = P.sb("mA", [128, 256], F32)
        mAb = P.buf("mA")
        mB = P.sb("mB", [128, 256], F32)
        mBb = P.buf("mB")
        mbf = P.sb("mbf", [128, 256], BF16)
        mbfb = P.buf("mbf")
        mTs = [P.sb(f"mTs{i}", [128, 2, 512], BF16) for i in range(2)]
        mTsb = [P.buf(f"mTs{i}") for i in range(2)]
        def per_head(name, shape, dtp):
            return [P.sb(f"{name}{h}", shape, dtp) for h in range(2)], [P.buf(f"{name}{h}") for h in range(2)]
        Gbc, Gbcb = per_head("Gbc", [128, 128], F32)
        LBc, LBcb = per_head("LBc", [128, 128], F32)
        hs, hsb = per_head("hs", [128, 16], F32)
        E1, E1b = per_head("E1", [128, 128], F32)
        E3, E3b = per_head("E3", [128, 128], F32)
        ER, ERb = per_head("ER", [128, 128], F32)
        qg, qgb = per_head("qg", [128, 128], BF16)
        Nm = [[P.sb(f"Nm{h}_{i}", [128, 128], F32) for i in range(2)] for h in range(2)]
        Nmb = [[P.buf(f"Nm{h}_{i}") for i in range(2)] for h in range(2)]
        Mm = [[P.sb(f"Mm{h}_{i}", [128, 128], F32) for i in range(2)] for h in range(2)]
        Mmb = [[P.buf(f"Mm{h}_{i}") for i in range(2)] for h in range(2)]
        Xm = [[P.sb(f"Xm{h}_{i}", [128, 128], F32) for i in range(2)] for h in range(2)]
        Xmb = [[P.buf(f"Xm{h}_{i}") for i in range(2)] for h in range(2)]
        Nf, Nfb = per_head("Nf", [128, 128], F32)
        Mf, Mfb = per_head("Mf", [128, 128], F32)
        Cm, Cmb = per_head("Cm", [128, 2, 128], F32)
        Tn, Tnb = per_head("Tn", [128, 128], F32)
        Pp, Ppb = per_head("Pp", [128, 128], F32)
        Xb, Xbb = per_head("Xb", [128, 128], BF16)
        QKt, QKtb = per_head("QKt", [128, 128], BF16)
        KDg, KDgb = per_head("KDg", [128, 128], BF16)
        VB, VBb = per_head("VB", [128, 128], F32)
        Zt, Ztb = per_head("Zt", [128, 128], BF16)
        Vn, Vnb = per_head("Vn", [128, 128], BF16)
        Sg, Sgb = per_head("Sg", [128, 128], F32)
        Sgf, Sgfb = per_head("Sgf", [128, 128], BF16)
        y1, y1b = per_head("y1", [128, 128], F32)
        junk = P.sb("junk", [128, 256], F32)
        junkb = P.buf("junk")

        mT_ov = mT_o.rearrange("(cc p) t -> p cc t", p=128)
        NT = nt or (NTOK // 512)

        def load_tile(ti):
            s = ti % 2
            t0 = ti * 512
            rk, to = ti // 2, (ti % 2) * 512
            P.dma("sp", hsl[s][:], hT_v[rk, :, :, to:to + 512], writes=[hslb[s]])
            P.dma("sp", cs_sl[s][:, 0, :], cos_v[rk, :, to:to + 512], writes=[cs_b[s]])
            P.dma("sp", cs_sl[s][:, 1, :], sin_v[rk, :, to:to + 512], writes=[cs_b[s]])

        def fm_stage(ti):
            s = ti % 2
            first = (ti % (SEQ // 512) == 0)
            for j in range(10):
                bk = j % 2

                def mm(e):
                    ins = None
                    for kc in range(KC):
                        ins = e.matmul(psFM[bk][:], wfm[:, kc, j * 128:(j + 1) * 128], hsl[s][:, kc, :],
                                       start=(kc == 0), stop=(kc == KC - 1))
                    return ins
                T(mm, [wfmb, hslb[s]], [psFMb[bk]])
                ps, psb_ = psFM[bk], psFMb[bk]
                if j in (0, 2):
                    V(lambda e: e.tensor_tensor(out=t1[:], in0=ps[:], in1=cs_sl[s][:, 0, :], op=ALU.mult),
                      [psb_, cs_b[s]], [t1b])
                elif j in (1, 3):
                    V(lambda e: e.tensor_tensor(out=t2[:], in0=ps[:], in1=cs_sl[s][:, 1, :], op=ALU.mult),
                      [psb_, cs_b[s]], [t2b])
                    G(lambda e: e.tensor_tensor(out=t1[:], in0=t1[:], in1=t2[:], op=ALU.add), [t1b, t2b], [t1b])
                    if j == 1:
                        A(lambda e: e.copy(out=qr[0][:], in_=t1[:]), [t1b], [qrb[0]])
                        G(lambda e: e.tensor_tensor(out=qd[0][:], in0=t1[:], in1=qdec, op=ALU.mult), [t1b, cstb], [qdb[0]])
                    else:
                        A(lambda e: e.copy(out=kr[0][:], in_=t1[:]), [t1b], [krb[0]])
                else:
                    jj = j - 4
                    stg, stgb = stage[jj], stageb[jj]
                    if first:
                        V(lambda e: e.memset(stg[:, 0:3], 0.0), [], [stgb])
                    A(lambda e: e.copy(out=stg[:, 3:515], in_=ps[:]), [psb_], [stgb])
                    a = jj % 2
                    cw = lambda i: cst[:, C_CONV + jj * 4 + i:C_CONV + jj * 4 + i + 1]
                    V(lambda e: e.tensor_scalar(out=acc[a][:], in0=stg[:, 0:512], scalar1=cw(0), scalar2=None, op0=ALU.mult),
                      [stgb, cstb], [accb[a]])
                    for i in range(1, 4):
                        V(lambda e: e.scalar_tensor_tensor(out=acc[a][:], in0=stg[:, i:i + 512], scalar=cw(i), in1=acc[a][:],
                                                           op0=ALU.mult, op1=ALU.add),
                          [stgb, cstb, accb[a]], [accb[a]])
                    A(lambda e: e.copy(out=stg[:, 0:3], in_=stg[:, 512:515]), [stgb], [stgb])
                    if jj >= 4:
                        A(lambda e: e.activation(out=gqkv[0][jj][:], in_=acc[a][:], func=AF.Silu), [accb[a]], [gqkvb[0][jj]])
                    else:
                        A(lambda e: e.activation(out=sl[a][:], in_=acc[a][:], func=AF.Silu), [accb[a]], [slb[a]])
                        A(lambda e: e.activation(out=sqv[:], in_=sl[a][:], func=AF.Square), [slb[a]], [sqvb])
                        T(lambda e: e.matmul(psN[:], ones_bf[:], sqv[:], start=True, stop=True), [onesb, sqvb], [psNb])
                        A(lambda e: e.activation(out=lnn[:], in_=psN[:], func=AF.Ln, bias=col(C_EPS6)), [psNb, cstb], [lnnb])
                        if jj < 2:
                            A(lambda e: e.activation(out=rn[:], in_=lnn[:], func=AF.Exp, scale=-0.5, bias=col(C_LNS)),
                              [lnnb, cstb], [rnb])
                        else:
                            A(lambda e: e.activation(out=rn[:], in_=lnn[:], func=AF.Exp, scale=-0.5), [lnnb], [rnb])
                        V(lambda e: e.tensor_tensor(out=gqkv[0][jj][:], in0=sl[a][:], in1=rn[:], op=ALU.mult),
                          [slb[a], rnb], [gqkvb[0][jj]])

        def block(ti, bi):
            s = ti % 2
            first = (ti % (SEQ // 512) == 0) and bi == 0
            bs = slice(bi * 128, (bi + 1) * 128)
            def tm(bk, c0, c1):
                def mm(e):
                    ins = None
                    for kc in range(KC):
                        ins = e.matmul(psTM[bk][:, 0:c1 - c0], hsl[s][:, kc, bs], wtm[:, kc, c0:c1],
                                       start=(kc == 0), stop=(kc == KC - 1))
                    return ins
                T(mm, [wtmb, hslb[s]], [psTMb[bk]])
            tm(0, 0, 512)
            A(lambda e: e.activation(out=sg[:], in_=psTM[0][:], func=AF.Silu), [psTMb[0]], [sgb])
            tm(1, 512, 1024)
            A(lambda e: e.activation(out=smg[:], in_=psTM[1][:], func=AF.Sigmoid), [psTMb[1]], [smgb])
            G(lambda e: e.tensor_tensor(out=Gt[:], in0=sg[:], in1=smg[:], op=ALU.mult), [sgb, smgb], [Gtb])
            tm(0, 1024, NTM)
            V(lambda e: e.tensor_copy(out=Vr[:], in_=psTM[0][:, 0:256]), [psTMb[0]], [Vrb])
            V(lambda e: e.tensor_tensor(out=sc[:, 0:2], in0=psTM[0][:, 256:258], in1=cst[:, C_HV + 2:C_HV + 4], op=ALU.add),
              [psTMb[0], cstb], [scb])
            A(lambda e: e.activation(out=sc[:, 2:4], in_=sc[:, 0:2], func=AF.Exp), [scb], [scb])
            A(lambda e: e.activation(out=sc[:, 4:6], in_=sc[:, 2:4], func=AF.Ln, bias=col(C_ONE)), [scb, cstb], [scb])
            V(lambda e: e.tensor_tensor(out=sc[:, 6:8], in0=sc[:, 4:6], in1=nea[:], op=ALU.mult), [scb, neab], [scb])
            A(lambda e: e.activation(out=sc[:, 8:10], in_=psTM[0][:, 258:260], func=AF.Exp, scale=-1.0), [psTMb[0], scb], [scb])
            A(lambda e: e.activation(out=sc[:, 10:12], in_=sc[:, 8:10], func=AF.Ln, bias=col(C_ONE)), [scb, cstb], [scb])
            A(lambda e: e.activation(out=sc[:, 12:14], in_=sc[:, 10:12], func=AF.Exp, scale=-1.0), [scb], [scb])

            if first:
                V(lambda e: e.memset(St[:], 0.0), [], [Stb])
                V(lambda e: e.memset(Stf[:], 0.0), [], [Stfb])
            r1, r1b = sm(2)
            T(lambda e: e.transpose(bfv(r1), kr[0][:, bs], identb[:]), [krb[0], identbb], [r1b])
            A(lambda e: e.activation(out=KD[:], in_=bfv(r1), func=AF.Copy, scale=col(C_KDEC)), [r1b, cstb], [KDb])
            r2, r2b = sm(2)
            T(lambda e: e.matmul(r2, kr[0][:, bs], qr[0][:, bs], start=True, stop=True), [krb[0], qrb[0]], [r2b])
            V(lambda e: e.tensor_tensor(out=Pt[:], in0=r2, in1=DMt, op=ALU.mult), [r2b, cstb], [Ptb])

            def mmo(e):
                e.matmul(psRo, Pt[:], Vr[:], start=True, stop=False)
                return e.matmul(psRo, qd[0][:, bs], Stf[:], start=False, stop=True)
            T(mmo, [Ptb, Vrb, qdb[0], Stfb], [psRob])
            T(lambda e: e.matmul(psRs, KD[:], Vr[:], start=True, stop=True), [KDb, Vrb], [psRsb])
            V(lambda e: e.scalar_tensor_tensor(out=St[:], in0=St[:], scalar=col(C_G128), in1=psRs, op0=ALU.mult, op1=ALU.add),
              [Stb, cstb, psRsb], [Stb])
            A(lambda e: e.copy(out=Stf[:], in_=St[:]), [Stb], [Stfb])
            V(lambda e: e.bn_stats(out=bst[:, 0:6], in_=psRo), [psRob], [bstb])
            V(lambda e: e.bn_aggr(out=bst[:, 6:8], in_=bst[:, 0:6]), [bstb], [bstb])
            A(lambda e: e.activation(out=bst[:, 0:1], in_=bst[:, 7:8], func=AF.Ln, bias=col(C_EPS5)), [bstb, cstb], [bstb])
            A(lambda e: e.activation(out=bst[:, 1:2], in_=bst[:, 0:1], func=AF.Exp, scale=-0.5), [bstb], [bstb])
            V(lambda e: e.tensor_scalar(out=yr[:], in0=psRo, scalar1=bst[:, 6:7], scalar2=bst[:, 1:2],
                                        op0=ALU.subtract, op1=ALU.mult), [psRob, bstb], [yrb])
            G(lambda e: e.tensor_tensor(out=yr[:], in0=yr[:], in1=retg, op=ALU.mult), [yrb, cstb], [yrb])
            G(lambda e: e.tensor_tensor(out=mA[:], in0=yr[:], in1=Gt[:, 0:256], op=ALU.mult), [yrb, Gtb], [mAb])

            def head_gen(h):
                    gq, gk, gv = gqkv[0][h], gqkv[0][2 + h], gqkv[0][4 + h]
                    gqb_, gkb_, gvb_ = gqkvb[0][h], gqkvb[0][2 + h], gqkvb[0][4 + h]
                    if first:
                        V(lambda e: e.memset(Sg[h][:], 0.0), [], [Sgb[h]])
                        V(lambda e: e.memset(Sgf[h][:], 0.0), [], [Sgfb[h]])
                    H, Hb = hs[h], hsb[h]
                    V(lambda e: e.tensor_scalar(out=Gbc[h][:], in0=ones_f[:], scalar1=sc[:, 6 + h:7 + h], scalar2=None, op0=ALU.mult),
                      [onesb, scb], [Gbcb[h]])
                    V(lambda e: e.tensor_scalar(out=LBc[h][:], in0=ones_f[:], scalar1=sc[:, 10 + h:11 + h], scalar2=-1.0,
                                                op0=ALU.mult, op1=ALU.mult), [onesb, scb], [LBcb[h]])
                    yield
                    pR, pRb = sm(h)
                    T(lambda e: e.matmul(pR, Gbc[h][:], Utri, start=True, stop=True), [Gbcb[h], cstb], [pRb])
                    pR2, pR2b = sm(h)

                    def mm2(e):
                        e.matmul(pR2, Gbc[h][:], Utri, start=True, stop=False)
                        return e.matmul(pR2, LBc[h][:], ident, start=False, stop=True)
                    T(mm2, [Gbcb[h], LBcb[h], cstb], [pR2b])
                    pc, pcb = bank[sm_last[0]][:, 128:256], bankb[sm_last[0]]
                    T(lambda e: e.matmul(pc[:, 0:1], Utri, sc[:, 6 + h:7 + h], start=True, stop=True), [cstb, scb], [pcb])
                    V(lambda e: e.tensor_scalar(out=H[:, 0:1], in0=pc[:, 0:1], scalar1=-1.0, scalar2=None, op0=ALU.mult), [pcb], [Hb])
                    V(lambda e: e.tensor_copy(out=H[:, 1:2], in_=pR[:, 127:128]), [pRb], [Hb])
                    A(lambda e: e.activation(out=H[:, 2:3], in_=H[:, 1:2], func=AF.Exp), [Hb], [Hb])
                    A(lambda e: e.activation(out=H[:, 3:4], in_=pc[:, 0:1], func=AF.Exp, scale=-1.0, bias=H[:, 1:2]), [pcb, Hb], [Hb])
                    V(lambda e: e.tensor_scalar(out=H[:, 5:6], in0=sc[:, 10 + h:11 + h], scalar1=-1.0, scalar2=None, op0=ALU.mult),
                      [scb], [Hb])
                    A(lambda e: e.activation(out=H[:, 4:5], in_=pc[:, 0:1], func=AF.Exp, bias=H[:, 5:6]), [pcb, Hb], [Hb])
                    V(lambda e: e.tensor_scalar(out=H[:, 4:5], in0=H[:, 4:5], scalar1=-1.0, scalar2=None, op0=ALU.mult), [Hb], [Hb])
                    V(lambda e: e.scalar_tensor_tensor(out=E1[h][:], in0=pR2, scalar=H[:, 0:1], in1=NEGs, op0=ALU.add, op1=ALU.add),
                      [pR2b, Hb, cstb], [E1b[h]])
                    A(lambda e: e.activation(out=E1[h][:], in_=E1[h][:], func=AF.Exp), [E1b[h]], [E1b[h]])
                    V(lambda e: e.scalar_tensor_tensor(out=E3[h][:], in0=pR, scalar=H[:, 0:1], in1=NEGc, op0=ALU.add, op1=ALU.add),
                      [pRb, Hb, cstb], [E3b[h]])
                    A(lambda e: e.activation(out=E3[h][:], in_=E3[h][:], func=AF.Exp), [E3b[h]], [E3b[h]])
                    A(lambda e: e.activation(out=ER[h][:], in_=pR, func=AF.Exp), [pRb], [ERb[h]])
                    G(lambda e: e.tensor_tensor(out=qg[h][:], in0=gq[:, bs], in1=ER[h][:], op=ALU.mult), [gqb_, ERb[h]], [qgb[h]])
                    yield
                    pKK, pKKb = sm(h)
                    T(lambda e: e.matmul(pKK, gk[:, bs], gk[:, bs], start=True, stop=True), [gkb_], [pKKb])
                    V(lambda e: e.tensor_tensor(out=Nf[h][:], in0=pKK, in1=E1[h][:], op=ALU.mult), [pKKb, E1b[h]], [Nfb[h]])
                    yield
                    pM, pMb = sm(h)
                    T(lambda e: e.transpose(pM, Nf[h][:], ident), [Nfb[h], cstb], [pMb])
                    A(lambda e: e.copy(out=Mf[h][:], in_=pM), [pMb], [Mfb[h]])
                    yield
                    pQK, pQKb = sm(h)
                    T(lambda e: e.matmul(pQK, gk[:, bs], gq[:, bs], start=True, stop=True), [gkb_, gqb_], [pQKb])
                    V(lambda e: e.tensor_tensor(out=QKt[h][:], in0=pQK, in1=E3[h][:], op=ALU.mult), [pQKb, E3b[h]], [QKtb[h]])
                    yield
                    G(lambda e: e.tensor_tensor(out=Nm[h][0][:], in0=Nf[h][:], in1=MD32, op=ALU.mult), [Nfb[h], cstb], [Nmb[h][0]])
                    G(lambda e: e.tensor_tensor(out=Mm[h][0][:], in0=Mf[h][:], in1=MD32, op=ALU.mult), [Mfb[h], cstb], [Mmb[h][0]])
                    G(lambda e: e.tensor_tensor(out=Cm[h][:, 0, :], in0=Mf[h][:], in1=MC0, op=ALU.mult), [Mfb[h], cstb], [Cmb[h]])
                    G(lambda e: e.tensor_tensor(out=Cm[h][:, 1, :], in0=Mf[h][:], in1=MC1, op=ALU.mult), [Mfb[h], cstb], [Cmb[h]])
                    V(lambda e: e.tensor_tensor(out=Xm[h][0][:], in0=ident, in1=Nm[h][0][:], op=ALU.subtract), [cstb, Nmb[h][0]], [Xmb[h][0]])
                    for k in range(1, 5):
                        a, b = (k - 1) % 2, k % 2
                        yield
                        pm, pmb = sm(h)
                        T(lambda e: e.matmul(pm, Nm[h][a][:], Mm[h][a][:], start=True, stop=True), [Nmb[h][a], Mmb[h][a]], [pmb])
                        if k < 4:
                            pn, pnb = sm(h)
                            T(lambda e: e.matmul(pn, Mm[h][a][:], Nm[h][a][:], start=True, stop=True), [Nmb[h][a], Mmb[h][a]], [pnb])
                        A(lambda e: e.copy(out=Mm[h][b][:], in_=pm), [pmb], [Mmb[h][b]])
                        if k < 4:
                            V(lambda e: e.tensor_copy(out=Nm[h][b][:], in_=pn), [pnb], [Nmb[h][b]])
                        yield
                        px, pxb = sm(h)
                        T(lambda e: e.matmul(px, Mm[h][b][:], Xm[h][a][:], start=True, stop=True), [Mmb[h][b], Xmb[h][a]], [pxb])
                        V(lambda e: e.tensor_tensor(out=Xm[h][b][:], in0=px, in1=Xm[h][a][:], op=ALU.add), [pxb, Xmb[h][a]], [Xmb[h][b]])
                    xc = 0
                    for lv in range(2):
                        xn = 1 - xc
                        yield
                        ptp, ptpb = sm(h)
                        T(lambda e: e.transpose(ptp, Xm[h][xc][:], ident), [Xmb[h][xc], cstb], [ptpb])
                        A(lambda e: e.copy(out=Tn[h][:], in_=ptp), [ptpb], [Tnb[h]])
                        yield
                        pp1, pp1b = sm(h)
                        T(lambda e: e.matmul(pp1, Cm[h][:, lv, :], Xm[h][xc][:], start=True, stop=True), [Cmb[h], Xmb[h][xc]], [pp1b])
                        V(lambda e: e.tensor_copy(out=Pp[h][:], in_=pp1), [pp1b], [Ppb[h]])
                        yield
                        pq, pqb = sm(h)
                        T(lambda e: e.matmul(pq, Tn[h][:], Pp[h][:], start=True, stop=True), [Tnb[h], Ppb[h]], [pqb])
                        V(lambda e: e.tensor_tensor(out=Xm[h][xn][:], in0=Xm[h][xc][:], in1=pq, op=ALU.subtract), [Xmb[h][xc], pqb], [Xmb[h][xn]])
                        xc = xn
                    yield
                    A(lambda e: e.copy(out=Xb[h][:], in_=Xm[h][xc][:]), [Xmb[h][xc]], [Xbb[h]])
                    X6, X6b = Xb[h], Xbb[h]
                    yield
                    pk, pkb = sm(h)
                    T(lambda e: e.transpose(bfv(pk), gk[:, bs], identb[:]), [gkb_, identbb], [pkb])
                    A(lambda e: e.activation(out=KDg[h][:], in_=bfv(pk), func=AF.Copy, scale=H[:, 3:4]), [pkb, Hb], [KDgb[h]])
                    yield
                    pv, pvb = sm(h)
                    T(lambda e: e.transpose(bfv(pv), gv[:, bs], identb[:]), [gvb_, identbb], [pvb])
                    A(lambda e: e.activation(out=VB[h][:], in_=bfv(pv), func=AF.Copy, scale=sc[:, 12 + h:13 + h]), [pvb, scb], [VBb[h]])
                    yield
                    pz, pzb = sm(h)
                    T(lambda e: e.matmul(pz, gk[:, bs], Sgf[h][:], start=True, stop=True), [gkb_, Sgfb[h]], [pzb])
                    V(lambda e: e.scalar_tensor_tensor(out=Zt[h][:], in0=pz, scalar=H[:, 4:5], in1=VB[h][:], op0=ALU.mult, op1=ALU.add),
                      [pzb, Hb, VBb[h]], [Ztb[h]])
                    yield
                    pvn, pvnb = sm(h)
                    T(lambda e: e.matmul(pvn, X6[:], Zt[h][:], start=True, stop=True), [X6b, Ztb[h]], [pvnb])
                    A(lambda e: e.copy(out=Vn[h][:], in_=pvn), [pvnb], [Vnb[h]])
                    yield
                    po, pob = sm(h)

                    def mmg(e):
                        e.matmul(po, qg[h][:], Sgf[h][:], start=True, stop=False)
                        return e.matmul(po, QKt[h][:], Vn[h][:], start=False, stop=True)
                    T(mmg, [qgb[h], Sgfb[h], QKtb[h], Vnb[h]], [pob])
                    yield
                    pss, pssb = sm(h)
                    T(lambda e: e.matmul(pss, KDg[h][:], Vn[h][:], start=True, stop=True), [KDgb[h], Vnb[h]], [pssb])
                    V(lambda e: e.scalar_tensor_tensor(out=Sg[h][:], in0=Sg[h][:], scalar=H[:, 2:3], in1=pss, op0=ALU.mult, op1=ALU.add),
                      [Sgb[h], Hb, pssb], [Sgb[h]])
                    A(lambda e: e.copy(out=Sgf[h][:], in_=Sg[h][:]), [Sgb[h]], [Sgfb[h]])
                    yield
                    A(lambda e: e.activation(out=junk[:, 0:128], in_=po, func=AF.Square, accum_out=H[:, 6:7]), [pob, Hb], [junkb, Hb])
                    A(lambda e: e.activation(out=H[:, 7:8], in_=H[:, 6:7], func=AF.Ln, scale=1.0 / 128, bias=col(C_EPS6)), [Hb, cstb], [Hb])
                    A(lambda e: e.activation(out=H[:, 8:9], in_=H[:, 7:8], func=AF.Exp, scale=-0.5), [Hb], [Hb])
                    V(lambda e: e.scalar_tensor_tensor(out=y1[h][:], in0=po, scalar=H[:, 8:9], in1=gdng, op0=ALU.mult, op1=ALU.mult),
                      [pob, Hb, cstb], [y1b[h]])
                    G(lambda e: e.tensor_tensor(out=mB[:, h * 128:(h + 1) * 128], in0=y1[h][:], in1=Gt[:, 256 + h * 128:256 + (h + 1) * 128],
                                                op=ALU.mult), [y1b[h], Gtb], [mBb])
            gens = [head_gen(0), head_gen(1)]
            while gens:
                for g_ in list(gens):
                    try:
                        next(g_)
                    except StopIteration:
                        gens.remove(g_)
            G(lambda e: e.tensor_tensor(out=mbf[:], in0=mA[:], in1=mB[:], op=ALU.add), [mAb, mBb], [mbfb])
            for cc in range(2):
                pt, ptb = sm(2)
                T(lambda e: e.transpose(bfv(pt), mbf[:, cc * 128:(cc + 1) * 128], identb[:]), [mbfb, identbb], [ptb])
                A(lambda e: e.copy(out=mTs[s][:, cc, bs], in_=bfv(pt)), [ptb], [mTsb[s]])

        load_tile(0)
        for ti in range(NT):
            if ti + 1 < NT:
                load_tile(ti + 1)
            fm_stage(ti)
            for bi in range(4):
                block(ti, bi)
            s = ti % 2
            P.dma("sp", mT_ov[:, :, ti * 512:(ti + 1) * 512], mTs[s][:], reads=[mTsb[s]])


def _ret_consts(c):
    lg = np.log1p(-np.exp2(-5.0 - np.float64(c)))
    idx = np.arange(128)
    jj, ii = idx[:, None], idx[None, :]
    same = (jj // 64) == (ii // 64)
    later = (jj < 64) & (ii >= 64)
    DMt = np.where(same, np.exp(lg * np.abs(ii - jj)), np.where(later, np.exp(lg * (ii - jj)), 0.0))
    DMt = DMt * (128.0 ** -0.5)
    qdec = np.exp(lg * (np.arange(512) % 128 + 1.0))
    kdec = np.exp(lg * (127.0 - idx)) * (128.0 ** -0.5)
    g128 = np.exp(lg * 128.0)
    return DMt, qdec, kdec, g128


def pack_cst(c, conv_w_l, a_log_l, dt_bias_l, ret_gn_g_l, gdn_norm_g_l):
    cst = np.zeros((128, NCST), np.float32)
    idx = np.arange(128)
    cst[:, C_ID:C_ID + 128] = np.eye(128)
    DMt, qdec, kdec, g128 = _ret_consts(c)
    cst[:, C_DM:C_DM + 128] = DMt
    cst[:, C_QD:C_QD + 512] = qdec[None, :]
    cst[:, C_U:C_U + 128] = (idx[:, None] <= idx[None, :])
    cst[:, C_NS:C_NS + 128] = np.where(idx[None, :] > idx[:, None], 0.0, NEG)
    cst[:, C_NC:C_NC + 128] = np.where(idx[None, :] >= idx[:, None], 0.0, NEG)
    cst[:, C_KDEC] = kdec
    cst[:, C_G128] = g128
    cst[:, C_ONE] = 1.0
    cst[:, C_EPS6] = 1e-6
    cst[:, C_EPS5] = 1e-5
    cst[:, C_LNS] = math.log(128.0 ** -0.5)
    cst[:, C_HV:C_HV + 2] = a_log_l[None, 2 * c:2 * c + 2]
    cst[:, C_HV + 2:C_HV + 4] = dt_bias_l[None, 2 * c:2 * c + 2]
    cst[:, C_RETG:C_RETG + 256] = ret_gn_g_l[None, c * 256:(c + 1) * 256]
    cst[:, C_GDNG:C_GDNG + 128] = gdn_norm_g_l[None, :]
    b32, b64 = idx // 32, idx // 64
    cst[:, C_MD32:C_MD32 + 128] = (b32[:, None] == b32[None, :])
    cst[:, C_MC0:C_MC0 + 128] = (b64[:, None] == b64[None, :]) & (b32[:, None] != b32[None, :])
    cst[:, C_MC1:C_MC1 + 128] = (b64[:, None] != b64[None, :])
    for jj in range(6):
        grp, h = jj // 2, jj % 2
        ch0 = grp * 2048 + (2 * c + h) * 128
        cst[:, C_CONV + jj * 4:C_CONV + jj * 4 + 4] = conv_w_l[:, ch0:ch0 + 128].T
    return cst


def pack_w_in(c, w_in_l):
    sw = (np.arange(128) + 64) % 128
    rq = w_in_l[:, O_RQ + c * 128:O_RQ + (c + 1) * 128]
    rk = w_in_l[:, O_RK + c * 128:O_RK + (c + 1) * 128]
    cols = [rq, rq[:, sw], rk, rk[:, sw]]
    for grp in range(3):
        for h in range(2):
            o = O_GQKV + grp * 2048 + (2 * c + h) * 128
            cols.append(w_in_l[:, o:o + 128])
    wfm = np.ascontiguousarray(np.concatenate(cols, axis=1))
    wtm = np.ascontiguousarray(np.concatenate([
        w_in_l[:, O_RG + c * 256:O_RG + (c + 1) * 256],
        w_in_l[:, O_GZ + c * 256:O_GZ + (c + 1) * 256],
        w_in_l[:, O_MA + c * 256:O_MA + (c + 1) * 256],
        w_in_l[:, O_MB + c * 256:O_MB + (c + 1) * 256],
        w_in_l[:, O_RV + c * 256:O_RV + (c + 1) * 256],
        w_in_l[:, O_GA + 2 * c:O_GA + 2 * c + 2],
        w_in_l[:, O_GB + 2 * c:O_GB + 2 * c + 2]], axis=1))
    return wfm, wtm


def build_fused(depth=DEPTH, nt=None, wdepth=DEPTH):
    nc = bass.Bass("TRN2", target_bir_lowering=False)
    din = lambda n, sh, d=F32: nc.dram_tensor(n, list(sh), d, kind="ExternalInput").ap()
    dint = lambda n, sh, d=F32: nc.dram_tensor(n, list(sh), d).ap()
    io = dict(
        x=din("x", [TPC, D]), pos=din("pos", [1, TPC], I32), g0=din("g0", [128, KC]), ident=din("ident", [128, 128]),
        ropec=din("ropec", [128, 2]), selm=din("selm", [128, NCORES]), mem=din("mem", [MEM, D]),
    )
    gains_all = din("gains_all", [wdepth, 128, 4 * KC])
    wfm_all = din("wfm_all", [wdepth, D, NFM])
    wtm_all = din("wtm_all", [wdepth, D, NTM])
    cst_all = din("cst_all", [wdepth, 128, NCST])
    W = {k: din(k, [wdepth] + sh) for k, sh in (("w_out", [D, D]), ("w_xq", [D, D]), ("w_xkv", [D, 2 * D]), ("w_xo", [D, D]),
                                                ("w_ff1", [D, DFF]), ("w_ff2", [DFF, D]))}
    out_o = nc.dram_tensor("out_o", [TPC, D], F32, kind="ExternalOutput").ap()
    xT_d = dint("xT_d", [D, TPC])
    hT_loc = dint("hT_loc", [D, TPC], BF16)
    hT_ag = dint("hT_ag", [NCORES * D, TPC], BF16)
    cos_loc, sin_loc = dint("cos_loc", [128, TPC]), dint("sin_loc", [128, TPC])
    cos_ag, sin_ag = dint("cos_ag", [NCORES * 128, TPC]), dint("sin_ag", [NCORES * 128, TPC])
    mT_loc = dint("mT_loc", [256, NTOK], BF16)
    mT_ag = dint("mT_ag", [NCORES * 256, NTOK], BF16)
    with ExitStack() as es:
        P = Prog(nc, es)
        P.push_scope()
        emit_A0(P, dict(io, xT_o=xT_d, hT_o=hT_loc, cos_o=cos_loc, sin_o=sin_loc))
        P.pop_scope()
        P.collective("AllGather", hT_loc, hT_ag)
        P.collective("AllGather", cos_loc, cos_ag)
        P.collective("AllGather", sin_loc, sin_ag)
        for l in range(depth):
            last = (l == depth - 1)
            P.push_scope()
            emit_B(P, dict(hT_ag=hT_ag, wfm=wfm_all[l], wtm=wtm_all[l], cst=cst_all[l], cos_ag=cos_ag, sin_ag=sin_ag,
                           mT_loc=mT_loc), nt=nt)
            P.pop_scope()
            P.collective("AllGather", mT_loc, mT_ag)
            P.push_scope()
            tio = dict(io, xT_d=xT_d, mT_ag=mT_ag, gains=gains_all[l], hT_o=hT_loc, out_o=out_o)
            tio.update({k: v[l] for k, v in W.items()})
            emit_T(P, tio, last)
            P.pop_scope()
            if not last:
                P.collective("AllGather", hT_loc, hT_ag)
        P.finish()
    return nc


def _single(emit, ins, outs, **kw):
    nc = bass.Bass("TRN2", target_bir_lowering=False)
    io = {}
    for n, (sh, d) in ins.items():
        io[n] = nc.dram_tensor(n, list(sh), d, kind="ExternalInput").ap()
    for n, (sh, d) in outs.items():
        io[n] = nc.dram_tensor(n, list(sh), d, kind="ExternalOutput").ap()
    with ExitStack() as es:
        P = Prog(nc, es)
        P.push_scope()
        emit(P, io, **kw)
        P.pop_scope()
        P.finish()
    return nc


def build_A0():
    return _single(emit_A0, dict(x=([TPC, D], F32), pos=([1, TPC], I32), g0=([128, KC], F32), ident=([128, 128], F32), ropec=([128, 2], F32)),
                   dict(xT_o=([D, TPC], F32), hT_o=([D, TPC], BF16), cos_o=([128, TPC], F32), sin_o=([128, TPC], F32)))


def build_B():
    return _single(emit_B, dict(hT_ag=([NCORES * D, TPC], BF16), wfm=([D, NFM], F32), wtm=([D, NTM], F32), cst=([128, NCST], F32),
                                cos_ag=([NCORES * 128, TPC], F32), sin_ag=([NCORES * 128, TPC], F32)),
                   dict(mT_loc=([256, NTOK], BF16)))


def build_T(last):
    ins = dict(xT_d=([D, TPC], F32), mT_ag=([D, NTOK], BF16), selm=([128, NCORES], F32), w_out=([D, D], F32), w_xq=([D, D], F32),
               w_xkv=([D, 2 * D], F32), w_xo=([D, D], F32), w_ff1=([D, DFF], F32), w_ff2=([DFF, D], F32),
               gains=([128, 4 * KC], F32), mem=([MEM, D], F32), ident=([128, 128], F32))
    outs = dict(out_o=([TPC, D], F32)) if last else dict(xT_o=([D, TPC], F32), hT_o=([D, TPC], BF16))
    return _single(emit_T, ins, outs, last=last)


_PROGS = {}


def _run(name, in_maps):
    if name not in _PROGS:
        _PROGS[name] = {"A0": build_A0, "B": build_B, "T": lambda: build_T(False), "TL": lambda: build_T(True)}[name]()
    return run_bass_kernel_spmd(_PROGS[name], in_maps, core_ids=list(range(NCORES))).results


LAUNCH_MODE = "multi"


_PROG = []


def kernel(x, mem, positions, norm_mix_g, w_in, conv_w, gdn_a_log, gdn_dt_bias, ret_gn_g, gdn_norm_g,
           w_out, norm_x_g, norm_mem_g, w_xq, w_xkv, w_xo, norm_ffn_g, w_ff1, w_ff2, norm_final_g):
    f = lambda a: np.ascontiguousarray(np.asarray(a, dtype=np.float32))
    x = f(x).reshape(NTOK, D)
    mem = f(mem)
    pos = np.ascontiguousarray(np.asarray(positions, dtype=np.int32)).reshape(NTOK)
    norm_mix_g, norm_x_g, norm_mem_g, norm_ffn_g, norm_final_g = map(f, (norm_mix_g, norm_x_g, norm_mem_g, norm_ffn_g, norm_final_g))
    w_in, conv_w, gdn_a_log, gdn_dt_bias, ret_gn_g, gdn_norm_g = map(f, (w_in, conv_w, gdn_a_log, gdn_dt_bias, ret_gn_g, gdn_norm_g))
    w_out, w_xq, w_xkv, w_xo, w_ff1, w_ff2 = map(f, (w_out, w_xq, w_xkv, w_xo, w_ff1, w_ff2))
    ident = np.eye(128, dtype=np.float32)
    inv_freq = (1.0 / (np.float32(10000.0) ** (np.arange(0, 128, 2, dtype=np.float32) / np.float32(128)))).astype(np.float32)
    ropec = np.zeros((128, 2), np.float32)
    ropec[:, 0] = np.concatenate([inv_freq, inv_freq])
    ropec[:64, 1] = -1.0
    ropec[64:, 1] = 1.0
    tsl = lambda c: slice(c * TPC, (c + 1) * TPC)
    gains_all = np.stack([np.concatenate([_lay_g(norm_x_g[l]), _lay_g(norm_mem_g[l]), _lay_g(norm_ffn_g[l]),
                                          _lay_g(norm_final_g if l == DEPTH - 1 else norm_mix_g[l + 1])], axis=1)
                          for l in range(DEPTH)])
    sel = []
    for c in range(NCORES):
        m_ = np.zeros((128, NCORES), np.float32)
        m_[:, c] = 1.0
        sel.append(m_)
    if LAUNCH_MODE == "multi":
        r = _run("A0", [dict(x=x[tsl(c)], pos=pos[tsl(c)].reshape(1, TPC), g0=_lay_g(norm_mix_g[0]), ident=ident, ropec=ropec)
                        for c in range(NCORES)])
        xT = [r[c]["xT_o"] for c in range(NCORES)]
        hT = [r[c]["hT_o"] for c in range(NCORES)]
        cos_ag = np.ascontiguousarray(np.concatenate([r[c]["cos_o"] for c in range(NCORES)], axis=0))
        sin_ag = np.ascontiguousarray(np.concatenate([r[c]["sin_o"] for c in range(NCORES)], axis=0))
        out = None
        for l in range(DEPTH):
            hT_ag = np.ascontiguousarray(np.concatenate(hT, axis=0))
            maps = []
            for c in range(NCORES):
                wfm, wtm = pack_w_in(c, w_in[l])
                maps.append(dict(hT_ag=hT_ag, wfm=wfm, wtm=wtm,
                                 cst=pack_cst(c, conv_w[l], gdn_a_log[l], gdn_dt_bias[l], ret_gn_g[l], gdn_norm_g[l]),
                                 cos_ag=cos_ag, sin_ag=sin_ag))
            r = _run("B", maps)
            mT_ag = np.ascontiguousarray(np.concatenate([r[c]["mT_loc"] for c in range(NCORES)], axis=0))
            last = (l == DEPTH - 1)
            maps = [dict(xT_d=xT[c], mT_ag=mT_ag, selm=sel[c], w_out=w_out[l], w_xq=w_xq[l], w_xkv=w_xkv[l], w_xo=w_xo[l],
                         w_ff1=w_ff1[l], w_ff2=w_ff2[l], gains=gains_all[l], mem=mem[c // 4], ident=ident) for c in range(NCORES)]
            r = _run("TL" if last else "T", maps)
            if last:
                out = np.concatenate([r[c]["out_o"] for c in range(NCORES)], axis=0)
            else:
                xT = [r[c]["xT_o"] for c in range(NCORES)]
                hT = [r[c]["hT_o"] for c in range(NCORES)]
        return np.ascontiguousarray(out.reshape(BATCH, SEQ, D).astype(np.float32))
    maps = []
    for c in range(NCORES):
        packs = [pack_w_in(c, w_in[l]) for l in range(DEPTH)]
        maps.append(dict(
            x=x[tsl(c)], pos=pos[tsl(c)].reshape(1, TPC), g0=_lay_g(norm_mix_g[0]), ident=ident, ropec=ropec, selm=sel[c],
            mem=mem[c // 4], gains_all=gains_all,
            wfm_all=np.stack([p[0] for p in packs]), wtm_all=np.stack([p[1] for p in packs]),
            cst_all=np.stack([pack_cst(c, conv_w[l], gdn_a_log[l], gdn_dt_bias[l], ret_gn_g[l], gdn_norm_g[l]) for l in range(DEPTH)]),
            w_out=w_out, w_xq=w_xq, w_xkv=w_xkv, w_xo=w_xo, w_ff1=w_ff1, w_ff2=w_ff2))
    if not _PROG:
        _PROG.append(build_fused())
    res = run_bass_kernel_spmd(_PROG[0], maps, core_ids=list(range(NCORES)))
    out = np.concatenate([res.results[c]["out_o"] for c in range(NCORES)], axis=0)
    return np.ascontiguousarray(out.reshape(BATCH, SEQ, D).astype(np.float32))
```

```python
import math
from contextlib import ExitStack

import numpy as np
import ml_dtypes

import concourse.bass as bass
import concourse.mybir as mybir
from concourse.bass_utils import run_bass_kernel_spmd

F32 = mybir.dt.float32
BF16 = mybir.dt.bfloat16
I32 = mybir.dt.int32
AF = mybir.ActivationFunctionType
ALU = mybir.AluOpType
AX = mybir.AxisListType

NCORES = 8
D = 2048
KC = 16
SEQ = 4096
BATCH = 2
NTOK = BATCH * SEQ
TPC = NTOK // NCORES
DEPTH = 4
MEM = 256
DFF = 8192
EPS = 1e-6
RET_HEADS = 8

O_RQ, O_RK, O_RV, O_RG = 0, 1024, 2048, 4096
O_GQKV = 6144
O_GA = O_GQKV + 6144
O_GB = O_GA + 16
O_GZ = O_GB + 16
O_MA = O_GZ + 2048
O_MB = O_MA + 2048


class Buf:
    __slots__ = ("name", "w", "r", "dsem", "excl")

    def __init__(self, name, excl=False):
        self.name = name
        self.w = None
        self.r = []
        self.dsem = None
        self.excl = excl


class Prog:
    ENG = ("pe", "act", "dve", "pool", "sp")

    def __init__(self, nc, es, same_sync=("act", "dve", "pool", "sp")):
        self.nc = nc
        self.es = es
        self.eng = dict(pe=nc.tensor, act=nc.scalar, dve=nc.vector, pool=nc.gpsimd, sp=nc.sync)
        self.sems = []
        self.esem = {}
        for e in self.ENG:
            self.esem[e] = self._newsem("e_" + e, False)
        self.seen = {e: {} for e in self.ENG}
        self.same_sync = same_sync
        self.out_toks = []
        self.nbuf = 0
        self.free_dsems = {}
        self.scope = None
        self.banks = [es.enter_context(nc.psum_tensor(f"p_bank{i}", [128, 512], F32)) for i in range(8)]
        self.bankb = [Buf(f"bank{i}", True) for i in range(8)]

    def _newsem(self, name, is_dma):
        h = self.es.enter_context(self.nc.semaphore(name))
        self.sems.append([h, 0, is_dma])
        return len(self.sems) - 1

    def buf(self, name=None, excl=False):
        self.nbuf += 1
        b = Buf(name or f"b{self.nbuf}", excl)
        if self.scope is not None:
            self.scope[1].append(b)
        return b

    def sb(self, name, shape, dt):
        st = self.scope[0] if self.scope is not None else self.es
        self.nbuf += 1
        return st.enter_context(self.nc.sbuf_tensor(f"s{self.nbuf}_" + name, list(shape), dt))

    def push_scope(self):
        assert self.scope is None
        self.scope = (ExitStack(), [])

    def pop_scope(self):
        self.barrier()
        st, bufs = self.scope
        for b in bufs:
            if b.dsem:
                for qc, k in b.dsem.items():
                    self.free_dsems.setdefault(qc, []).append(k)
                b.dsem = None
        st.close()
        self.scope = None
        for b in self.bankb:
            b.w = None
            b.r = []

    def barrier(self):
        toks = [(k, v[1]) for k, v in enumerate(self.sems) if v[1] > 0]
        for e in self.ENG:
            self._wait(e, toks)

    def collective(self, kind, in_ap, out_ap):
        self.barrier()
        k = self._newsem(f"cc{len(self.sems)}", True)
        inst = self.nc.gpsimd.collective_compute(kind, ALU.bypass, replica_groups=[list(range(NCORES))],
                                                 ins=[in_ap.opt()], outs=[out_ap.opt()])
        inst.then_inc(self.sems[k][0])
        self.sems[k][1] = 1
        self.barrier()

    def ps(self, name, shape, dt=F32):
        return self.es.enter_context(self.nc.psum_tensor("p_" + name, list(shape), dt))

    def _wait(self, e, toks):
        need = {}
        for t in toks:
            if t is None:
                continue
            k, v = t
            if need.get(k, 0) < v:
                need[k] = v
        for k, v in need.items():
            h, issued, is_dma = self.sems[k]
            if is_dma:
                v = issued
            if k == self.esem[e] and e not in self.same_sync:
                continue
            if self.seen[e].get(k, 0) >= v:
                continue
            self.seen[e][k] = v
            self.eng[e].wait_ge(h, v)

    def _deps(self, reads, writes):
        toks = []
        for b in reads:
            toks.append(b.w)
        for b in writes:
            toks.append(b.w)
            toks.extend(b.r)
        return toks

    def _commit(self, tok, reads, writes):
        for b in reads:
            b.r.append(tok)
        for b in writes:
            b.w = tok
            b.r = []

    def op(self, e, fn, reads=(), writes=()):
        if any(b.excl for b in reads):
            writes = list(writes) + [b for b in reads if b.excl]
            reads = [b for b in reads if not b.excl]
        self._wait(e, self._deps(reads, writes))
        inst = fn(self.eng[e])
        k = self.esem[e]
        self.sems[k][1] += 1
        inst.then_inc(self.sems[k][0], 1)
        tok = (k, self.sems[k][1])
        self._commit(tok, reads, writes)
        return tok

    def dma(self, q, out, in_, reads=(), writes=(), sem_buf=None, is_output=False, **kw):
        self._wait(q, self._deps(reads, writes))
        sb = sem_buf or (writes[0] if writes else reads[0])
        if sb.dsem is None:
            sb.dsem = {}
        qc = "sw" if q == "pool" else "hw"
        if qc not in sb.dsem:
            fl = self.free_dsems.setdefault(qc, [])
            sb.dsem[qc] = fl.pop() if fl else self._newsem(f"d{len(self.sems)}", True)
        k = sb.dsem[qc]
        inst = self.eng[q].dma_start(out=out, in_=in_, **kw)
        self.sems[k][1] += 16
        inst.then_inc(self.sems[k][0], 16)
        tok = (k, self.sems[k][1])
        self._commit(tok, reads, writes)
        if is_output:
            self.out_toks.append(tok)
        return tok

    def finish(self):
        self.barrier()


class WStream:
    def __init__(self, P, nslot=2):
        self.P = P
        self.n = nslot
        self.slots = [P.sb(f"wslot{i}", [128, KC, 512], BF16) for i in range(nslot)]
        self.bufs = [P.buf(f"wslot{i}") for i in range(nslot)]
        self.tiles = []
        self.issued = 0

    def add(self, w_ap, r0, c0):
        self.tiles.append(w_ap[r0:r0 + 2048, c0:c0 + 512].rearrange("(kc p) n -> p kc n", p=128))
        return len(self.tiles) - 1

    def _issue(self, i):
        s = i % self.n
        self.P.dma("pool", self.slots[s][:], self.tiles[i], writes=[self.bufs[s]])

    def get(self, i):
        while self.issued <= min(i + self.n - 1, len(self.tiles) - 1):
            self._issue(self.issued)
            self.issued += 1
        s = i % self.n
        return self.slots[s], self.bufs[s]


def _consts():
    ident = np.eye(128, dtype=np.float32)
    return ident


def _lay_g(g):
    return np.ascontiguousarray(np.asarray(g, np.float32).reshape(KC, 128).T)


class TokPhase:
    def __init__(self, P):
        self.nc = P.nc
        self.P = P
        self.xT = P.sb("xT", [128, KC, TPC], F32)
        self.xb = [[P.buf(f"x{c}_{tt}") for tt in range(2)] for c in range(KC)]
        self.hT = P.sb("hT", [128, KC, TPC], BF16)
        self.hb = [[P.buf(f"h{c}_{tt}") for tt in range(2)] for c in range(KC)]
        self.ident = P.sb("ident", [128, 128], F32)
        self.identb = P.buf("ident")
        self.ones = P.sb("ones", [128, 128], BF16)
        self.onesb = P.buf("ones")
        self.eps = P.sb("eps", [128, 1], F32)
        self.epsb = P.buf("eps")
        self.sq = [P.sb(f"sq{i}", [128, 512], BF16) for i in range(2)]
        self.sqb = [P.buf(f"sq{i}") for i in range(2)]
        self.tmp = P.sb("tmpn", [128, 512], F32)
        self.tmpb = P.buf("tmpn")
        self.rstd = P.sb("rstd", [128, 512], F32)
        self.rstdb = P.buf("rstd")
        self.psb = P.banks
        self.psbb = P.bankb
        P.op("dve", lambda e: e.memset(self.ones[:], 1.0), writes=[self.onesb])
        P.op("dve", lambda e: e.memset(self.eps[:], EPS), writes=[self.epsb])

    def load_ident(self, ident_ap):
        self.P.dma("sp", self.ident[:], ident_ap, writes=[self.identb])

    def norm(self, g_sb, g_b, emit, bank=4):
        P = self.P
        ps_n, ps_nb = self.psb[bank], self.psbb[bank]
        for tt in range(2):
            ts = slice(tt * 512, (tt + 1) * 512)
            for c in range(KC):
                s = c % 2
                P.op("act", lambda e: e.activation(out=self.sq[s][:], in_=self.xT[:, c, ts], func=AF.Square),
                     reads=[self.xb[c][tt]], writes=[self.sqb[s]])
                P.op("pe", lambda e: e.matmul(ps_n[:], self.ones[:], self.sq[s][:], start=(c == 0), stop=(c == KC - 1)),
                     reads=[self.sqb[s], self.onesb], writes=[ps_nb])
            P.op("act", lambda e: e.activation(out=self.tmp[:], in_=ps_n[:], func=AF.Ln, scale=1.0 / D,
                                               bias=self.eps[:, 0:1]),
                 reads=[ps_nb, self.epsb], writes=[self.tmpb])
            P.op("act", lambda e: e.activation(out=self.rstd[:], in_=self.tmp[:], func=AF.Exp, scale=-0.5),
                 reads=[self.tmpb], writes=[self.rstdb])
            for c in range(KC):
                emit(c, tt, ts)

    def norm_to_hT(self, g_sb, g_b):
        P = self.P

        def emit(c, tt, ts):
            P.op("dve", lambda e: e.scalar_tensor_tensor(out=self.hT[:, c, ts], in0=self.xT[:, c, ts],
                                                         scalar=g_sb[:, c:c + 1], in1=self.rstd[:],
                                                         op0=ALU.mult, op1=ALU.mult),
                 reads=[self.xb[c][tt], g_b, self.rstdb], writes=[self.hb[c][tt]])
        self.norm(g_sb, g_b, emit)


def emit_A0(P, io):
    nc = P.nc
    x, pos, g0, ident, ropec = io["x"], io["pos"], io["g0"], io["ident"], io["ropec"]
    xT_o, hT_o, cos_o, sin_o = io["xT_o"], io["hT_o"], io["cos_o"], io["sin_o"]
    if True:
        S = TokPhase(P)
        S.load_ident(ident)
        g_sb = P.sb("g0", [128, KC], F32)
        g_b = P.buf("g0")
        P.dma("sp", g_sb[:], g0, writes=[g_b])
        xin = [P.sb(f"xin{i}", [128, D], F32) for i in range(2)]
        xinb = [P.buf(f"xin{i}") for i in range(2)]
        for blk in range(TPC // 128):
            s = blk % 2
            tt = blk // 4
            P.dma("sp", xin[s][:], x[blk * 128:(blk + 1) * 128, :], writes=[xinb[s]])
            for c in range(KC):
                bk = c % 4
                P.op("pe", lambda e: e.transpose(S.psb[bk][:, 0:128], xin[s][:, c * 128:(c + 1) * 128], S.ident[:]),
                     reads=[xinb[s], S.identb], writes=[S.psbb[bk]])
                eng = "act" if c % 2 else "dve"
                if eng == "act":
                    P.op("act", lambda e: e.copy(out=S.xT[:, c, blk * 128:(blk + 1) * 128], in_=S.psb[bk][:, 0:128]),
                         reads=[S.psbb[bk]], writes=[S.xb[c][tt]])
                else:
                    P.op("dve", lambda e: e.tensor_copy(out=S.xT[:, c, blk * 128:(blk + 1) * 128], in_=S.psb[bk][:, 0:128]),
                         reads=[S.psbb[bk]], writes=[S.xb[c][tt]])
        xT_ov = xT_o.rearrange("(kc p) t -> p kc t", p=128)
        for c4 in range(4):
            P.dma("sp", xT_ov[:, c4 * 4:(c4 + 1) * 4, :], S.xT[:, c4 * 4:(c4 + 1) * 4, :],
                  reads=[S.xb[c][tt] for c in range(c4 * 4, c4 * 4 + 4) for tt in range(2)],
                  sem_buf=S.xb[c4 * 4][0])
        S.norm_to_hT(g_sb, g_b)
        hT_ov = hT_o.rearrange("(kc p) t -> p kc t", p=128)
        for c4 in range(4):
            P.dma("sp", hT_ov[:, c4 * 4:(c4 + 1) * 4, :], S.hT[:, c4 * 4:(c4 + 1) * 4, :],
                  reads=[S.hb[c][tt] for c in range(c4 * 4, c4 * 4 + 4) for tt in range(2)],
                  sem_buf=S.hb[c4 * 4][0])
        rc = P.sb("ropec", [128, 2], F32)
        rcb = P.buf("ropec")
        P.dma("sp", rc[:], ropec, writes=[rcb])
        pi_ = P.sb("pos_i", [128, TPC], I32)
        pib = P.buf("pos_i")
        P.dma("sp", pi_[:], pos.partition_broadcast(128) if hasattr(pos, "partition_broadcast") else pos,
              writes=[pib])
        ang = P.sb("ang", [128, TPC], F32)
        angb = P.buf("ang")
        kf = P.sb("kf", [128, TPC], F32)
        kfb = P.buf("kf")
        ki = P.sb("ki", [128, TPC], I32)
        kib = P.buf("ki")
        r = P.sb("rr", [128, TPC], F32)
        rb = P.buf("rr")
        m = P.sb("mm", [128, TPC], F32)
        mb = P.buf("mm")
        so = P.sb("so", [128, TPC], F32)
        sob = P.buf("so")
        TWO_PI = 2.0 * math.pi
        C1 = 6.28125
        C2 = TWO_PI - C1
        P.op("dve", lambda e: e.tensor_copy(out=ang[:], in_=pi_[:]), reads=[pib], writes=[angb])
        P.op("dve", lambda e: e.tensor_scalar(out=ang[:], in0=ang[:], scalar1=rc[:, 0:1], scalar2=None, op0=ALU.mult),
             reads=[angb, rcb], writes=[angb])
        P.op("dve", lambda e: e.tensor_scalar(out=kf[:], in0=ang[:], scalar1=1.0 / TWO_PI, scalar2=None, op0=ALU.mult),
             reads=[angb], writes=[kfb])
        P.op("dve", lambda e: e.tensor_copy(out=ki[:], in_=kf[:]), reads=[kfb], writes=[kib])
        P.op("dve", lambda e: e.tensor_copy(out=kf[:], in_=ki[:]), reads=[kib], writes=[kfb])
        P.op("dve", lambda e: e.scalar_tensor_tensor(out=r[:], in0=kf[:], scalar=-C1, in1=ang[:], op0=ALU.mult, op1=ALU.add),
             reads=[kfb, angb], writes=[rb])
        P.op("dve", lambda e: e.scalar_tensor_tensor(out=r[:], in0=kf[:], scalar=-C2, in1=r[:], op0=ALU.mult, op1=ALU.add),
             reads=[kfb, rb], writes=[rb])

        def wrap(t, tb):
            P.op("dve", lambda e: e.tensor_single_scalar(out=m[:], in_=t[:], scalar=math.pi, op=ALU.is_gt),
                 reads=[tb], writes=[mb])
            P.op("dve", lambda e: e.scalar_tensor_tensor(out=t[:], in0=m[:], scalar=-TWO_PI, in1=t[:], op0=ALU.mult, op1=ALU.add),
                 reads=[mb, tb], writes=[tb])
            P.op("dve", lambda e: e.tensor_single_scalar(out=m[:], in_=t[:], scalar=-math.pi, op=ALU.is_lt),
                 reads=[tb], writes=[mb])
            P.op("dve", lambda e: e.scalar_tensor_tensor(out=t[:], in0=m[:], scalar=TWO_PI, in1=t[:], op0=ALU.mult, op1=ALU.add),
                 reads=[mb, tb], writes=[tb])
            P.op("dve", lambda e: e.tensor_scalar(out=t[:], in0=t[:], scalar1=math.pi, scalar2=-math.pi, op0=ALU.min, op1=ALU.max),
                 reads=[tb], writes=[tb])

        wrap(r, rb)
        P.op("act", lambda e: e.activation(out=so[:], in_=r[:], func=AF.Sin), reads=[rb], writes=[sob])
        P.op("dve", lambda e: e.tensor_scalar(out=so[:], in0=so[:], scalar1=rc[:, 1:2], scalar2=None, op0=ALU.mult),
             reads=[sob, rcb], writes=[sob])
        P.dma("sp", sin_o, so[:], reads=[sob])
        P.op("dve", lambda e: e.tensor_scalar(out=r[:], in0=r[:], scalar1=math.pi / 2, scalar2=None, op0=ALU.add),
             reads=[rb], writes=[rb])
        wrap(r, rb)
        P.op("act", lambda e: e.activation(out=kf[:], in_=r[:], func=AF.Sin), reads=[rb], writes=[kfb])
        P.dma("sp", cos_o, kf[:], reads=[kfb])


def emit_T(P, io, last):
    nc = P.nc
    xT_i, mT_ag, selm = io["xT_d"], io["mT_ag"], io["selm"]
    w_out, w_xq, w_xkv, w_xo, w_ff1, w_ff2 = (io[k] for k in ("w_out", "w_xq", "w_xkv", "w_xo", "w_ff1", "w_ff2"))
    gains, mem, ident = io["gains"], io["mem"], io["ident"]
    if last:
        out_o = io["out_o"]
    else:
        xT_o, hT_o = io.get("xT_o", io["xT_d"]), io["hT_o"]
    if True:
        S = TokPhase(P)
        xT, hT, xb, hb = S.xT, S.hT, S.xb, S.hb
        psb, psbb = S.psb, S.psbb
        S.load_ident(ident)
        g_sb = P.sb("gains", [128, 4 * KC], F32)
        g_b = P.buf("gains")
        P.dma("sp", g_sb[:], gains, writes=[g_b])
        gX, gM, gF, gN = (g_sb[:, i * KC:(i + 1) * KC] for i in range(4))
        big2 = P.sb("big2", [128, KC, TPC], BF16)
        b2b = [[P.buf(f"b2_{c}_{tt}") for tt in range(2)] for c in range(KC)]
        xT_iv = xT_i.rearrange("(kc p) t -> p kc t", p=128)
        sel_sb = P.sb("selm", [128, NCORES], F32)
        sel_b = P.buf("selm")
        P.dma("sp", sel_sb[:], selm, writes=[sel_b])
        mT_v = mT_ag.rearrange("(kc p) t -> p kc t", p=128)
        allh = [hb[c][tt] for c in range(KC) for tt in range(2)]
        for tt in range(2):
            hts = [hb[c][tt] for c in range(KC)]
            for r in range(NCORES):
                half = r % 2
                hbuf = [b2b[c][half] for c in range(KC)]
                P.dma("sp", big2[:, :, half * 512:(half + 1) * 512], mT_v[:, :, r * TPC + tt * 512:r * TPC + (tt + 1) * 512],
                      writes=hbuf, sem_buf=hbuf[0])
                if r == 0:
                    P.op("dve", lambda e: e.tensor_scalar(out=hT[:, :, tt * 512:(tt + 1) * 512], in0=big2[:, :, half * 512:(half + 1) * 512],
                                                          scalar1=sel_sb[:, r:r + 1], scalar2=None, op0=ALU.mult),
                         reads=hbuf + [sel_b], writes=hts)
                else:
                    P.op("dve", lambda e: e.scalar_tensor_tensor(out=hT[:, :, tt * 512:(tt + 1) * 512], in0=big2[:, :, half * 512:(half + 1) * 512],
                                                                 scalar=sel_sb[:, r:r + 1], in1=hT[:, :, tt * 512:(tt + 1) * 512],
                                                                 op0=ALU.mult, op1=ALU.add),
                         reads=hbuf + [sel_b], writes=hts)
        for c4 in range(4):
            cs = slice(c4 * 4, c4 * 4 + 4)
            P.dma("sp", xT[:, cs, :], xT_iv[:, cs, :], writes=[xb[c][tt] for c in range(c4 * 4, c4 * 4 + 4) for tt in range(2)],
                  sem_buf=xb[c4 * 4][0])
        WS = WStream(P, nslot=2)
        t_out = [WS.add(w_out, 0, j * 512) for j in range(4)]
        t_kv = [WS.add(w_xkv, 0, j * 512) for j in range(8)]
        t_xq = [WS.add(w_xq, 0, j * 512) for j in range(4)]
        t_xo = [WS.add(w_xo, 0, j * 512) for j in range(4)]
        t_ff = []
        for hg in range(4):
            t_ff.append(([WS.add(w_ff1, 0, hg * 2048 + j * 512) for j in range(4)],
                         [WS.add(w_ff2, hg * 2048, j * 512) for j in range(4)]))
        bank_ctr = [0]

        def linear(tiles, rhs, rhsb, evac):
            for j, ti in enumerate(tiles):
                wsl, wb = WS.get(ti)
                for oc in range(4):
                    for tt in range(2):
                        bk = bank_ctr[0] % 4
                        bank_ctr[0] += 1
                        ts = slice(tt * 512, (tt + 1) * 512)

                        def mm(e):
                            ins = None
                            for kc in range(KC):
                                ins = e.matmul(psb[bk][:], wsl[:, kc, oc * 128:(oc + 1) * 128], rhs[:, kc, ts],
                                               start=(kc == 0), stop=(kc == KC - 1))
                            return ins
                        P.op("pe", mm, reads=[wb] + [rhsb[kc][tt] for kc in range(KC)], writes=[psbb[bk]])
                        evac(j * 4 + oc, tt, ts, psb[bk], psbb[bk])

        def evac_add_x(c, tt, ts, ps, psb_):
            P.op("dve", lambda e: e.tensor_tensor(out=xT[:, c, ts], in0=ps[:], in1=xT[:, c, ts], op=ALU.add),
                 reads=[psb_, xb[c][tt]], writes=[xb[c][tt]])

        linear(t_out, hT, hb, evac_add_x)

        b2flat = big2[:].rearrange("p a b -> p (a b)")
        mst = b2flat[:, 0:4096].bitcast(F32)
        msq = b2flat[:, 4096:6144]
        memT = b2flat[:, 6144:6144 + KC * MEM].rearrange("p (c m) -> p c m", c=KC)
        memTb = P.buf("memT")
        mstb = P.buf("mst")
        msqb = P.buf("msq")
        mss = P.sb("mss", [128, 4], F32)
        mssb = P.buf("mss")
        allb2 = [b2b[c][tt] for c in range(KC) for tt in range(2)]
        for mc in range(2):
            P.dma("sp", mst[:], mem[mc * 128:(mc + 1) * 128, :], writes=[mstb] + allb2, sem_buf=mstb)
            P.op("act", lambda e: e.activation(out=msq[:], in_=mst[:], func=AF.Square, accum_out=mss[:, 0:1]),
                 reads=[mstb], writes=[msqb, mssb] + allb2)
            P.op("act", lambda e: e.activation(out=mss[:, 1:2], in_=mss[:, 0:1], func=AF.Ln, scale=1.0 / D, bias=S.eps[:, 0:1]),
                 reads=[mssb, S.epsb], writes=[mssb])
            P.op("act", lambda e: e.activation(out=mss[:, 2:3], in_=mss[:, 1:2], func=AF.Exp, scale=-0.5),
                 reads=[mssb], writes=[mssb])
            P.op("dve", lambda e: e.tensor_scalar(out=mst[:], in0=mst[:], scalar1=mss[:, 2:3], scalar2=None, op0=ALU.mult),
                 reads=[mstb, mssb], writes=[mstb])
            for c in range(KC):
                bk = 4 + c % 2
                P.op("pe", lambda e: e.transpose(psb[bk][:, 0:128], mst[:, c * 128:(c + 1) * 128], S.ident[:]),
                     reads=[mstb, S.identb], writes=[psbb[bk]])
                P.op("act", lambda e: e.activation(out=memT[:, c, mc * 128:(mc + 1) * 128], in_=psb[bk][:, 0:128],
                                                   func=AF.Copy, scale=gM[:, c:c + 1]),
                     reads=[psbb[bk], g_b], writes=[memTb] + allb2)
        kT = P.sb("kT", [128, KC, MEM], BF16)
        kTb = P.buf("kT")
        V = P.sb("V", [128, 2, D], BF16)
        Vb = P.buf("V")
        for j in range(4):
            wsl, wb = WS.get(t_kv[j])
            for oc in range(4):
                bk = bank_ctr[0] % 4
                bank_ctr[0] += 1

                def mm(e):
                    ins = None
                    for kc in range(KC):
                        ins = e.matmul(psb[bk][:, 0:MEM], wsl[:, kc, oc * 128:(oc + 1) * 128], memT[:, kc, :],
                                       start=(kc == 0), stop=(kc == KC - 1))
                    return ins
                P.op("pe", mm, reads=[wb, memTb], writes=[psbb[bk]])
                P.op("act", lambda e: e.copy(out=kT[:, j * 4 + oc, :], in_=psb[bk][:, 0:MEM]),
                     reads=[psbb[bk]], writes=[kTb])
        for j in range(4):
            wsl, wb = WS.get(t_kv[4 + j])
            for mc in range(2):
                bk = bank_ctr[0] % 4
                bank_ctr[0] += 1

                def mm(e):
                    ins = None
                    for kc in range(KC):
                        ins = e.matmul(psb[bk][:], memT[:, kc, mc * 128:(mc + 1) * 128], wsl[:, kc, :],
                                       start=(kc == 0), stop=(kc == KC - 1))
                    return ins
                P.op("pe", mm, reads=[wb, memTb], writes=[psbb[bk]])
                P.op("act", lambda e: e.copy(out=V[:, mc, j * 512:(j + 1) * 512], in_=psb[bk][:]),
                     reads=[psbb[bk]], writes=[Vb])
        S.norm_to_hT(gX, g_b)

        def evac_q(c, tt, ts, ps, psb_):
            P.op("act", lambda e: e.copy(out=big2[:, c, ts], in_=ps[:]), reads=[psb_],
                 writes=[b2b[c][tt], mstb, msqb, memTb])
        linear(t_xq, hT, hb, evac_q)
        expP = P.sb("expP", [128, 2, 512], BF16)
        expPb = [P.buf("expP0"), P.buf("expP1")]
        rden = P.sb("rden", [128, 512], F32)
        rdenb = P.buf("rden")
        SCALE = 512.0 ** -0.5
        for hd in range(4):
            for tt in range(2):
                ts = slice(tt * 512, (tt + 1) * 512)
                for mc in range(2):
                    bk = 4 + mc

                    def mm(e):
                        ins = None
                        for dc in range(4):
                            ins = e.matmul(psb[bk][:], kT[:, hd * 4 + dc, mc * 128:(mc + 1) * 128], big2[:, hd * 4 + dc, ts],
                                           start=(dc == 0), stop=(dc == 3))
                        return ins
                    P.op("pe", mm, reads=[kTb] + [b2b[hd * 4 + dc][tt] for dc in range(4)], writes=[psbb[bk]])
                    P.op("act", lambda e: e.activation(out=expP[:, mc, :], in_=psb[bk][:], func=AF.Exp, scale=SCALE),
                         reads=[psbb[bk]], writes=[expPb[mc]])

                def mmd(e):
                    e.matmul(psb[6][:], S.ones[:], expP[:, 0, :], start=True, stop=False)
                    return e.matmul(psb[6][:], S.ones[:], expP[:, 1, :], start=False, stop=True)
                P.op("pe", mmd, reads=[S.onesb] + expPb, writes=[psbb[6]])
                P.op("act", lambda e: e.activation(out=S.tmp[:], in_=psb[6][:], func=AF.Ln), reads=[psbb[6]], writes=[S.tmpb])
                P.op("act", lambda e: e.activation(out=rden[:], in_=S.tmp[:], func=AF.Exp, scale=-1.0),
                     reads=[S.tmpb], writes=[rdenb])
                for dc in range(4):
                    bk = bank_ctr[0] % 4
                    bank_ctr[0] += 1
                    c = hd * 4 + dc

                    def mmo(e):
                        e.matmul(psb[bk][:], V[:, 0, c * 128:(c + 1) * 128], expP[:, 0, :], start=True, stop=False)
                        return e.matmul(psb[bk][:], V[:, 1, c * 128:(c + 1) * 128], expP[:, 1, :], start=False, stop=True)
                    P.op("pe", mmo, reads=[Vb] + expPb, writes=[psbb[bk]])
                    P.op("dve", lambda e: e.tensor_tensor(out=hT[:, c, ts], in0=psb[bk][:], in1=rden[:], op=ALU.mult),
                         reads=[psbb[bk], rdenb], writes=[hb[c][tt]])
        linear(t_xo, hT, hb, evac_add_x)

        S.norm_to_hT(gF, g_b)
        rl = [P.sb(f"rl{i}", [128, 512], F32) for i in range(2)]
        rlb = [P.buf(f"rl{i}") for i in range(2)]
        rl_ctr = [0]

        def evac_h(cc_base):
            def f(c, tt, ts, ps, psb_):
                s = rl_ctr[0] % 2
                rl_ctr[0] += 1
                P.op("act", lambda e: e.activation(out=rl[s][:], in_=ps[:], func=AF.Relu), reads=[psb_], writes=[rlb[s]])
                P.op("dve", lambda e: e.tensor_tensor(out=big2[:, c, ts], in0=rl[s][:], in1=rl[s][:], op=ALU.mult),
                     reads=[rlb[s]], writes=[b2b[c][tt]])
            return f
        for hg in range(4):
            linear(t_ff[hg][0], hT, hb, evac_h(hg))
            linear(t_ff[hg][1], big2, b2b, evac_add_x)

        if not last:
            xT_ov = xT_o.rearrange("(kc p) t -> p kc t", p=128)
            for c4 in range(4):
                P.dma("sp", xT_ov[:, c4 * 4:(c4 + 1) * 4, :], xT[:, c4 * 4:(c4 + 1) * 4, :],
                      reads=[xb[c][tt] for c in range(c4 * 4, c4 * 4 + 4) for tt in range(2)],
                      sem_buf=xb[c4 * 4][0])
            S.norm_to_hT(gN, g_b)
            hT_ov = hT_o.rearrange("(kc p) t -> p kc t", p=128)
            for c4 in range(4):
                P.dma("sp", hT_ov[:, c4 * 4:(c4 + 1) * 4, :], hT[:, c4 * 4:(c4 + 1) * 4, :],
                      reads=[hb[c][tt] for c in range(c4 * 4, c4 * 4 + 4) for tt in range(2)],
                      sem_buf=hb[c4 * 4][0])
        else:
            hflat = hT[:].rearrange("p a b -> p (a b)").bitcast(F32)
            ost = [hflat[:, i * D:(i + 1) * D] for i in range(2)]
            ostb = [P.buf(f"ost{i}") for i in range(2)]
            for ob in ostb:
                for c in range(KC):
                    for tt in range(2):
                        ob.r.extend(hb[c][tt].r)
                        ob.r.append(hb[c][tt].w)
            finv = big2[:].rearrange("p a b -> p (a b)").bitcast(F32)
            finb = P.buf("fin")

            def emit(c, tt, ts):
                P.op("dve", lambda e: e.scalar_tensor_tensor(out=finv[:, c * 512:(c + 1) * 512], in0=xT[:, c, ts],
                                                             scalar=gN[:, c:c + 1], in1=S.rstd[:],
                                                             op0=ALU.mult, op1=ALU.mult),
                     reads=[xb[c][tt], g_b, S.rstdb] + [b2b[cc][t2] for cc in range(KC) for t2 in range(2)],
                     writes=[finb])
                if c == KC - 1:
                    for tb in range(4):
                        blk = tt * 4 + tb
                        s = blk % 2
                        for cc in range(KC):
                            bk = cc % 4
                            P.op("pe", lambda e: e.transpose(psb[bk][:, 0:128],
                                                             finv[:, cc * 512 + tb * 128: cc * 512 + (tb + 1) * 128], S.ident[:]),
                                 reads=[finb, S.identb], writes=[psbb[bk]])
                            if cc % 2:
                                P.op("act", lambda e: e.copy(out=ost[s][:, cc * 128:(cc + 1) * 128], in_=psb[bk][:, 0:128]),
                                     reads=[psbb[bk]], writes=[ostb[s]])
                            else:
                                P.op("dve", lambda e: e.tensor_copy(out=ost[s][:, cc * 128:(cc + 1) * 128], in_=psb[bk][:, 0:128]),
                                     reads=[psbb[bk]], writes=[ostb[s]])
                        P.dma("sp", out_o[blk * 128:(blk + 1) * 128, :], ost[s], reads=[ostb[s]], is_output=True)
            S.norm(gN, g_b, emit)


NFM = 10 * 128
NTM = 1284
C_ID, C_DM, C_QD, C_U, C_NS, C_NC = 0, 128, 256, 768, 896, 1024
C_KDEC, C_G128, C_ONE, C_EPS6, C_EPS5, C_LNS, C_HV = 1152, 1153, 1154, 1155, 1156, 1157, 1158
C_RETG = 1162
C_GDNG = C_RETG + 256
C_CONV = C_GDNG + 128
C_MD32 = C_CONV + 24
C_MC0 = C_MD32 + 128
C_MC1 = C_MC0 + 128
NCST = C_MC1 + 128
NEG = -30000.0


def emit_B(P, io, nt=None):
    nc = P.nc
    hT_ag, wfm_d, wtm_d, cst_d, cos_ag, sin_ag, mT_o = (io[k] for k in ("hT_ag", "wfm", "wtm", "cst", "cos_ag", "sin_ag", "mT_loc"))
    if True:
        A = lambda fn, r=(), w=(): P.op("act", fn, r, w)
        V = lambda fn, r=(), w=(): P.op("dve", fn, r, w)
        G = lambda fn, r=(), w=(): P.op("pool", fn, r, w)
        T = lambda fn, r=(), w=(): P.op("pe", fn, r, w)
        cst = P.sb("cst", [128, NCST], F32)
        cstb = P.buf("cst")
        P.dma("sp", cst[:], cst_d, writes=[cstb])
        ident = cst[:, C_ID:C_ID + 128]
        DMt = cst[:, C_DM:C_DM + 128]
        qdec = cst[:, C_QD:C_QD + 512]
        Utri = cst[:, C_U:C_U + 128]
        NEGs = cst[:, C_NS:C_NS + 128]
        NEGc = cst[:, C_NC:C_NC + 128]
        col = lambda i: cst[:, i:i + 1]
        MD32 = cst[:, C_MD32:C_MD32 + 128]
        MC0 = cst[:, C_MC0:C_MC0 + 128]
        MC1 = cst[:, C_MC1:C_MC1 + 128]
        retg = cst[:, C_RETG:C_RETG + 256]
        gdng = cst[:, C_GDNG:C_GDNG + 128]
        identb = P.sb("identb", [128, 128], BF16)
        identbb = P.buf("identb")
        V(lambda e: e.tensor_copy(out=identb[:], in_=ident), [cstb], [identbb])
        ones_bf = P.sb("ones_bf", [128, 128], BF16)
        ones_f = P.sb("ones_f", [128, 128], F32)
        onesb = P.buf("ones")
        V(lambda e: e.memset(ones_bf[:], 1.0), [], [onesb])
        V(lambda e: e.memset(ones_f[:], 1.0), [], [onesb])
        nea = P.sb("nea", [128, 2], F32)
        neab = P.buf("nea")
        A(lambda e: e.activation(out=nea[:], in_=cst[:, C_HV:C_HV + 2], func=AF.Exp), [cstb], [neab])
        V(lambda e: e.tensor_scalar(out=nea[:], in0=nea[:], scalar1=-1.0, scalar2=None, op0=ALU.mult), [neab], [neab])
        wfm = P.sb("wfm", [128, KC, NFM], BF16)
        wtm = P.sb("wtm", [128, KC, NTM], BF16)
        wfmb = P.buf("wfm")
        wtmb = P.buf("wtm")
        wfm_v = wfm_d.rearrange("(kc p) n -> p kc n", p=128)
        wtm_v = wtm_d.rearrange("(kc p) n -> p kc n", p=128)
        for a, b in ((0, 512), (512, 1024), (1024, NFM)):
            P.dma("pool", wfm[:, :, a:b], wfm_v[:, :, a:b], writes=[wfmb])
        for a, b in ((0, 512), (512, 1024), (1024, NTM)):
            P.dma("pool", wtm[:, :, a:b], wtm_v[:, :, a:b], writes=[wtmb])
        hsl = [P.sb(f"hsl{i}", [128, KC, 512], BF16) for i in range(2)]
        hslb = [P.buf(f"hsl{i}") for i in range(2)]
        cs_sl = [P.sb(f"cs{i}", [128, 2, 512], F32) for i in range(2)]
        cs_b = [P.buf(f"cs{i}") for i in range(2)]
        hT_v = hT_ag.rearrange("(r kc p) t -> r p kc t", p=128, kc=KC)
        cos_v = cos_ag.rearrange("(r p) t -> r p t", p=128)
        sin_v = sin_ag.rearrange("(r p) t -> r p t", p=128)
        bank, bankb = P.banks, P.bankb
        psFM, psFMb = bank[0:2], bankb[0:2]
        psTM, psTMb = bank[0:2], bankb[0:2]
        psN, psNb = bank[2], bankb[2]
        psRo, psRs = bank[2][:, 0:256], bank[2][:, 256:512]
        psRob, psRsb = bankb[2], bankb[2]
        pools = {0: [3, 4], 1: [5, 6], 2: [7]}
        sm_ctr = {0: 0, 1: 0, 2: 0}
        sm_last = [0]

        def sm(pool):
            i = pools[pool][sm_ctr[pool] % len(pools[pool])]
            sm_ctr[pool] += 1
            sm_last[0] = i
            return bank[i][:, 0:128], bankb[i]

        def bfv(ap):
            return ap.bitcast(BF16)[:, 0:128]

        t1 = P.sb("t1", [128, 512], F32)
        t2 = P.sb("t2", [128, 512], F32)
        t1b, t2b = P.buf("t1"), P.buf("t2")
        qr = [P.sb(f"qr{i}", [128, 512], BF16) for i in range(1)]
        qd = [P.sb(f"qd{i}", [128, 512], BF16) for i in range(1)]
        kr = [P.sb(f"kr{i}", [128, 512], BF16) for i in range(1)]
        qrb = [P.buf(f"qr{i}") for i in range(1)]
        qdb = [P.buf(f"qd{i}") for i in range(1)]
        krb = [P.buf(f"kr{i}") for i in range(1)]
        stage = [P.sb(f"stage{j}", [128, 515], F32) for j in range(6)]
        stageb = [P.buf(f"stage{j}") for j in range(6)]
        acc = [P.sb(f"acc{i}", [128, 512], F32) for i in range(2)]
        accb = [P.buf(f"acc{i}") for i in range(2)]
        sl = [P.sb(f"sl{i}", [128, 512], F32) for i in range(2)]
        slb = [P.buf(f"sl{i}") for i in range(2)]
        sqv = P.sb("sqv", [128, 512], BF16)
        sqvb = P.buf("sqv")
        lnn = P.sb("lnn", [128, 512], F32)
        lnnb = P.buf("lnn")
        rn = P.sb("rn", [128, 512], F32)
        rnb = P.buf("rn")
        gqkv = [[P.sb(f"gqkv{s}_{j}", [128, 512], BF16) for j in range(6)] for s in range(1)]
        gqkvb = [[P.buf(f"gqkv{s}_{j}") for j in range(6)] for s in range(1)]
        sg = P.sb("sg", [128, 512], F32)
        smg = P.sb("smg", [128, 512], F32)
        Gt = P.sb("Gt", [128, 512], F32)
        sgb, smgb, Gtb = P.buf("sg"), P.buf("smg"), P.buf("Gt")
        Vr = P.sb("Vr", [128, 256], BF16)
        Vrb = P.buf("Vr")
        sc = P.sb("sc", [128, 32], F32)
        scb = P.buf("sc")
        KD = P.sb("KD", [128, 128], BF16)
        KDb = P.buf("KD")
        Pt = P.sb("Pt", [128, 128], BF16)
        Ptb = P.buf("Pt")
        St = P.sb("St", [128, 256], F32)
        Stf = P.sb("Stf", [128, 256], BF16)
        Stb, Stfb = P.buf("St"), P.buf("Stf")
        bst = P.sb("bst", [128, 8], F32)
        bstb = P.buf("bst")
        yr = P.sb("yr", [128, 256], F32)
        yrb = P.buf("yr")
        mA = P.sb("mA", [128, 256], F32)
        mAb = P.buf("mA")
        mB = P.sb("mB", [128, 256], F32)
        mBb = P.buf("mB")
        mbf = P.sb("mbf", [128, 256], BF16)
        mbfb = P.buf("mbf")
        mTs = [P.sb(f"mTs{i}", [128, 2, 512], BF16) for i in range(2)]
        mTsb = [P.buf(f"mTs{i}") for i in range(2)]
        def per_head(name, shape, dtp):
            return [P.sb(f"{name}{h}", shape, dtp) for h in range(2)], [P.buf(f"{name}{h}") for h in range(2)]
        Gbc, Gbcb = per_head("Gbc", [128, 128], F32)
        LBc, LBcb = per_head("LBc", [128, 128], F32)
        hs, hsb = per_head("hs", [128, 16], F32)
        E1, E1b = per_head("E1", [128, 128], F32)
        E3, E3b = per_head("E3", [128, 128], F32)
        ER, ERb = per_head("ER", [128, 128], F32)
        qg, qgb = per_head("qg", [128, 128], BF16)
        Nm = [[P.sb(f"Nm{h}_{i}", [128, 128], F32) for i in range(2)] for h in range(2)]
        Nmb = [[P.buf(f"Nm{h}_{i}") for i in range(2)] for h in range(2)]
        Mm = [[P.sb(f"Mm{h}_{i}", [128, 128], F32) for i in range(2)] for h in range(2)]
        Mmb = [[P.buf(f"Mm{h}_{i}") for i in range(2)] for h in range(2)]
        Xm = [[P.sb(f"Xm{h}_{i}", [128, 128], F32) for i in range(2)] for h in range(2)]
        Xmb = [[P.buf(f"Xm{h}_{i}") for i in range(2)] for h in range(2)]
        Nf, Nfb = per_head("Nf", [128, 128], F32)
        Mf, Mfb = per_head("Mf", [128, 128], F32)
        Cm, Cmb = per_head("Cm", [128, 2, 128], F32)
        Tn, Tnb = per_head("Tn", [128, 128], F32)
        Pp, Ppb = per_head("Pp", [128, 128], F32)
        Xb, Xbb = per_head("Xb", [128, 128], BF16)
        QKt, QKtb = per_head("QKt", [128, 128], BF16)
        KDg, KDgb = per_head("KDg", [128, 128], BF16)
        VB, VBb = per_head("VB", [128, 128], F32)
        Zt, Ztb = per_head("Zt", [128, 128], BF16)
        Vn, Vnb = per_head("Vn", [128, 128], BF16)
        Sg, Sgb = per_head("Sg", [128, 128], F32)
        Sgf, Sgfb = per_head("Sgf", [128, 128], BF16)
        y1, y1b = per_head("y1", [128, 128], F32)
        junk = P.sb("junk", [128, 256], F32)
        junkb = P.buf("junk")

        mT_ov = mT_o.rearrange("(cc p) t -> p cc t", p=128)
        NT = nt or (NTOK // 512)

        def load_tile(ti):
            s = ti % 2
            t0 = ti * 512
            rk, to = ti // 2, (ti % 2) * 512
            P.dma("sp", hsl[s][:], hT_v[rk, :, :, to:to + 512], writes=[hslb[s]])
            P.dma("sp", cs_sl[s][:, 0, :], cos_v[rk, :, to:to + 512], writes=[cs_b[s]])
            P.dma("sp", cs_sl[s][:, 1, :], sin_v[rk, :, to:to + 512], writes=[cs_b[s]])

        def fm_stage(ti):
            s = ti % 2
            first = (ti % (SEQ // 512) == 0)
            for j in range(10):
                bk = j % 2

                def mm(e):
                    ins = None
                    for kc in range(KC):
                        ins = e.matmul(psFM[bk][:], wfm[:, kc, j * 128:(j + 1) * 128], hsl[s][:, kc, :],
                                       start=(kc == 0), stop=(kc == KC - 1))
                    return ins
                T(mm, [wfmb, hslb[s]], [psFMb[bk]])
                ps, psb_ = psFM[bk], psFMb[bk]
                if j in (0, 2):
                    V(lambda e: e.tensor_tensor(out=t1[:], in0=ps[:], in1=cs_sl[s][:, 0, :], op=ALU.mult),
                      [psb_, cs_b[s]], [t1b])
                elif j in (1, 3):
                    V(lambda e: e.tensor_tensor(out=t2[:], in0=ps[:], in1=cs_sl[s][:, 1, :], op=ALU.mult),
                      [psb_, cs_b[s]], [t2b])
                    G(lambda e: e.tensor_tensor(out=t1[:], in0=t1[:], in1=t2[:], op=ALU.add), [t1b, t2b], [t1b])
                    if j == 1:
                        A(lambda e: e.copy(out=qr[0][:], in_=t1[:]), [t1b], [qrb[0]])
                        G(lambda e: e.tensor_tensor(out=qd[0][:], in0=t1[:], in1=qdec, op=ALU.mult), [t1b, cstb], [qdb[0]])
                    else:
                        A(lambda e: e.copy(out=kr[0][:], in_=t1[:]), [t1b], [krb[0]])
                else:
                    jj = j - 4
                    stg, stgb = stage[jj], stageb[jj]
                    if first:
                        V(lambda e: e.memset(stg[:, 0:3], 0.0), [], [stgb])
                    A(lambda e: e.copy(out=stg[:, 3:515], in_=ps[:]), [psb_], [stgb])
                    a = jj % 2
                    cw = lambda i: cst[:, C_CONV + jj * 4 + i:C_CONV + jj * 4 + i + 1]
                    V(lambda e: e.tensor_scalar(out=acc[a][:], in0=stg[:, 0:512], scalar1=cw(0), scalar2=None, op0=ALU.mult),
                      [stgb, cstb], [accb[a]])
                    for i in range(1, 4):
                        V(lambda e: e.scalar_tensor_tensor(out=acc[a][:], in0=stg[:, i:i + 512], scalar=cw(i), in1=acc[a][:],
                                                           op0=ALU.mult, op1=ALU.add),
                          [stgb, cstb, accb[a]], [accb[a]])
                    A(lambda e: e.copy(out=stg[:, 0:3], in_=stg[:, 512:515]), [stgb], [stgb])
                    if jj >= 4:
                        A(lambda e: e.activation(out=gqkv[0][jj][:], in_=acc[a][:], func=AF.Silu), [accb[a]], [gqkvb[0][jj]])
                    else:
                        A(lambda e: e.activation(out=sl[a][:], in_=acc[a][:], func=AF.Silu), [accb[a]], [slb[a]])
                        A(lambda e: e.activation(out=sqv[:], in_=sl[a][:], func=AF.Square), [slb[a]], [sqvb])
                        T(lambda e: e.matmul(psN[:], ones_bf[:], sqv[:], start=True, stop=True), [onesb, sqvb], [psNb])
                        A(lambda e: e.activation(out=lnn[:], in_=psN[:], func=AF.Ln, bias=col(C_EPS6)), [psNb, cstb], [lnnb])
                        if jj < 2:
                            A(lambda e: e.activation(out=rn[:], in_=lnn[:], func=AF.Exp, scale=-0.5, bias=col(C_LNS)),
                              [lnnb, cstb], [rnb])
                        else:
                            A(lambda e: e.activation(out=rn[:], in_=lnn[:], func=AF.Exp, scale=-0.5), [lnnb], [rnb])
                        V(lambda e: e.tensor_tensor(out=gqkv[0][jj][:], in0=sl[a][:], in1=rn[:], op=ALU.mult),
                          [slb[a], rnb], [gqkvb[0][jj]])

        def block(ti, bi):
            s = ti % 2
            first = (ti % (SEQ // 512) == 0) and bi == 0
            bs = slice(bi * 128, (bi + 1) * 128)
            def tm(bk, c0, c1):
                def mm(e):
                    ins = None
                    for kc in range(KC):
                        ins = e.matmul(psTM[bk][:, 0:c1 - c0], hsl[s][:, kc, bs], wtm[:, kc, c0:c1],
                                       start=(kc == 0), stop=(kc == KC - 1))
                    return ins
                T(mm, [wtmb, hslb[s]], [psTMb[bk]])
            tm(0, 0, 512)
            A(lambda e: e.activation(out=sg[:], in_=psTM[0][:], func=AF.Silu), [psTMb[0]], [sgb])
            tm(1, 512, 1024)
            A(lambda e: e.activation(out=smg[:], in_=psTM[1][:], func=AF.Sigmoid), [psTMb[1]], [smgb])
            G(lambda e: e.tensor_tensor(out=Gt[:], in0=sg[:], in1=smg[:], op=ALU.mult), [sgb, smgb], [Gtb])
            tm(0, 1024, NTM)
            V(lambda e: e.tensor_copy(out=Vr[:], in_=psTM[0][:, 0:256]), [psTMb[0]], [Vrb])
            V(lambda e: e.tensor_tensor(out=sc[:, 0:2], in0=psTM[0][:, 256:258], in1=cst[:, C_HV + 2:C_HV + 4], op=ALU.add),
              [psTMb[0], cstb], [scb])
            A(lambda e: e.activation(out=sc[:, 2:4], in_=sc[:, 0:2], func=AF.Exp), [scb], [scb])
            A(lambda e: e.activation(out=sc[:, 4:6], in_=sc[:, 2:4], func=AF.Ln, bias=col(C_ONE)), [scb, cstb], [scb])
            V(lambda e: e.tensor_tensor(out=sc[:, 6:8], in0=sc[:, 4:6], in1=nea[:], op=ALU.mult), [scb, neab], [scb])
            A(lambda e: e.activation(out=sc[:, 8:10], in_=psTM[0][:, 258:260], func=AF.Exp, scale=-1.0), [psTMb[0], scb], [scb])
            A(lambda e: e.activation(out=sc[:, 10:12], in_=sc[:, 8:10], func=AF.Ln, bias=col(C_ONE)), [scb, cstb], [scb])
            A(lambda e: e.activation(out=sc[:, 12:14], in_=sc[:, 10:12], func=AF.Exp, scale=-1.0), [scb], [scb])

            def ret_gen():
                if first:
                    V(lambda e: e.memset(St[:], 0.0), [], [Stb])
                    V(lambda e: e.memset(Stf[:], 0.0), [], [Stfb])
                yield
                r1, r1b = sm(2)
                T(lambda e: e.transpose(bfv(r1), kr[0][:, bs], identb[:]), [krb[0], identbb], [r1b])
                A(lambda e: e.activation(out=KD[:], in_=bfv(r1), func=AF.Copy, scale=col(C_KDEC)), [r1b, cstb], [KDb])
                yield
                r2, r2b = sm(2)
                T(lambda e: e.matmul(r2, kr[0][:, bs], qr[0][:, bs], start=True, stop=True), [krb[0], qrb[0]], [r2b])
                V(lambda e: e.tensor_tensor(out=Pt[:], in0=r2, in1=DMt, op=ALU.mult), [r2b, cstb], [Ptb])

                yield
                def mmo(e):
                    e.matmul(psRo, Pt[:], Vr[:], start=True, stop=False)
                    return e.matmul(psRo, qd[0][:, bs], Stf[:], start=False, stop=True)
                T(mmo, [Ptb, Vrb, qdb[0], Stfb], [psRob])
                yield
                T(lambda e: e.matmul(psRs, KD[:], Vr[:], start=True, stop=True), [KDb, Vrb], [psRsb])
                V(lambda e: e.scalar_tensor_tensor(out=St[:], in0=St[:], scalar=col(C_G128), in1=psRs, op0=ALU.mult, op1=ALU.add),
                  [Stb, cstb, psRsb], [Stb])
                A(lambda e: e.copy(out=Stf[:], in_=St[:]), [Stb], [Stfb])
                yield
                V(lambda e: e.bn_stats(out=bst[:, 0:6], in_=psRo), [psRob], [bstb])
                V(lambda e: e.bn_aggr(out=bst[:, 6:8], in_=bst[:, 0:6]), [bstb], [bstb])
                A(lambda e: e.activation(out=bst[:, 0:1], in_=bst[:, 7:8], func=AF.Ln, bias=col(C_EPS5)), [bstb, cstb], [bstb])
                A(lambda e: e.activation(out=bst[:, 1:2], in_=bst[:, 0:1], func=AF.Exp, scale=-0.5), [bstb], [bstb])
                V(lambda e: e.tensor_scalar(out=yr[:], in0=psRo, scalar1=bst[:, 6:7], scalar2=bst[:, 1:2],
                                            op0=ALU.subtract, op1=ALU.mult), [psRob, bstb], [yrb])
                yield
                G(lambda e: e.tensor_tensor(out=yr[:], in0=yr[:], in1=retg, op=ALU.mult), [yrb, cstb], [yrb])
                G(lambda e: e.tensor_tensor(out=mA[:], in0=yr[:], in1=Gt[:, 0:256], op=ALU.mult), [yrb, Gtb], [mAb])

                yield
            def head_gen(h):
                    gq, gk, gv = gqkv[0][h], gqkv[0][2 + h], gqkv[0][4 + h]
                    gqb_, gkb_, gvb_ = gqkvb[0][h], gqkvb[0][2 + h], gqkvb[0][4 + h]
                    if first:
                        V(lambda e: e.memset(Sg[h][:], 0.0), [], [Sgb[h]])
                        V(lambda e: e.memset(Sgf[h][:], 0.0), [], [Sgfb[h]])
                    H, Hb = hs[h], hsb[h]
                    V(lambda e: e.tensor_scalar(out=Gbc[h][:], in0=ones_f[:], scalar1=sc[:, 6 + h:7 + h], scalar2=None, op0=ALU.mult),
                      [onesb, scb], [Gbcb[h]])
                    V(lambda e: e.tensor_scalar(out=LBc[h][:], in0=ones_f[:], scalar1=sc[:, 10 + h:11 + h], scalar2=-1.0,
                                                op0=ALU.mult, op1=ALU.mult), [onesb, scb], [LBcb[h]])
                    yield
                    pR, pRb = sm(h)
                    T(lambda e: e.matmul(pR, Gbc[h][:], Utri, start=True, stop=True), [Gbcb[h], cstb], [pRb])
                    pR2, pR2b = sm(h)

                    def mm2(e):
                        e.matmul(pR2, Gbc[h][:], Utri, start=True, stop=False)
                        return e.matmul(pR2, LBc[h][:], ident, start=False, stop=True)
                    T(mm2, [Gbcb[h], LBcb[h], cstb], [pR2b])
                    pc, pcb = bank[sm_last[0]][:, 128:256], bankb[sm_last[0]]
                    T(lambda e: e.matmul(pc[:, 0:1], Utri, sc[:, 6 + h:7 + h], start=True, stop=True), [cstb, scb], [pcb])
                    V(lambda e: e.tensor_scalar(out=H[:, 0:1], in0=pc[:, 0:1], scalar1=-1.0, scalar2=None, op0=ALU.mult), [pcb], [Hb])
                    V(lambda e: e.tensor_copy(out=H[:, 1:2], in_=pR[:, 127:128]), [pRb], [Hb])
                    A(lambda e: e.activation(out=H[:, 2:3], in_=H[:, 1:2], func=AF.Exp), [Hb], [Hb])
                    A(lambda e: e.activation(out=H[:, 3:4], in_=pc[:, 0:1], func=AF.Exp, scale=-1.0, bias=H[:, 1:2]), [pcb, Hb], [Hb])
                    V(lambda e: e.tensor_scalar(out=H[:, 5:6], in0=sc[:, 10 + h:11 + h], scalar1=-1.0, scalar2=None, op0=ALU.mult),
                      [scb], [Hb])
                    A(lambda e: e.activation(out=H[:, 4:5], in_=pc[:, 0:1], func=AF.Exp, bias=H[:, 5:6]), [pcb, Hb], [Hb])
                    V(lambda e: e.tensor_scalar(out=H[:, 4:5], in0=H[:, 4:5], scalar1=-1.0, scalar2=None, op0=ALU.mult), [Hb], [Hb])
                    V(lambda e: e.scalar_tensor_tensor(out=E1[h][:], in0=pR2, scalar=H[:, 0:1], in1=NEGs, op0=ALU.add, op1=ALU.add),
                      [pR2b, Hb, cstb], [E1b[h]])
                    A(lambda e: e.activation(out=E1[h][:], in_=E1[h][:], func=AF.Exp), [E1b[h]], [E1b[h]])
                    V(lambda e: e.scalar_tensor_tensor(out=E3[h][:], in0=pR, scalar=H[:, 0:1], in1=NEGc, op0=ALU.add, op1=ALU.add),
                      [pRb, Hb, cstb], [E3b[h]])
                    A(lambda e: e.activation(out=E3[h][:], in_=E3[h][:], func=AF.Exp), [E3b[h]], [E3b[h]])
                    A(lambda e: e.activation(out=ER[h][:], in_=pR, func=AF.Exp), [pRb], [ERb[h]])
                    G(lambda e: e.tensor_tensor(out=qg[h][:], in0=gq[:, bs], in1=ER[h][:], op=ALU.mult), [gqb_, ERb[h]], [qgb[h]])
                    yield
                    pKK, pKKb = sm(h)
                    T(lambda e: e.matmul(pKK, gk[:, bs], gk[:, bs], start=True, stop=True), [gkb_], [pKKb])
                    V(lambda e: e.tensor_tensor(out=Nf[h][:], in0=pKK, in1=E1[h][:], op=ALU.mult), [pKKb, E1b[h]], [Nfb[h]])
                    yield
                    pM, pMb = sm(h)
                    T(lambda e: e.transpose(pM, Nf[h][:], ident), [Nfb[h], cstb], [pMb])
                    A(lambda e: e.copy(out=Mf[h][:], in_=pM), [pMb], [Mfb[h]])
                    yield
                    pQK, pQKb = sm(h)
                    T(lambda e: e.matmul(pQK, gk[:, bs], gq[:, bs], start=True, stop=True), [gkb_, gqb_], [pQKb])
                    V(lambda e: e.tensor_tensor(out=QKt[h][:], in0=pQK, in1=E3[h][:], op=ALU.mult), [pQKb, E3b[h]], [QKtb[h]])
                    yield
                    G(lambda e: e.tensor_tensor(out=Nm[h][0][:], in0=Nf[h][:], in1=MD32, op=ALU.mult), [Nfb[h], cstb], [Nmb[h][0]])
                    G(lambda e: e.tensor_tensor(out=Mm[h][0][:], in0=Mf[h][:], in1=MD32, op=ALU.mult), [Mfb[h], cstb], [Mmb[h][0]])
                    G(lambda e: e.tensor_tensor(out=Cm[h][:, 0, :], in0=Mf[h][:], in1=MC0, op=ALU.mult), [Mfb[h], cstb], [Cmb[h]])
                    G(lambda e: e.tensor_tensor(out=Cm[h][:, 1, :], in0=Mf[h][:], in1=MC1, op=ALU.mult), [Mfb[h], cstb], [Cmb[h]])
                    V(lambda e: e.tensor_tensor(out=Xm[h][0][:], in0=ident, in1=Nm[h][0][:], op=ALU.subtract), [cstb, Nmb[h][0]], [Xmb[h][0]])
                    for k in range(1, 5):
                        a, b = (k - 1) % 2, k % 2
                        yield
                        pm, pmb = sm(h)
                        T(lambda e: e.matmul(pm, Nm[h][a][:], Mm[h][a][:], start=True, stop=True), [Nmb[h][a], Mmb[h][a]], [pmb])
                        if k < 4:
                            pn, pnb = sm(h)
                            T(lambda e: e.matmul(pn, Mm[h][a][:], Nm[h][a][:], start=True, stop=True), [Nmb[h][a], Mmb[h][a]], [pnb])
                        A(lambda e: e.copy(out=Mm[h][b][:], in_=pm), [pmb], [Mmb[h][b]])
                        if k < 4:
                            V(lambda e: e.tensor_copy(out=Nm[h][b][:], in_=pn), [pnb], [Nmb[h][b]])
                        yield
                        px, pxb = sm(h)
                        T(lambda e: e.matmul(px, Mm[h][b][:], Xm[h][a][:], start=True, stop=True), [Mmb[h][b], Xmb[h][a]], [pxb])
                        V(lambda e: e.tensor_tensor(out=Xm[h][b][:], in0=px, in1=Xm[h][a][:], op=ALU.add), [pxb, Xmb[h][a]], [Xmb[h][b]])
                    xc = 0
                    for lv in range(2):
                        xn = 1 - xc
                        yield
                        ptp, ptpb = sm(h)
                        T(lambda e: e.transpose(ptp, Xm[h][xc][:], ident), [Xmb[h][xc], cstb], [ptpb])
                        A(lambda e: e.copy(out=Tn[h][:], in_=ptp), [ptpb], [Tnb[h]])
                        yield
                        pp1, pp1b = sm(h)
                        T(lambda e: e.matmul(pp1, Cm[h][:, lv, :], Xm[h][xc][:], start=True, stop=True), [Cmb[h], Xmb[h][xc]], [pp1b])
                        V(lambda e: e.tensor_copy(out=Pp[h][:], in_=pp1), [pp1b], [Ppb[h]])
                        yield
                        pq, pqb = sm(h)
                        T(lambda e: e.matmul(pq, Tn[h][:], Pp[h][:], start=True, stop=True), [Tnb[h], Ppb[h]], [pqb])
                        V(lambda e: e.tensor_tensor(out=Xm[h][xn][:], in0=Xm[h][xc][:], in1=pq, op=ALU.subtract), [Xmb[h][xc], pqb], [Xmb[h][xn]])
                        xc = xn
                    yield
                    A(lambda e: e.copy(out=Xb[h][:], in_=Xm[h][xc][:]), [Xmb[h][xc]], [Xbb[h]])
                    X6, X6b = Xb[h], Xbb[h]
                    yield
                    pk, pkb = sm(h)
                    T(lambda e: e.transpose(bfv(pk), gk[:, bs], identb[:]), [gkb_, identbb], [pkb])
                    A(lambda e: e.activation(out=KDg[h][:], in_=bfv(pk), func=AF.Copy, scale=H[:, 3:4]), [pkb, Hb], [KDgb[h]])
                    yield
                    pv, pvb = sm(h)
                    T(lambda e: e.transpose(bfv(pv), gv[:, bs], identb[:]), [gvb_, identbb], [pvb])
                    A(lambda e: e.activation(out=VB[h][:], in_=bfv(pv), func=AF.Copy, scale=sc[:, 12 + h:13 + h]), [pvb, scb], [VBb[h]])
                    yield
                    pz, pzb = sm(h)
                    T(lambda e: e.matmul(pz, gk[:, bs], Sgf[h][:], start=True, stop=True), [gkb_, Sgfb[h]], [pzb])
                    V(lambda e: e.scalar_tensor_tensor(out=Zt[h][:], in0=pz, scalar=H[:, 4:5], in1=VB[h][:], op0=ALU.mult, op1=ALU.add),
                      [pzb, Hb, VBb[h]], [Ztb[h]])
                    yield
                    pvn, pvnb = sm(h)
                    T(lambda e: e.matmul(pvn, X6[:], Zt[h][:], start=True, stop=True), [X6b, Ztb[h]], [pvnb])
                    A(lambda e: e.copy(out=Vn[h][:], in_=pvn), [pvnb], [Vnb[h]])
                    yield
                    po, pob = sm(h)

                    def mmg(e):
                        e.matmul(po, qg[h][:], Sgf[h][:], start=True, stop=False)
                        return e.matmul(po, QKt[h][:], Vn[h][:], start=False, stop=True)
                    T(mmg, [qgb[h], Sgfb[h], QKtb[h], Vnb[h]], [pob])
                    yield
                    pss, pssb = sm(h)
                    T(lambda e: e.matmul(pss, KDg[h][:], Vn[h][:], start=True, stop=True), [KDgb[h], Vnb[h]], [pssb])
                    V(lambda e: e.scalar_tensor_tensor(out=Sg[h][:], in0=Sg[h][:], scalar=H[:, 2:3], in1=pss, op0=ALU.mult, op1=ALU.add),
                      [Sgb[h], Hb, pssb], [Sgb[h]])
                    A(lambda e: e.copy(out=Sgf[h][:], in_=Sg[h][:]), [Sgb[h]], [Sgfb[h]])
                    yield
                    A(lambda e: e.activation(out=junk[:, 0:128], in_=po, func=AF.Square, accum_out=H[:, 6:7]), [pob, Hb], [junkb, Hb])
                    A(lambda e: e.activation(out=H[:, 7:8], in_=H[:, 6:7], func=AF.Ln, scale=1.0 / 128, bias=col(C_EPS6)), [Hb, cstb], [Hb])
                    A(lambda e: e.activation(out=H[:, 8:9], in_=H[:, 7:8], func=AF.Exp, scale=-0.5), [Hb], [Hb])
                    V(lambda e: e.scalar_tensor_tensor(out=y1[h][:], in0=po, scalar=H[:, 8:9], in1=gdng, op0=ALU.mult, op1=ALU.mult),
                      [pob, Hb, cstb], [y1b[h]])
                    G(lambda e: e.tensor_tensor(out=mB[:, h * 128:(h + 1) * 128], in0=y1[h][:], in1=Gt[:, 256 + h * 128:256 + (h + 1) * 128],
                                                op=ALU.mult), [y1b[h], Gtb], [mBb])
            gens = [head_gen(0), head_gen(1), ret_gen()]
            while gens:
                for g_ in list(gens):
                    try:
                        next(g_)
                    except StopIteration:
                        gens.remove(g_)
            G(lambda e: e.tensor_tensor(out=mbf[:], in0=mA[:], in1=mB[:], op=ALU.add), [mAb, mBb], [mbfb])
            for cc in range(2):
                pt, ptb = sm(2)
                T(lambda e: e.transpose(bfv(pt), mbf[:, cc * 128:(cc + 1) * 128], identb[:]), [mbfb, identbb], [ptb])
                A(lambda e: e.copy(out=mTs[s][:, cc, bs], in_=bfv(pt)), [ptb], [mTsb[s]])

        load_tile(0)
        for ti in range(NT):
            if ti + 1 < NT:
                load_tile(ti + 1)
            fm_stage(ti)
            for bi in range(4):
                block(ti, bi)
            s = ti % 2
            P.dma("sp", mT_ov[:, :, ti * 512:(ti + 1) * 512], mTs[s][:], reads=[mTsb[s]])


def _ret_consts(c):
    lg = np.log1p(-np.exp2(-5.0 - np.float64(c)))
    idx = np.arange(128)
    jj, ii = idx[:, None], idx[None, :]
    same = (jj // 64) == (ii // 64)
    later = (jj < 64) & (ii >= 64)
    DMt = np.where(same, np.exp(lg * np.abs(ii - jj)), np.where(later, np.exp(lg * (ii - jj)), 0.0))
    DMt = DMt * (128.0 ** -0.5)
    qdec = np.exp(lg * (np.arange(512) % 128 + 1.0))
    kdec = np.exp(lg * (127.0 - idx)) * (128.0 ** -0.5)
    g128 = np.exp(lg * 128.0)
    return DMt, qdec, kdec, g128


def pack_cst(c, conv_w_l, a_log_l, dt_bias_l, ret_gn_g_l, gdn_norm_g_l):
    cst = np.zeros((128, NCST), np.float32)
    idx = np.arange(128)
    cst[:, C_ID:C_ID + 128] = np.eye(128)
    DMt, qdec, kdec, g128 = _ret_consts(c)
    cst[:, C_DM:C_DM + 128] = DMt
    cst[:, C_QD:C_QD + 512] = qdec[None, :]
    cst[:, C_U:C_U + 128] = (idx[:, None] <= idx[None, :])
    cst[:, C_NS:C_NS + 128] = np.where(idx[None, :] > idx[:, None], 0.0, NEG)
    cst[:, C_NC:C_NC + 128] = np.where(idx[None, :] >= idx[:, None], 0.0, NEG)
    cst[:, C_KDEC] = kdec
    cst[:, C_G128] = g128
    cst[:, C_ONE] = 1.0
    cst[:, C_EPS6] = 1e-6
    cst[:, C_EPS5] = 1e-5
    cst[:, C_LNS] = math.log(128.0 ** -0.5)
    cst[:, C_HV:C_HV + 2] = a_log_l[None, 2 * c:2 * c + 2]
    cst[:, C_HV + 2:C_HV + 4] = dt_bias_l[None, 2 * c:2 * c + 2]
    cst[:, C_RETG:C_RETG + 256] = ret_gn_g_l[None, c * 256:(c + 1) * 256]
    cst[:, C_GDNG:C_GDNG + 128] = gdn_norm_g_l[None, :]
    b32, b64 = idx // 32, idx // 64
    cst[:, C_MD32:C_MD32 + 128] = (b32[:, None] == b32[None, :])
    cst[:, C_MC0:C_MC0 + 128] = (b64[:, None] == b64[None, :]) & (b32[:, None] != b32[None, :])
    cst[:, C_MC1:C_MC1 + 128] = (b64[:, None] != b64[None, :])
    for jj in range(6):
        grp, h = jj // 2, jj % 2
        ch0 = grp * 2048 + (2 * c + h) * 128
        cst[:, C_CONV + jj * 4:C_CONV + jj * 4 + 4] = conv_w_l[:, ch0:ch0 + 128].T
    return cst


def pack_w_in(c, w_in_l):
    sw = (np.arange(128) + 64) % 128
    rq = w_in_l[:, O_RQ + c * 128:O_RQ + (c + 1) * 128]
    rk = w_in_l[:, O_RK + c * 128:O_RK + (c + 1) * 128]
    cols = [rq, rq[:, sw], rk, rk[:, sw]]
    for grp in range(3):
        for h in range(2):
            o = O_GQKV + grp * 2048 + (2 * c + h) * 128
            cols.append(w_in_l[:, o:o + 128])
    wfm = np.ascontiguousarray(np.concatenate(cols, axis=1))
    wtm = np.ascontiguousarray(np.concatenate([
        w_in_l[:, O_RG + c * 256:O_RG + (c + 1) * 256],
        w_in_l[:, O_GZ + c * 256:O_GZ + (c + 1) * 256],
        w_in_l[:, O_MA + c * 256:O_MA + (c + 1) * 256],
        w_in_l[:, O_MB + c * 256:O_MB + (c + 1) * 256],
        w_in_l[:, O_RV + c * 256:O_RV + (c + 1) * 256],
        w_in_l[:, O_GA + 2 * c:O_GA + 2 * c + 2],
        w_in_l[:, O_GB + 2 * c:O_GB + 2 * c + 2]], axis=1))
    return wfm, wtm


def build_fused(depth=DEPTH, nt=None, wdepth=DEPTH):
    nc = bass.Bass("TRN2", target_bir_lowering=False)
    din = lambda n, sh, d=F32: nc.dram_tensor(n, list(sh), d, kind="ExternalInput").ap()
    dint = lambda n, sh, d=F32: nc.dram_tensor(n, list(sh), d).ap()
    io = dict(
        x=din("x", [TPC, D]), pos=din("pos", [1, TPC], I32), g0=din("g0", [128, KC]), ident=din("ident", [128, 128]),
        ropec=din("ropec", [128, 2]), selm=din("selm", [128, NCORES]), mem=din("mem", [MEM, D]),
    )
    gains_all = din("gains_all", [wdepth, 128, 4 * KC])
    wfm_all = din("wfm_all", [wdepth, D, NFM])
    wtm_all = din("wtm_all", [wdepth, D, NTM])
    cst_all = din("cst_all", [wdepth, 128, NCST])
    W = {k: din(k, [wdepth] + sh) for k, sh in (("w_out", [D, D]), ("w_xq", [D, D]), ("w_xkv", [D, 2 * D]), ("w_xo", [D, D]),
                                                ("w_ff1", [D, DFF]), ("w_ff2", [DFF, D]))}
    out_o = nc.dram_tensor("out_o", [TPC, D], F32, kind="ExternalOutput").ap()
    xT_d = dint("xT_d", [D, TPC])
    hT_loc = dint("hT_loc", [D, TPC], BF16)
    hT_ag = dint("hT_ag", [NCORES * D, TPC], BF16)
    cos_loc, sin_loc = dint("cos_loc", [128, TPC]), dint("sin_loc", [128, TPC])
    cos_ag, sin_ag = dint("cos_ag", [NCORES * 128, TPC]), dint("sin_ag", [NCORES * 128, TPC])
    mT_loc = dint("mT_loc", [256, NTOK], BF16)
    mT_ag = dint("mT_ag", [NCORES * 256, NTOK], BF16)
    with ExitStack() as es:
        P = Prog(nc, es)
        P.push_scope()
        emit_A0(P, dict(io, xT_o=xT_d, hT_o=hT_loc, cos_o=cos_loc, sin_o=sin_loc))
        P.pop_scope()
        P.collective("AllGather", hT_loc, hT_ag)
        P.collective("AllGather", cos_loc, cos_ag)
        P.collective("AllGather", sin_loc, sin_ag)
        for l in range(depth):
            last = (l == depth - 1)
            P.push_scope()
            emit_B(P, dict(hT_ag=hT_ag, wfm=wfm_all[l], wtm=wtm_all[l], cst=cst_all[l], cos_ag=cos_ag, sin_ag=sin_ag,
                           mT_loc=mT_loc), nt=nt)
            P.pop_scope()
            P.collective("AllGather", mT_loc, mT_ag)
            P.push_scope()
            tio = dict(io, xT_d=xT_d, mT_ag=mT_ag, gains=gains_all[l], hT_o=hT_loc, out_o=out_o)
            tio.update({k: v[l] for k, v in W.items()})
            emit_T(P, tio, last)
            P.pop_scope()
            if not last:
                P.collective("AllGather", hT_loc, hT_ag)
        P.finish()
    return nc


def _single(emit, ins, outs, **kw):
    nc = bass.Bass("TRN2", target_bir_lowering=False)
    io = {}
    for n, (sh, d) in ins.items():
        io[n] = nc.dram_tensor(n, list(sh), d, kind="ExternalInput").ap()
    for n, (sh, d) in outs.items():
        io[n] = nc.dram_tensor(n, list(sh), d, kind="ExternalOutput").ap()
    with ExitStack() as es:
        P = Prog(nc, es)
        P.push_scope()
        emit(P, io, **kw)
        P.pop_scope()
        P.finish()
    return nc


def build_A0():
    return _single(emit_A0, dict(x=([TPC, D], F32), pos=([1, TPC], I32), g0=([128, KC], F32), ident=([128, 128], F32), ropec=([128, 2], F32)),
                   dict(xT_o=([D, TPC], F32), hT_o=([D, TPC], BF16), cos_o=([128, TPC], F32), sin_o=([128, TPC], F32)))


def build_B():
    return _single(emit_B, dict(hT_ag=([NCORES * D, TPC], BF16), wfm=([D, NFM], F32), wtm=([D, NTM], F32), cst=([128, NCST], F32),
                                cos_ag=([NCORES * 128, TPC], F32), sin_ag=([NCORES * 128, TPC], F32)),
                   dict(mT_loc=([256, NTOK], BF16)))


def build_T(last):
    ins = dict(xT_d=([D, TPC], F32), mT_ag=([D, NTOK], BF16), selm=([128, NCORES], F32), w_out=([D, D], F32), w_xq=([D, D], F32),
               w_xkv=([D, 2 * D], F32), w_xo=([D, D], F32), w_ff1=([D, DFF], F32), w_ff2=([DFF, D], F32),
               gains=([128, 4 * KC], F32), mem=([MEM, D], F32), ident=([128, 128], F32))
    outs = dict(out_o=([TPC, D], F32)) if last else dict(xT_o=([D, TPC], F32), hT_o=([D, TPC], BF16))
    return _single(emit_T, ins, outs, last=last)


_PROGS = {}


def _run(name, in_maps):
    if name not in _PROGS:
        _PROGS[name] = {"A0": build_A0, "B": build_B, "T": lambda: build_T(False), "TL": lambda: build_T(True)}[name]()
    return run_bass_kernel_spmd(_PROGS[name], in_maps, core_ids=list(range(NCORES))).results


LAUNCH_MODE = "multi"


_PROG = []


def kernel(x, mem, positions, norm_mix_g, w_in, conv_w, gdn_a_log, gdn_dt_bias, ret_gn_g, gdn_norm_g,
           w_out, norm_x_g, norm_mem_g, w_xq, w_xkv, w_xo, norm_ffn_g, w_ff1, w_ff2, norm_final_g):
    f = lambda a: np.ascontiguousarray(np.asarray(a, dtype=np.float32))
    x = f(x).reshape(NTOK, D)
    mem = f(mem)
    pos = np.ascontiguousarray(np.asarray(positions, dtype=np.int32)).reshape(NTOK)
    norm_mix_g, norm_x_g, norm_mem_g, norm_ffn_g, norm_final_g = map(f, (norm_mix_g, norm_x_g, norm_mem_g, norm_ffn_g, norm_final_g))
    w_in, conv_w, gdn_a_log, gdn_dt_bias, ret_gn_g, gdn_norm_g = map(f, (w_in, conv_w, gdn_a_log, gdn_dt_bias, ret_gn_g, gdn_norm_g))
    w_out, w_xq, w_xkv, w_xo, w_ff1, w_ff2 = map(f, (w_out, w_xq, w_xkv, w_xo, w_ff1, w_ff2))
    ident = np.eye(128, dtype=np.float32)
    inv_freq = (1.0 / (np.float32(10000.0) ** (np.arange(0, 128, 2, dtype=np.float32) / np.float32(128)))).astype(np.float32)
    ropec = np.zeros((128, 2), np.float32)
    ropec[:, 0] = np.concatenate([inv_freq, inv_freq])
    ropec[:64, 1] = -1.0
    ropec[64:, 1] = 1.0
    tsl = lambda c: slice(c * TPC, (c + 1) * TPC)
    gains_all = np.stack([np.concatenate([_lay_g(norm_x_g[l]), _lay_g(norm_mem_g[l]), _lay_g(norm_ffn_g[l]),
                                          _lay_g(norm_final_g if l == DEPTH - 1 else norm_mix_g[l + 1])], axis=1)
                          for l in range(DEPTH)])
    sel = []
    for c in range(NCORES):
        m_ = np.zeros((128, NCORES), np.float32)
        m_[:, c] = 1.0
        sel.append(m_)
    if LAUNCH_MODE == "multi":
        r = _run("A0", [dict(x=x[tsl(c)], pos=pos[tsl(c)].reshape(1, TPC), g0=_lay_g(norm_mix_g[0]), ident=ident, ropec=ropec)
                        for c in range(NCORES)])
        xT = [r[c]["xT_o"] for c in range(NCORES)]
        hT = [r[c]["hT_o"] for c in range(NCORES)]
        cos_ag = np.ascontiguousarray(np.concatenate([r[c]["cos_o"] for c in range(NCORES)], axis=0))
        sin_ag = np.ascontiguousarray(np.concatenate([r[c]["sin_o"] for c in range(NCORES)], axis=0))
        out = None
        for l in range(DEPTH):
            hT_ag = np.ascontiguousarray(np.concatenate(hT, axis=0))
            maps = []
            for c in range(NCORES):
                wfm, wtm = pack_w_in(c, w_in[l])
                maps.append(dict(hT_ag=hT_ag, wfm=wfm, wtm=wtm,
                                 cst=pack_cst(c, conv_w[l], gdn_a_log[l], gdn_dt_bias[l], ret_gn_g[l], gdn_norm_g[l]),
                                 cos_ag=cos_ag, sin_ag=sin_ag))
            r = _run("B", maps)
            mT_ag = np.ascontiguousarray(np.concatenate([r[c]["mT_loc"] for c in range(NCORES)], axis=0))
            last = (l == DEPTH - 1)
            maps = [dict(xT_d=xT[c], mT_ag=mT_ag, selm=sel[c], w_out=w_out[l], w_xq=w_xq[l], w_xkv=w_xkv[l], w_xo=w_xo[l],
                         w_ff1=w_ff1[l], w_ff2=w_ff2[l], gains=gains_all[l], mem=mem[c // 4], ident=ident) for c in range(NCORES)]
            r = _run("TL" if last else "T", maps)
            if last:
                out = np.concatenate([r[c]["out_o"] for c in range(NCORES)], axis=0)
            else:
                xT = [r[c]["xT_o"] for c in range(NCORES)]
                hT = [r[c]["hT_o"] for c in range(NCORES)]
        return np.ascontiguousarray(out.reshape(BATCH, SEQ, D).astype(np.float32))
    maps = []
    for c in range(NCORES):
        packs = [pack_w_in(c, w_in[l]) for l in range(DEPTH)]
        maps.append(dict(
            x=x[tsl(c)], pos=pos[tsl(c)].reshape(1, TPC), g0=_lay_g(norm_mix_g[0]), ident=ident, ropec=ropec, selm=sel[c],
            mem=mem[c // 4], gains_all=gains_all,
            wfm_all=np.stack([p[0] for p in packs]), wtm_all=np.stack([p[1] for p in packs]),
            cst_all=np.stack([pack_cst(c, conv_w[l], gdn_a_log[l], gdn_dt_bias[l], ret_gn_g[l], gdn_norm_g[l]) for l in range(DEPTH)]),
            w_out=w_out, w_xq=w_xq, w_xkv=w_xkv, w_xo=w_xo, w_ff1=w_ff1, w_ff2=w_ff2))
    if not _PROG:
        _PROG.append(build_fused())
    res = run_bass_kernel_spmd(_PROG[0], maps, core_ids=list(range(NCORES)))
    out = np.concatenate([res.results[c]["out_o"] for c in range(NCORES)], axis=0)
    return np.ascontiguousarray(out.reshape(BATCH, SEQ, D).astype(np.float32))
```
